# Optimizing a Trainium2 kernel written in Bass

```python
import math
import jax, jax.numpy as jnp
from jax import lax
import numpy as np

D_MODEL = 1024
BATCH = 8
SEQ = 2048
DEPTH = 2

MEM_LEN = 256
EPS = 1e-6
Q_BLOCK = 128

A_HEADS = 8
A_QK_DIM = 64
A_V_DIM = 2 * A_QK_DIM
A_WIDTH = A_HEADS * A_V_DIM
B_WIDTH = D_MODEL // 2
CONV_W = 3
C_HEADS = 12
C_DIM = 128
C_WIDTH = C_HEADS * C_DIM
X_HEADS = 4
X_DIM = 128
X_WIDTH = X_HEADS * X_DIM

EVEN_MIX = A_WIDTH + B_WIDTH + X_WIDTH
ODD_MIX = C_WIDTH + X_WIDTH
EVEN_IN = 4 * A_WIDTH + 4 * B_WIDTH + 2 * X_WIDTH
ODD_IN = 4 * C_WIDTH + C_HEADS + 2 * X_WIDTH

kernel_name = "hybrid_diffattn_shortconv_fox_memxattn"


def rms_norm(x, g):
    xf = x.astype(jnp.float32)
    y = xf * lax.rsqrt(jnp.mean(xf * xf, axis=-1, keepdims=True) + EPS)
    return (y * g.astype(jnp.float32)).astype(x.dtype)


def alibi_slopes(n_heads):
    return jnp.asarray(np.array([2.0 ** (-8.0 * (i + 1) / n_heads) for i in range(n_heads)], dtype=np.float32))


def sweep_query_blocks(block_fn, *q_arrays):
    seq = q_arrays[0].shape[2]
    nb = seq // Q_BLOCK

    def split(a):
        a = a.reshape(a.shape[:2] + (nb, Q_BLOCK) + a.shape[3:])
        return jnp.moveaxis(a, 2, 0)

    starts = jnp.arange(nb, dtype=jnp.int32) * Q_BLOCK
    out = lax.map(lambda args: block_fn(args[0], *args[1:]), (starts,) + tuple(split(a) for a in q_arrays))
    out = jnp.moveaxis(out, 0, 2)
    return out.reshape(out.shape[:2] + (seq,) + out.shape[4:])


def differential_attention(q, k, v, q_g, k_g, lam_params, out_g, lam_init):
    bsz, seq, _ = q.shape
    q = rms_norm(q.reshape(bsz, seq, A_HEADS, 2, A_QK_DIM), q_g).transpose(0, 2, 3, 1, 4)
    k = rms_norm(k.reshape(bsz, seq, A_HEADS, 2, A_QK_DIM), k_g).transpose(0, 2, 3, 1, 4)
    v = v.reshape(bsz, seq, A_HEADS, A_V_DIM).transpose(0, 2, 1, 3)
    lp = lam_params.astype(jnp.float32)
    lam = jnp.exp(jnp.sum(lp[0] * lp[1])) - jnp.exp(jnp.sum(lp[2] * lp[3])) + lam_init
    slopes = alibi_slopes(A_HEADS)
    scale = A_QK_DIM ** -0.5
    k1, k2 = k[:, :, 0], k[:, :, 1]
    k_pos = jnp.arange(seq, dtype=jnp.int32)

    def block(start, q1b, q2b):
        q_pos = start + jnp.arange(Q_BLOCK, dtype=jnp.int32)
        dist = q_pos[:, None] - k_pos[None, :]
        causal = dist >= 0
        bias = -slopes[:, None, None] * dist.astype(jnp.float32)
        s1 = jnp.einsum('bhqd,bhkd->bhqk', q1b, k1).astype(jnp.float32) * scale + bias
        s2 = jnp.einsum('bhqd,bhkd->bhqk', q2b, k2).astype(jnp.float32) * scale + bias
        p1 = jax.nn.softmax(jnp.where(causal, s1, -jnp.inf), axis=-1)
        p2 = jax.nn.softmax(jnp.where(causal, s2, -jnp.inf), axis=-1)
        attn = (p1 - lam * p2).astype(v.dtype)
        return jnp.einsum('bhqk,bhkd->bhqd', attn, v)

    o = sweep_query_blocks(block, q[:, :, 0], q[:, :, 1])
    o = rms_norm(o, out_g) * (1.0 - lam_init)
    return o.transpose(0, 2, 1, 3).reshape(bsz, seq, A_WIDTH)


def causal_depthwise_conv(u, w, b):
    c = u.shape[-1]
    y = lax.conv_general_dilated(u, w[:, None, :].astype(u.dtype), window_strides=(1,),
                                 padding=[(CONV_W - 1, 0)],
                                 dimension_numbers=('NWC', 'WIO', 'NWC'),
                                 feature_group_count=c)
    return y + b.astype(u.dtype)


def short_gated_conv(h, c_gate, b_gate, w, b):
    return b_gate * causal_depthwise_conv(c_gate * h, w, b)


def forgetting_attention(q, k, v, f_logit, f_bias, q_g, k_g):
    bsz, seq, _ = q.shape
    q = rms_norm(q.reshape(bsz, seq, C_HEADS, C_DIM), q_g).transpose(0, 2, 1, 3)
    k = rms_norm(k.reshape(bsz, seq, C_HEADS, C_DIM), k_g).transpose(0, 2, 1, 3)
    v = v.reshape(bsz, seq, C_HEADS, C_DIM).transpose(0, 2, 1, 3)
    log_f = jax.nn.log_sigmoid(f_logit.astype(jnp.float32) + f_bias.astype(jnp.float32))
    cum = jnp.cumsum(log_f, axis=1).transpose(0, 2, 1)
    scale = C_DIM ** -0.5
    k_pos = jnp.arange(seq, dtype=jnp.int32)

    def block(start, qb, cqb):
        q_pos = start + jnp.arange(Q_BLOCK, dtype=jnp.int32)
        causal = q_pos[:, None] >= k_pos[None, :]
        s = jnp.einsum('bhqd,bhkd->bhqk', qb, k).astype(jnp.float32) * scale
        s = s + (cqb[..., :, None] - cum[:, :, None, :])
        p = jax.nn.softmax(jnp.where(causal, s, -jnp.inf), axis=-1).astype(v.dtype)
        return jnp.einsum('bhqk,bhkd->bhqd', p, v)

    o = sweep_query_blocks(block, q, cum)
    return o.transpose(0, 2, 1, 3).reshape(bsz, seq, C_WIDTH)


def memory_cross_attention(q, mem, mem_g, w_mem_kv, q_g, k_g):
    bsz, seq, _ = q.shape
    m = mem.shape[1]
    mk, mv = jnp.split(rms_norm(mem, mem_g) @ w_mem_kv, 2, axis=-1)
    q = rms_norm(q.reshape(bsz, seq, X_HEADS, X_DIM), q_g)
    mk = rms_norm(mk.reshape(bsz, m, X_HEADS, X_DIM), k_g)
    mv = mv.reshape(bsz, m, X_HEADS, X_DIM)
    s = jnp.einsum('bshd,bmhd->bhsm', q, mk).astype(jnp.float32) * (X_DIM ** -0.5)
    p = jax.nn.softmax(s, axis=-1).astype(mv.dtype)
    o = jnp.einsum('bhsm,bmhd->bshd', p, mv)
    return o.reshape(bsz, seq, X_WIDTH)


def even_layer(h, mem, layer, norm_g, w_in, w_out, a_qn, a_kn, a_lam, a_on,
               b_cw, b_cb, x_qn, x_kn, mem_g, w_mem_kv):
    u = rms_norm(h, norm_g)
    proj = u @ w_in
    cuts = np.cumsum([A_WIDTH] * 4 + [B_WIDTH] * 4 + [X_WIDTH] * 2)[:-1].tolist()
    aq, ak, av, az, bh, bc, bb, bz, xq, xz = jnp.split(proj, cuts, axis=-1)
    lam_init = 0.8 - 0.6 * math.exp(-0.3 * layer)
    ya = differential_attention(aq, ak, av, a_qn, a_kn, a_lam, a_on, lam_init)
    yb = short_gated_conv(bh, bc, bb, b_cw, b_cb)
    yx = memory_cross_attention(xq, mem, mem_g, w_mem_kv, x_qn, x_kn)
    y = jnp.concatenate([ya * jax.nn.silu(az), yb * jax.nn.silu(bz), yx * jax.nn.silu(xz)], axis=-1)
    return h + y @ w_out


def odd_layer(h, mem, norm_g, w_in, w_out, c_qn, c_kn, c_fb, x_qn, x_kn, mem_g, w_mem_kv):
    u = rms_norm(h, norm_g)
    proj = u @ w_in
    cuts = np.cumsum([C_WIDTH] * 4 + [C_HEADS] + [X_WIDTH] * 2)[:-1].tolist()
    cq, ck, cv, cz, cf, xq, xz = jnp.split(proj, cuts, axis=-1)
    yc = forgetting_attention(cq, ck, cv, cf, c_fb, c_qn, c_kn)
    yx = memory_cross_attention(xq, mem, mem_g, w_mem_kv, x_qn, x_kn)
    y = jnp.concatenate([yc * jax.nn.silu(cz), yx * jax.nn.silu(xz)], axis=-1)
    return h + y @ w_out


def setup_inputs(seed: int = 0) -> dict:
    key = jax.random.key(seed)
    ks = jax.random.split(key, 32)
    it = iter(range(32))
    ne = (DEPTH + 1) // 2
    no = DEPTH // 2

    def nrm(shape, scale):
        return scale * jax.random.normal(ks[next(it)], shape, jnp.float32)

    def gain(shape):
        return 1.0 + 0.02 * jax.random.normal(ks[next(it)], shape, jnp.float32)

    return {
        "x": nrm((BATCH, SEQ, D_MODEL), 1.0),
        "mem": nrm((BATCH, MEM_LEN, D_MODEL), 1.0),
        "e_norm_g": gain((ne, D_MODEL)),
        "e_w_in": nrm((ne, D_MODEL, EVEN_IN), D_MODEL ** -0.5),
        "e_w_out": nrm((ne, EVEN_MIX, D_MODEL), EVEN_MIX ** -0.5),
        "e_a_q_norm_g": gain((ne, A_QK_DIM)),
        "e_a_k_norm_g": gain((ne, A_QK_DIM)),
        "e_a_lambda": nrm((ne, 4, A_QK_DIM), 0.1),
        "e_a_out_norm_g": gain((ne, A_V_DIM)),
        "e_b_conv_w": nrm((ne, CONV_W, B_WIDTH), CONV_W ** -0.5),
        "e_b_conv_b": nrm((ne, B_WIDTH), 0.02),
        "e_x_q_norm_g": gain((ne, X_DIM)),
        "e_x_k_norm_g": gain((ne, X_DIM)),
        "e_mem_norm_g": gain((ne, D_MODEL)),
        "e_w_mem_kv": nrm((ne, D_MODEL, 2 * X_WIDTH), D_MODEL ** -0.5),
        "o_norm_g": gain((no, D_MODEL)),
        "o_w_in": nrm((no, D_MODEL, ODD_IN), D_MODEL ** -0.5),
        "o_w_out": nrm((no, ODD_MIX, D_MODEL), ODD_MIX ** -0.5),
        "o_c_q_norm_g": gain((no, C_DIM)),
        "o_c_k_norm_g": gain((no, C_DIM)),
        "o_c_forget_b": 3.0 + nrm((no, C_HEADS), 0.5),
        "o_x_q_norm_g": gain((no, X_DIM)),
        "o_x_k_norm_g": gain((no, X_DIM)),
        "o_mem_norm_g": gain((no, D_MODEL)),
        "o_w_mem_kv": nrm((no, D_MODEL, 2 * X_WIDTH), D_MODEL ** -0.5),
    }


def reference(x, mem, e_norm_g, e_w_in, e_w_out, e_a_q_norm_g, e_a_k_norm_g, e_a_lambda,
              e_a_out_norm_g, e_b_conv_w, e_b_conv_b, e_x_q_norm_g, e_x_k_norm_g,
              e_mem_norm_g, e_w_mem_kv, o_norm_g, o_w_in, o_w_out, o_c_q_norm_g,
              o_c_k_norm_g, o_c_forget_b, o_x_q_norm_g, o_x_k_norm_g, o_mem_norm_g,
              o_w_mem_kv):
    h = x
    for layer in range(DEPTH):
        i = layer // 2
        if layer % 2 == 0:
            h = even_layer(h, mem, layer, e_norm_g[i], e_w_in[i], e_w_out[i],
                           e_a_q_norm_g[i], e_a_k_norm_g[i], e_a_lambda[i], e_a_out_norm_g[i],
                           e_b_conv_w[i], e_b_conv_b[i], e_x_q_norm_g[i], e_x_k_norm_g[i],
                           e_mem_norm_g[i], e_w_mem_kv[i])
        else:
            h = odd_layer(h, mem, o_norm_g[i], o_w_in[i], o_w_out[i],
                          o_c_q_norm_g[i], o_c_k_norm_g[i], o_c_forget_b[i],
                          o_x_q_norm_g[i], o_x_k_norm_g[i], o_mem_norm_g[i], o_w_mem_kv[i])
    return h
```

```python
import contextlib
import math
import numpy as np
import ml_dtypes
import concourse.bass as bass
import concourse.mybir as mybir
from concourse.bass_utils import run_bass_kernel_spmd

F32 = mybir.dt.float32
BF16 = mybir.dt.bfloat16
AF = mybir.ActivationFunctionType
ALU = mybir.AluOpType
AX = mybir.AxisListType

ENGS = ("pe", "act", "dve", "pool", "sp")
EPS = 1e-6
S_LEN = 2048
D = 1024
NT = 16
NB = 4


class Op:
    __slots__ = ("eng", "fn", "reads", "writes", "dma", "idx", "sig", "deps", "dsem", "dtarget", "dprev")

    def __init__(self, eng, fn, reads=(), writes=(), dma=False):
        self.eng = eng
        self.fn = fn
        self.reads = tuple(reads)
        self.writes = tuple(writes)
        self.dma = dma
        self.sig = None
        self.deps = ()
        self.dsem = None


class Sched:
    NDMA_SEMS = 8

    def __init__(self, same_engine_sync=True):
        self.ops = []
        self.same_engine_sync = same_engine_sync

    def add(self, ops):
        if isinstance(ops, Op):
            self.ops.append(ops)
        else:
            for o in ops:
                self.add(o)

    def plan(self):
        last_w = {}
        readers = {}
        ops = self.ops
        for i, op in enumerate(ops):
            op.idx = i
            deps = set()
            for r in op.reads:
                w = last_w.get(r)
                if w is not None:
                    deps.add(w)
            for wr in op.writes:
                w = last_w.get(wr)
                if w is not None:
                    deps.add(w)
                rs = readers.get(wr)
                if rs:
                    deps.update(rs)
            deps.discard(i)
            for r in op.reads:
                readers.setdefault(r, []).append(i)
            for wr in op.writes:
                last_w[wr] = i
                readers[wr] = []
            latest = {}
            dmadeps = []
            for dix in deps:
                d = ops[dix]
                if d.dma:
                    dmadeps.append(dix)
                else:
                    if d.eng == op.eng and not op.dma:
                        if op.eng == "pe" or not self.same_engine_sync:
                            continue
                    if d.eng not in latest or latest[d.eng] < dix:
                        latest[d.eng] = dix
            op.deps = tuple(sorted(list(latest.values()) + dmadeps))
        need = set()
        for op in ops:
            for dix in op.deps:
                if not ops[dix].dma:
                    need.add(dix)
        cnt = {e: 0 for e in ENGS}
        for op in ops:
            if op.dma:
                continue
            if op.idx in need:
                cnt[op.eng] += 1
                op.sig = cnt[op.eng]
        dcount = {e: 0 for e in ENGS}
        for op in ops:
            if op.dma:
                n = dcount[op.eng]
                dcount[op.eng] += 1
                op.dsem = (op.eng, n % self.NDMA_SEMS)
                op.dtarget = 16 * (n // self.NDMA_SEMS + 1)
                op.dprev = 16 * (n // self.NDMA_SEMS)
        self.sigcount = cnt
        self.dmacount = dcount

    def emit(self, block, esems, dsems):
        handles = {"pe": block.tensor, "act": block.scalar, "dve": block.vector, "pool": block.gpsimd,
                   "sp": block.sync}
        ops = self.ops
        for eng in ENGS:
            mine = [op for op in ops if op.eng == eng]

            def body(e, mine=mine, eng=eng):
                waited = {}

                def wait(key, sem, val):
                    if waited.get(key, 0) >= val:
                        return
                    waited[key] = val
                    e.wait_ge(sem, val)

                for op in mine:
                    for dix in op.deps:
                        d = ops[dix]
                        if d.dma:
                            wait(d.dsem, dsems[d.dsem], d.dtarget)
                        else:
                            wait(d.eng, esems[d.eng], d.sig)
                    if op.dma and op.dprev > 0:
                        wait(op.dsem, dsems[op.dsem], op.dprev)
                    if op.fn is None:
                        continue
                    ins = op.fn(e)
                    if op.dma:
                        ins.then_inc(dsems[op.dsem], 16)
                    elif op.sig is not None:
                        ins.then_inc(esems[eng], 1)

            handles[eng](body)


def interleave(units, fillers):
    out = []
    nu, nf = len(units), len(fillers)
    if nu == 0:
        for f in fillers:
            out.extend(f)
        return out
    fi = 0
    for u in range(nu):
        out.extend(units[u])
        tgt = ((u + 1) * nf) // nu
        while fi < tgt:
            out.extend(fillers[fi])
            fi += 1
    return out


def _consts():
    bf = ml_dtypes.bfloat16
    tri = (np.arange(128)[None, :] >= np.arange(128)[:, None]).astype(np.float32)
    bd = np.zeros((128, 128), np.float32)
    bd[:64, :64] = 1.0
    bd[64:, 64:] = 1.0
    ident = np.eye(128, dtype=np.float32)
    ones = np.ones((128, 128), np.float32)
    cst = np.concatenate([tri, bd, ident, ones], axis=1).astype(bf)
    pos = np.arange(S_LEN)
    qa = np.stack([(pos % 128).astype(np.float32), (pos // 128).astype(np.float32),
                   np.ones(S_LEN, np.float32), np.ones(S_LEN, np.float32)], 0)
    ka = np.zeros((8, 4, S_LEN), np.float32)
    for h in range(8):
        sl = 2.0 ** (-(h + 1))
        ka[h, 0] = -sl
        ka[h, 1] = -sl * 128.0
        ka[h, 2] = sl * (pos % 128)
        ka[h, 3] = sl * 128.0 * (pos // 128)
    assert np.array_equal(ka.astype(bf).astype(np.float32), ka)
    assert np.array_equal(qa.astype(bf).astype(np.float32), qa)
    sel = np.zeros((76, 12 * 128), np.float32)
    for h in range(12):
        for r in (h, 32 + h, 64 + h):
            sel[r, h * 128:(h + 1) * 128] = 1.0
    identf = np.eye(128, dtype=np.float32)
    return {"c_cst": cst, "c_qa": qa.astype(bf), "c_ka": ka.astype(bf), "c_sel": sel.astype(bf),
            "c_identf": identf}


L0_NAMES = ["e_norm_g", "e_w_in", "e_w_out", "e_a_q_norm_g", "e_a_k_norm_g", "e_a_lambda", "e_a_out_norm_g",
            "e_b_conv_w", "e_b_conv_b", "e_x_q_norm_g", "e_x_k_norm_g", "e_mem_norm_g", "e_w_mem_kv"]
L1_NAMES = ["o_norm_g", "o_w_in", "o_w_out", "o_c_q_norm_g", "o_c_k_norm_g", "o_c_forget_b", "o_x_q_norm_g",
            "o_x_k_norm_g", "o_mem_norm_g", "o_w_mem_kv"]
SHAPES = {
    "e_norm_g": [1024], "e_w_in": [1024, 7168], "e_w_out": [2048, 1024], "e_a_q_norm_g": [64],
    "e_a_k_norm_g": [64], "e_a_lambda": [4, 64], "e_a_out_norm_g": [128], "e_b_conv_w": [3, 512],
    "e_b_conv_b": [512], "e_x_q_norm_g": [128], "e_x_k_norm_g": [128], "e_mem_norm_g": [1024],
    "e_w_mem_kv": [1024, 1024],
    "o_norm_g": [1024], "o_w_in": [1024, 7180], "o_w_out": [2048, 1024], "o_c_q_norm_g": [128],
    "o_c_k_norm_g": [128], "o_c_forget_b": [12], "o_x_q_norm_g": [128], "o_x_k_norm_g": [128],
    "o_mem_norm_g": [1024], "o_w_mem_kv": [1024, 1024],
}

PJ, PB, S0, S1, A0, A1, A2, A3 = range(8)


class Gen:
    def __init__(self, layers, interleave_on=True):
        self.layers = tuple(layers)
        self.il = interleave_on
        self.nc = bass.Bass("TRN2", target_bir_lowering=False)
        self.S = Sched()
        self.stack = contextlib.ExitStack()
        self.sb_bytes = 0
        self.arena_keys = set()
        self.wctr = 0
        self.pctr = 0
        self.sctr = 0

    def sb(self, name, shape, dt):
        n = 1
        for s in shape[1:]:
            n *= s
        self.sb_bytes += n * (4 if dt == F32 else 2)
        return self.stack.enter_context(self.nc.sbuf_tensor(name, shape, dt))

    def carve(self, nbytes):
        off = self.ar_off
        self.ar_off += nbytes
        assert self.ar_off <= self.AR_BYTES, (self.ar_off, self.AR_BYTES)
        return off

    def arv(self, off, shape, dt):
        n = 1
        for s in shape:
            n *= s
        nb = n * (4 if dt == F32 else 2)
        v = self.AR[:, off // 2:(off + nb) // 2]
        if dt == F32:
            v = v.bitcast(F32)
        if len(shape) == 2:
            v = v.rearrange("p (a b) -> p a b", a=shape[0])
        elif len(shape) == 3:
            v = v.rearrange("p (a b c) -> p a b c", a=shape[0], b=shape[1])
        return v

    def ak(self, *key):
        self.arena_keys.add(key)
        return key

    def op(self, eng, reads, writes, fn, dma=False):
        return Op(eng, fn, reads, writes, dma)

    def fence(self):
        keys = sorted(self.arena_keys, key=repr)
        d = self.dummy
        return [Op("pool", lambda e: e.memset(d[:], 0.0), reads=(), writes=keys)]

    def build(self):
        nc = self.nc
        T = {}
        T["x"] = nc.dram_tensor("x", [S_LEN, D], F32, kind="ExternalInput").ap()
        T["mem"] = nc.dram_tensor("mem", [256, D], F32, kind="ExternalInput").ap()
        names = []
        if 0 in self.layers:
            names += L0_NAMES
        if 1 in self.layers:
            names += L1_NAMES
        for n in names:
            T[n] = nc.dram_tensor(n, SHAPES[n], F32, kind="ExternalInput").ap()
        T["c_cst"] = nc.dram_tensor("c_cst", [128, 512], BF16, kind="ExternalInput").ap()
        T["c_qa"] = nc.dram_tensor("c_qa", [4, S_LEN], BF16, kind="ExternalInput").ap()
        T["c_ka"] = nc.dram_tensor("c_ka", [8, 4, S_LEN], BF16, kind="ExternalInput").ap()
        T["c_sel"] = nc.dram_tensor("c_sel", [76, 1536], BF16, kind="ExternalInput").ap()
        T["c_identf"] = nc.dram_tensor("c_identf", [128, 128], F32, kind="ExternalInput").ap()
        T["out"] = nc.dram_tensor("out", [S_LEN, D], F32, kind="ExternalOutput").ap()
        self.T = T
        self.in_names = ["x", "mem"] + names + ["c_cst", "c_qa", "c_ka", "c_sel", "c_identf"]

        st = self.stack
        sb = self.sb
        self.h = sb("h", [128, NT, D], F32)
        self.UT = sb("UT", [128, 8, S_LEN], BF16)
        self.KA = [sb(f"KA{s}", [128, S_LEN], BF16) for s in range(2)]
        self.KB = [sb(f"KB{s}", [128, S_LEN], BF16) for s in range(2)]
        self.V = [sb(f"V{s}", [128, NT, 128], BF16) for s in range(2)]
        self.W = [[sb(f"W{s}_{k}", [128, 8, 128], BF16) for k in range(4)] for s in range(2)]
        self.WO = sb("WO", [128, 4, D], BF16)
        self.cols = sb("cols", [128, 64], F32)
        self.cst = sb("cst", [128, 512], BF16)
        self.MKT = sb("MKT", [128, 4, 256], BF16)
        self.MV = sb("MV", [128, 2, 512], BF16)
        self.Csplit = sb("Csplit", [128, S_LEN], BF16)
        self.kb = sb("kb", [128, 192], F32)
        self.sel = sb("sel", [128, 1536], BF16)
        self.identf = sb("identf", [128, 128], F32)
        self.mhalf = sb("mhalf", [128, 512], F32)
        self.dummy = sb("dmy_t", [128, 2], F32)
        self.small = sb("small", [128, 64], F32)
        self.AR_BYTES = 45056
        self.AR = sb("AR", [128, self.AR_BYTES // 2], BF16)
        assert self.sb_bytes <= 210000, self.sb_bytes
        self.ar_off = 0
        c = self.carve
        self.YT = self.arv(c(16384), [4, S_LEN], BF16)
        self.QA = [self.arv(c(1024), [512], BF16) for _ in range(4)]
        self.QB = [self.arv(c(1024), [512], BF16) for _ in range(4)]
        self.ZS = [self.arv(c(2048), [512], F32) for _ in range(2)]
        self.t1 = self.arv(c(2048), [512], F32)
        self.t2 = self.arv(c(2048), [512], F32)
        xoff = self.ar_off
        self.P = [self.arv(c(1024), [512], BF16) for _ in range(4)]
        self.t3 = self.arv(c(2048), [512], F32)
        self.sq = [self.arv(c(1024), [512], BF16) for _ in range(2)]
        self.sse = self.arv(c(2048), [512], F32)
        self.rstd = self.arv(c(2048), [512], F32)
        assert self.ar_off <= self.AR_BYTES
        self.xpad = self.arv(xoff, [2064], F32)
        self.ar_off = 0
        self.gbc = self.arv(c(4096), [D], F32)
        self.Usc = [self.arv(c(2048), [D], BF16) for _ in range(2)]
        woff = self.ar_off
        self.WMh = self.arv(c(8192), [8, 512], BF16)
        self.ncum = self.arv(woff, [S_LEN], F32)
        self.memf = self.arv(c(4096), [D], F32)
        self.memUT = self.arv(c(4096), [8, 256], BF16)
        foff = self.ar_off
        self.fx = [self.arv(c(2048), [512], F32) for _ in range(3)]
        self.junk = self.arv(foff, [D], BF16)
        self.fxb = [self.arv(c(1024), [512], BF16) for _ in range(2)]
        self.WF = self.arv(c(2048), [8, 128], BF16)
        self.onesf = self.arv(c(2048), [512], F32)
        assert self.ar_off <= 38912, self.ar_off

        self.ps = st.enter_context(nc.psum_tensor("ps", [128, 8, 512], F32))
        self.esems = {e: st.enter_context(nc.semaphore(f"s_{e}")) for e in ENGS}
        self.dsems = {(e, k): st.enter_context(nc.semaphore(f"d_{e}{k}"))
                      for e in ("sp", "pool") for k in range(Sched.NDMA_SEMS)}

        S = self.S
        segs = [self.setup_ops()]
        for li, L in enumerate(self.layers):
            segs.append("fence")
            segs.append(self.prologue(L))
            segs.append("fence")
            segs.append(self.main_phase(L))
        segs.append(self.store_out())
        for sg in segs:
            S.add(self.fence() if isinstance(sg, str) else sg)
        S.plan()
        with nc.Block() as block:
            S.emit(block, self.esems, self.dsems)
        self.stack.close()
        return nc

    def bank(self, b, rows=128, c0=0, c1=512):
        return self.ps[0:rows, b, c0:c1]

    def col(self, j, rows=128, r0=0):
        return self.cols[r0:r0 + rows, j:j + 1]

    def hk(self, t):
        return [("h", t, 0), ("h", t, 1)]

    def setup_ops(self):
        T = self.T
        o = []
        op = self.op
        h = self.h
        for t in range(NT):
            o.append(op("sp", [], self.hk(t), lambda e, t=t: e.dma_start(out=h[:, t, :], in_=T["x"][t * 128:(t + 1) * 128, :]), dma=True))
        o.append(op("sp", [], ["cst"], lambda e: e.dma_start(out=self.cst[:], in_=T["c_cst"]), dma=True))
        o.append(op("sp", [], ["sel"], lambda e: e.dma_start(out=self.sel[0:76, :], in_=T["c_sel"]), dma=True))
        o.append(op("sp", [], ["identf"], lambda e: e.dma_start(out=self.identf[:], in_=T["c_identf"]), dma=True))
        o.append(op("pool", [], ["mhalf"], lambda e: e.memset(self.mhalf[:], -0.5)))
        o.append(op("pool", [], ["dummy"], lambda e: e.memset(self.dummy[:], 0.0)))
        o.append(op("pool", [], [("col", j) for j in range(64)], lambda e: e.memset(self.cols[:], 0.0)))
        for s in range(2):
            o.append(op("pool", [], [("K", s, b) for b in range(NB)] + [("Kaug", s)], lambda e, s=s: e.memset(self.KB[s][0:64, :], 0.0)))
        return o

    def consts(self):
        cst = self.cst
        return {"tri": cst[:, 0:128], "bd": cst[:, 128:256], "ident": cst[:, 256:384], "ones": cst[:, 384:512]}

    def small_rstd(self, src_key, src_ap, dst_key, dst_ap, n, scale, eps):
        o = []
        tmpk = ("small_tmp",)
        tmp = self.small[:, 32:32 + n]
        o.append(self.op("dve", [src_key], [tmpk], lambda e: e.tensor_scalar(out=tmp, in0=src_ap, scalar1=scale, scalar2=eps, op0=ALU.mult, op1=ALU.add)))
        o.append(self.op("pool", [tmpk, "mhalf"], [dst_key], lambda e: e.tensor_tensor(out=dst_ap, in0=tmp, in1=self.mhalf[:, 0:n], op=ALU.pow)))
        return o

    def norm_rows_to_T(self, src_key_list, src_ap, rstd_col_key, rstd_col, ui, dst_writes, dst_ap_fn, tb):
        o = []
        C = self.consts()
        uk = self.ak("Usc", ui)
        U = self.Usc[ui]
        o.append(self.op("dve", list(src_key_list) + [rstd_col_key, self.ak("gbc")], [uk],
                         lambda e: e.scalar_tensor_tensor(out=U, in0=src_ap, scalar=rstd_col, in1=self.gbc, op0=ALU.mult, op1=ALU.mult)))
        psT = self.ps[:, tb, :].bitcast(BF16)
        for c in range(8):
            o.append(self.op("pe", [uk, "cst"], [("ps", tb)],
                             lambda e, c=c: e.transpose(out=psT[:, c * 128:(c + 1) * 128], in_=U[:, c * 128:(c + 1) * 128], identity=C["ident"])))
        o.append(self.op("act", [("ps", tb)], dst_writes,
                         lambda e: e.activation(out=dst_ap_fn(), in_=psT.rearrange("p (c j) -> p c j", c=8), func=AF.Copy)))
        return o

    def prologue(self, L):
        T = self.T
        op = self.op
        pre = "e_" if L == 0 else "o_"
        o = []
        C = self.consts()
        h = self.h
        cols = self.cols
        def vec_col(name, j, n=128, r0=0, src_off=0):
            src = T[name]
            ap = src[src_off:src_off + n].rearrange("(p o) -> p o", o=1)
            return op("sp", [], [("col", j)], lambda e: e.dma_start(out=cols[r0:r0 + n, j:j + 1], in_=ap), dma=True)

        def scale_col(j, f):
            return op("dve", [("col", j)], [("col", j)], lambda e: e.tensor_scalar(out=cols[:, j:j + 1], in0=cols[:, j:j + 1], scalar1=float(f), scalar2=None, op0=ALU.mult))

        if L == 0:
            for r0 in (0, 64):
                o.append(vec_col("e_a_q_norm_g", 0, 64, r0))
                o.append(vec_col("e_a_k_norm_g", 1, 64, r0))
            o.append(scale_col(1, 8.0))
            o.append(vec_col("e_a_out_norm_g", 2))
            lam_init = 0.8 - 0.6 * math.exp(-0.3 * 0)
            o.append(scale_col(2, math.sqrt(128.0) * (1.0 - lam_init) * 0.5))
            lpb = self.fx[0][:, 0:256]
            lpk = self.ak("fx", 0)
            o.append(op("sp", [], [lpk], lambda e: e.dma_start(out=lpb, in_=T["e_a_lambda"].rearrange("a b -> (a b)").rearrange("(o n) -> o n", o=1).partition_broadcast(128)), dma=True))
            prod = self.fx[1][:, 0:128]
            pk = self.ak("fx", 1)
            lp4 = lpb.rearrange("p (a b) -> p a b", a=4)
            for q in range(2):
                o.append(op("dve", [lpk], [pk], lambda e, q=q: e.tensor_tensor(out=prod[:, q * 64:(q + 1) * 64], in0=lp4[:, 2 * q, :], in1=lp4[:, 2 * q + 1, :], op=ALU.mult)))
            sm = self.small
            o.append(op("dve", [pk], [("sm", 0)], lambda e: e.reduce_sum(out=sm[:, 0:2], in_=prod.rearrange("p (a b) -> p a b", a=2), axis=AX.X)))
            o.append(op("act", [("sm", 0)], [("sm", 1)], lambda e: e.activation(out=sm[:, 2:4], in_=sm[:, 0:2], func=AF.Exp)))
            o.append(op("dve", [("sm", 1)], [("sm", 2)], lambda e: e.tensor_tensor(out=sm[:, 4:5], in0=sm[:, 2:3], in1=sm[:, 3:4], op=ALU.subtract)))
            o.append(op("dve", [("sm", 2)], [("col", 5)], lambda e: e.tensor_scalar(out=cols[:, 5:6], in0=sm[:, 4:5], scalar1=lam_init, scalar2=-1.0, op0=ALU.add, op1=ALU.mult)))
            for cc in range(4):
                for k in range(3):
                    o.append(op("sp", [], [("col", 6 + cc * 4 + k)], lambda e, cc=cc, k=k: e.dma_start(out=cols[:, 6 + cc * 4 + k:7 + cc * 4 + k], in_=T["e_b_conv_w"][k, cc * 128:(cc + 1) * 128].rearrange("(p o) -> p o", o=1)), dma=True))
                o.append(op("sp", [], [("col", 6 + cc * 4 + 3)], lambda e, cc=cc: e.dma_start(out=cols[:, 9 + cc * 4:10 + cc * 4], in_=T["e_b_conv_b"][cc * 128:(cc + 1) * 128].rearrange("(p o) -> p o", o=1)), dma=True))
        else:
            o.append(vec_col("o_c_q_norm_g", 0))
            o.append(vec_col("o_c_k_norm_g", 1))
            o.append(scale_col(1, math.sqrt(128.0)))
        o.append(vec_col(pre + "x_q_norm_g", 3))
        o.append(vec_col(pre + "x_k_norm_g", 4))
        o.append(scale_col(4, math.sqrt(128.0)))

        gk = self.ak("gbc")
        o.append(op("sp", [], [gk], lambda e: e.dma_start(out=self.gbc, in_=T[pre + "mem_norm_g"].rearrange("(o n) -> o n", o=1).partition_broadcast(128)), dma=True))
        mfk = self.ak("memf")
        mutk = self.ak("memUT")
        sm = self.small
        for mt in range(2):
            o.append(op("sp", [], [mfk], lambda e, mt=mt: e.dma_start(out=self.memf, in_=T["mem"][mt * 128:(mt + 1) * 128, :]), dma=True))
            jk = self.ak("fx", 0)
            o.append(op("act", [mfk], [jk, ("sm", 10)], lambda e: e.activation(out=self.junk, in_=self.memf, func=AF.Square, accum_out=sm[:, 10:11])))
            o += self.small_rstd(("sm", 10), sm[:, 10:11], ("sm", 11), sm[:, 11:12], 1, 1.0 / D, EPS)
            tb = A0 + mt
            o += self.norm_rows_to_T([mfk], self.memf, ("sm", 11), sm[:, 11:12], mt, [mutk],
                                     lambda mt=mt: self.memUT[:, :, mt * 128:(mt + 1) * 128], tb)
        wmk = self.ak("WMh")
        wsrc = T[pre + "w_mem_kv"]
        o.append(op("pool", [], [wmk], lambda e: e.dma_start(out=self.WMh, in_=wsrc[:, 0:512].rearrange("(c p) j -> p c j", p=128)), dma=True))
        for hx in range(4):
            pj = self.bank(PJ, 128, 0, 256)
            for c in range(8):
                o.append(op("pe", [wmk, mutk], [("ps", PJ)], lambda e, c=c, hx=hx: e.matmul(pj, lhsT=self.WMh[:, c, hx * 128:(hx + 1) * 128], rhs=self.memUT[:, c, :], start=(c == 0), stop=(c == 7))))
            o += self.norm_chain(PJ, 256, C["ones"], 128, 4, [("MKT", hx)],
                                 [(0, 128, self.MKT[:, hx, :])], 0)
        o.append(op("pool", [], [wmk], lambda e: e.dma_start(out=self.WMh, in_=wsrc[:, 512:1024].rearrange("(c p) j -> p c j", p=128)), dma=True))
        for mt in range(2):
            pv = self.bank(PJ)
            for c in range(8):
                o.append(op("pe", [wmk, mutk], [("ps", PJ)], lambda e, c=c, mt=mt: e.matmul(pv, lhsT=self.memUT[:, c, mt * 128:(mt + 1) * 128], rhs=self.WMh[:, c, :], start=(c == 0), stop=(c == 7))))
            o.append(op("act", [("ps", PJ)], [("MV", mt)], lambda e, mt=mt: e.activation(out=self.MV[:, mt, :], in_=pv, func=AF.Copy)))

        o.append(op("sp", [], [gk], lambda e: e.dma_start(out=self.gbc, in_=T[pre + "norm_g"].rearrange("(o n) -> o n", o=1).partition_broadcast(128)), dma=True))
        jk = self.ak("fx", 0)
        for t in range(NT):
            o.append(op("act", self.hk(t), [jk, ("hss", t)], lambda e, t=t: e.activation(out=self.junk, in_=h[:, t, :], func=AF.Square, accum_out=sm[:, 12 + t:13 + t])))
        o.append(op("dve", [("hss", t) for t in range(NT)], [("small_tmp",)], lambda e: e.tensor_scalar(out=sm[:, 32:48], in0=sm[:, 12:28], scalar1=1.0 / D, scalar2=EPS, op0=ALU.mult, op1=ALU.add)))
        o.append(op("pool", [("small_tmp",), "mhalf"], [("hrstd",)], lambda e: e.tensor_tensor(out=sm[:, 48:64], in0=sm[:, 32:48], in1=self.mhalf[:, 0:16], op=ALU.pow)))
        for t in range(NT):
            o += self.norm_rows_to_T(self.hk(t), h[:, t, :], ("hrstd",), sm[:, 48 + t:49 + t], t % 2, [("UT", t)],
                                     lambda t=t: self.UT[:, :, t * 128:(t + 1) * 128], A0 + (t % 4))
        if L == 1:
            o += self.fox_prologue()
        return o

    def norm_chain(self, pbank, n, ones_ap, dsz, gcol, dst_writes, dsts, si):
        op = self.op
        o = []
        pj = self.bank(pbank, 128, 0, n)
        sq = self.sq[si][:, 0:n]
        sse = self.sse[:, 0:n]
        rstd = self.rstd[:, 0:n]
        sqk, ssek, rk = self.ak("sq", si), self.ak("sse"), self.ak("rstd")
        pb = self.bank(PB, 128, 0, n)
        o.append(op("act", [("ps", pbank)], [sqk], lambda e: e.activation(out=sq, in_=pj, func=AF.Square)))
        o.append(op("pe", [sqk, "cst"], [("ps", PB)], lambda e: e.matmul(pb, lhsT=ones_ap, rhs=sq, start=True, stop=True)))
        o.append(op("dve", [("ps", PB)], [ssek], lambda e: e.tensor_scalar(out=sse, in0=pb, scalar1=float(dsz * EPS), scalar2=None, op0=ALU.add)))
        o.append(op("pool", [ssek, "mhalf"], [rk], lambda e: e.tensor_tensor(out=rstd, in0=sse, in1=self.mhalf[:, 0:n], op=ALU.pow)))
        for (r0, nr, dst) in dsts:
            o.append(op("dve", [("ps", pbank), rk, ("col", gcol)], dst_writes,
                        lambda e, r0=r0, nr=nr, dst=dst: e.scalar_tensor_tensor(out=dst, in0=self.ps[r0:r0 + nr, pbank, 0:n], scalar=self.cols[r0:r0 + nr, gcol:gcol + 1], in1=self.rstd[r0:r0 + nr, 0:n], op0=ALU.mult, op1=ALU.mult)))
        return o

    def load_w(self, wname, col0, s, k, ncols=128):
        src = self.T[wname][:, col0:col0 + ncols].rearrange("(c p) j -> p c j", p=128)
        dst = self.W[s][k]
        return self.op("pool", [], [("W", s, k)], lambda e: e.dma_start(out=dst[:, :, 0:ncols], in_=src), dma=True)

    def proj_fm(self, s, k, b, pbank=PJ, n=512):
        o = []
        W = self.W[s][k]
        out = self.bank(pbank)
        for c in range(8):
            o.append(self.op("pe", [("W", s, k)] + [("UT", 4 * b + i) for i in range(4)], [("ps", pbank)],
                             lambda e, c=c: e.matmul(out, lhsT=W[:, c, :], rhs=self.UT[:, c, b * 512:(b + 1) * 512], start=(c == 0), stop=(c == 7))))
        return o

    def gate_ops(self, pbank, zi):
        o = []
        zk = self.ak("ZS", zi)
        Z = self.ZS[zi]
        pj = self.bank(pbank)
        o.append(self.op("act", [("ps", pbank)], [zk], lambda e: e.activation(out=Z, in_=pj, func=AF.Tanh, scale=0.5)))
        o.append(self.op("dve", [("ps", pbank), zk], [zk], lambda e: e.scalar_tensor_tensor(out=Z, in0=Z, scalar=1.0, in1=pj, op0=ALU.add, op1=ALU.mult)))
        return o

    def attn_block(self, units, post):
        LA = 2
        groups = []
        n = len(units)
        for u in range(n + LA):
            g = []
            if u < n:
                g += units[u]["qk"] + units[u]["exp"]
            if u - LA >= 0:
                g += units[u - LA]["pv"]
            groups.append(g)
        return groups, post

    def next_p(self):
        i = self.pctr % 4
        self.pctr += 1
        return i

    def next_s(self):
        i = self.sctr % 2
        self.sctr += 1
        return S0 + i

    def make_unit(self, kT, k_reads, qT, q_reads, n0, extra_mm, bias, bias_reads, diag, v_ap, v_reads, obank, lbank, first, last):
        op = self.op
        C = self.consts()
        sb = self.next_s()
        pi = self.next_p()
        P = self.P[pi]
        pk = self.ak("P", pi)
        sc = self.bank(sb, 128, n0, 512)
        qk = []
        qk.append(op("pe", k_reads + q_reads, [("ps", sb)], lambda e: e.matmul(sc, lhsT=kT, rhs=qT, start=True, stop=(extra_mm is None))))
        if extra_mm is not None:
            l2, r2, rd2 = extra_mm
            qk.append(op("pe", rd2, [("ps", sb)], lambda e: e.matmul(sc, lhsT=l2, rhs=r2, start=False, stop=True)))
        ex = []
        if bias is None:
            ex.append(op("act", [("ps", sb)], [pk], lambda e: e.activation(out=P[:, n0:512], in_=sc, func=AF.Exp)))
        else:
            ex.append(op("act", [("ps", sb)] + bias_reads, [pk], lambda e: e.activation(out=P[:, n0:512], in_=sc, func=AF.Exp, bias=bias)))
        if diag:
            ex.append(op("pool", [pk, "cst"], [pk], lambda e: e.tensor_tensor(out=P[:, n0:n0 + 128], in0=P[:, n0:n0 + 128], in1=C["tri"], op=ALU.mult)))
        pv = []
        ob = self.bank(obank, 128, n0, 512)
        lb = self.bank(lbank, 128, n0, 512)
        pv.append(op("pe", [pk] + v_reads, [("ps", obank)], lambda e: e.matmul(ob, lhsT=v_ap, rhs=P[:, n0:512], start=first, stop=last)))
        pv.append(op("pe", [pk, "cst"], [("ps", lbank)], lambda e: e.matmul(lb, lhsT=C["ones"], rhs=P[:, n0:512], start=first, stop=last)))
        return {"qk": qk, "exp": ex, "pv": pv}

    def outproj(self, L, g):
        op = self.op
        o = []
        wname = "e_w_out" if L == 0 else "o_w_out"
        src = self.T[wname][g * 512:(g + 1) * 512, :].rearrange("(c p) j -> p c j", p=128)
        o.append(op("pool", [], [("WO",)], lambda e: e.dma_start(out=self.WO[:], in_=src), dma=True))
        banks = [A0, A1, A2, A3]
        i = 0
        for t in range(NT):
            for n in range(2):
                bnk = banks[i % 4]
                i += 1
                out = self.bank(bnk)
                for c in range(4):
                    o.append(op("pe", [("WO",), self.ak("YT", c, t // 4)], [("ps", bnk)],
                                lambda e, c=c, t=t, n=n, out=out: e.matmul(out, lhsT=self.YT[:, c, t * 128:(t + 1) * 128], rhs=self.WO[:, c, n * 512:(n + 1) * 512], start=(c == 0), stop=(c == 3))))
                hv = self.h[:, t, n * 512:(n + 1) * 512]
                o.append(op("dve", [("ps", bnk), ("h", t, n)], [("h", t, n)], lambda e, hv=hv, out=out: e.tensor_tensor(out=hv, in0=out, in1=hv, op=ALU.add)))
        return o

    def prep_kv(self, L, s, ones_ap, dsz, split):
        op = self.op
        groups = []
        for b in range(NB):
            g1 = self.proj_fm(s, 1, b)
            if split:
                dsts = [(0, 64, self.KA[s][0:64, b * 512:(b + 1) * 512]), (64, 64, self.KB[s][64:128, b * 512:(b + 1) * 512])]
            else:
                dsts = [(0, 128, self.KA[s][:, b * 512:(b + 1) * 512])]
            g2 = self.norm_chain(PJ, 512, ones_ap, dsz, 1, [("K", s, b)], dsts, b % 2)
            groups.append(g1 + g2[:1])
            groups.append(g2[1:])
        Wv = self.W[s][2]
        for tg in range(4):
            g = []
            pv = self.bank(PJ)
            for ti in range(4):
                t = tg * 4 + ti
                for c in range(8):
                    g.append(op("pe", [("W", s, 2), ("UT", t)], [("ps", PJ)],
                                lambda e, c=c, t=t, ti=ti: e.matmul(self.ps[:, PJ, ti * 128:(ti + 1) * 128], lhsT=self.UT[:, c, t * 128:(t + 1) * 128], rhs=Wv[:, c, :], start=(c == 0), stop=(c == 7))))
            g.append(op("act", [("ps", PJ)], [("V", s, tg * 4 + ti) for ti in range(4)],
                        lambda e, tg=tg: e.activation(out=self.V[s][:, tg * 4:(tg + 1) * 4, :], in_=pv.rearrange("p (a b) -> p a b", a=4), func=AF.Copy)))
            groups.append(g)
        return groups

    def prep_q(self, L, s, j, ones_ap, dsz, split, gcol=0):
        g1 = self.proj_fm(s, 0, j)
        if split:
            dsts = [(0, 64, self.QA[j][0:64, :]), (64, 64, self.QB[j][64:128, :])]
        else:
            dsts = [(0, 128, self.QA[j][:, :])]
        g2 = self.norm_chain(PJ, 512, ones_ap, dsz, gcol, [self.ak("Q", j)], dsts, j % 2)
        return [g1 + g2[:1], g2[1:]]

    def prep_z(self, s, k, j):
        return [self.proj_fm(s, k, j), self.gate_ops(PJ, j % 2)]

    def attn_head(self, L, hd, slot, s):
        op = self.op
        C = self.consts()
        split = (L == 0)
        ones_ap = C["bd"] if split else C["ones"]
        dsz = 64 if split else 128
        kv = self.prep_kv(L, s, ones_ap, dsz, split)
        if L == 0:
            kv = [[op("sp", [], [("Kaug", s)], lambda e: e.dma_start(out=self.KA[s][64:68, :], in_=self.T["c_ka"][hd]), dma=True),
                   op("sp", [], [("Kaug", s)], lambda e: e.dma_start(out=self.KB[s][0:4, :], in_=self.T["c_ka"][hd]), dma=True)]] + kv
        blocks = []
        for j in range(NB):
            qz = self.prep_q(L, s, j, ones_ap, dsz, split) + self.prep_z(s, 3, j)
            units = []
            ntile = 4 * j + 4
            subs = (0, 1) if L == 0 else (0,)
            for i in range(ntile):
                n0 = max(0, i - 4 * j) * 128
                diag = i >= 4 * j
                for sub in subs:
                    if L == 0:
                        if sub == 0:
                            kT = self.KA[s][0:68, i * 128:(i + 1) * 128]
                            qT = self.QA[j][0:68, n0:512]
                        else:
                            kT = self.KB[s][:, i * 128:(i + 1) * 128]
                            qT = self.QB[j][:, n0:512]
                        obank, lbank = (A0, A2) if sub == 0 else (A1, A3)
                        u = self.make_unit(kT, [("K", s, i // 4), ("Kaug", s)], qT, [self.ak("Q", j), self.ak("Qaug")], n0, None, None, [], diag,
                                           self.V[s][:, i, :], [("V", s, i)], obank, lbank, i == 0, i == ntile - 1)
                    else:
                        kT = self.KA[s][:, i * 128:(i + 1) * 128]
                        qT = self.QA[j][:, n0:512]
                        extra = (self.sel[0:76, hd * 128:(hd + 1) * 128], self.Csplit[0:76, j * 512 + n0:(j + 1) * 512], ["sel", ("Csplit",)])
                        bias = self.kb[:, i * 12 + hd:i * 12 + hd + 1]
                        u = self.make_unit(kT, [("K", s, i // 4)], qT, [self.ak("Q", j)], n0, extra, bias, [("kb",)], diag,
                                           self.V[s][:, i, :], [("V", s, i)], A0, A1, i == 0, i == ntile - 1)
                    units.append(u)
            ugroups, _ = self.attn_block(units, None)
            post = []
            yk = self.ak("YT", slot, j)
            Y = self.YT[:, slot, j * 512:(j + 1) * 512]
            zk = self.ak("ZS", j % 2)
            Z = self.ZS[j % 2]
            t1, t2, t3 = self.t1, self.t2, self.t3
            k1, k2, k3 = self.ak("t1"), self.ak("t2"), self.ak("t3")
            if L == 1:
                post.append(op("dve", [("ps", A1)], [k1], lambda e: e.reciprocal(out=t1, in_=self.bank(A1))))
                post.append(op("dve", [("ps", A0), k1], [k1], lambda e: e.tensor_tensor(out=t1, in0=self.bank(A0), in1=t1, op=ALU.mult)))
                post.append(op("dve", [k1, zk], [yk], lambda e, Y=Y, Z=Z: e.scalar_tensor_tensor(out=Y, in0=t1, scalar=0.5, in1=Z, op0=ALU.mult, op1=ALU.mult)))
            else:
                post.append(op("dve", [("ps", A2)], [k1], lambda e: e.reciprocal(out=t1, in_=self.bank(A2))))
                post.append(op("dve", [("ps", A3)], [k2], lambda e: e.reciprocal(out=t2, in_=self.bank(A3))))
                post.append(op("dve", [("ps", A0), k1], [k1], lambda e: e.tensor_tensor(out=t1, in0=self.bank(A0), in1=t1, op=ALU.mult)))
                post.append(op("dve", [("ps", A1), k2], [k2], lambda e: e.tensor_tensor(out=t2, in0=self.bank(A1), in1=t2, op=ALU.mult)))
                post.append(op("dve", [k1, k2, ("col", 5)], [k1], lambda e: e.scalar_tensor_tensor(out=t1, in0=t2, scalar=self.cols[:, 5:6], in1=t1, op0=ALU.mult, op1=ALU.add)))
                sqk, ssek, rk = self.ak("sq", 0), self.ak("sse"), self.ak("rstd")
                sq, sse, rstd = self.sq[0], self.sse, self.rstd
                pb = self.bank(PB)
                post.append(op("act", [k1], [sqk], lambda e: e.activation(out=sq, in_=t1, func=AF.Square)))
                post.append(op("pe", [sqk, "cst"], [("ps", PB)], lambda e: e.matmul(pb, lhsT=C["ones"], rhs=sq, start=True, stop=True)))
                post.append(op("dve", [("ps", PB)], [ssek], lambda e: e.tensor_scalar(out=sse, in0=pb, scalar1=float(128 * EPS), scalar2=None, op0=ALU.add)))
                post.append(op("pool", [ssek, "mhalf"], [rk], lambda e: e.tensor_tensor(out=rstd, in0=sse, in1=self.mhalf[:, 0:512], op=ALU.pow)))
                post.append(op("dve", [k1, rk, ("col", 2)], [k1], lambda e: e.scalar_tensor_tensor(out=t1, in0=t1, scalar=self.cols[:, 2:3], in1=rstd, op0=ALU.mult, op1=ALU.mult)))
                post.append(op("dve", [k1, zk], [yk], lambda e, Y=Y, Z=Z: e.tensor_tensor(out=Y, in0=t1, in1=Z, op=ALU.mult)))
            blocks.append((qz, ugroups, post))
        return kv, blocks

    def xattn_head(self, L, hx, slot, s):
        op = self.op
        C = self.consts()
        blocks = []
        for j in range(NB):
            qz = self.prep_q(L, s, j, C["ones"], 128, False, gcol=3) + self.prep_z(s, 3, j)
            units = []
            for mt in range(2):
                kT = self.MKT[:, hx, mt * 128:(mt + 1) * 128]
                qT = self.QA[j][:, :]
                u = self.make_unit(kT, [("MKT", hx)], qT, [self.ak("Q", j)], 0, None, None, [], False,
                                   self.MV[:, mt, hx * 128:(hx + 1) * 128], [("MV", mt)], A0, A1, mt == 0, mt == 1)
                units.append(u)
            ugroups, _ = self.attn_block(units, None)
            post = []
            yk = self.ak("YT", slot, j)
            Y = self.YT[:, slot, j * 512:(j + 1) * 512]
            zk = self.ak("ZS", j % 2)
            Z = self.ZS[j % 2]
            t1 = self.t1
            k1 = self.ak("t1")
            post.append(op("dve", [("ps", A1)], [k1], lambda e: e.reciprocal(out=t1, in_=self.bank(A1))))
            post.append(op("dve", [("ps", A0), k1], [k1], lambda e: e.tensor_tensor(out=t1, in0=self.bank(A0), in1=t1, op=ALU.mult)))
            post.append(op("dve", [k1, zk], [yk], lambda e, Y=Y, Z=Z: e.scalar_tensor_tensor(out=Y, in0=t1, scalar=0.5, in1=Z, op0=ALU.mult, op1=ALU.mult)))
            blocks.append((qz, ugroups, post))
        return [], blocks

    def conv_chunk(self, cc, slot, s):
        op = self.op
        o = []
        xk = self.ak("xpad")
        xall = [xk] + [self.ak("P", i) for i in range(4)] + [self.ak("t3"), self.ak("sq", 0), self.ak("sq", 1), self.ak("sse")]
        xp = self.xpad
        cb = 6 + cc * 4
        o.append(op("pool", [], xall, lambda e: e.memset(xp[:, 0:2], 0.0)))
        t1, t2, t3 = self.t1, self.t2, self.t3
        k1, k2, k3 = self.ak("t1"), self.ak("t2"), self.ak("t3")
        for j in range(NB):
            yk = self.ak("YT", slot, j)
            Y = self.YT[:, slot, j * 512:(j + 1) * 512]
            o += self.proj_fm(s, 1, j)
            o.append(op("act", [("ps", PJ)], [k1], lambda e: e.activation(out=t1, in_=self.bank(PJ), func=AF.Copy)))
            o += self.proj_fm(s, 0, j, pbank=PB)
            o.append(op("dve", [("ps", PB), k1], xall, lambda e, j=j: e.tensor_tensor(out=xp[:, 2 + j * 512:2 + (j + 1) * 512], in0=self.bank(PB), in1=t1, op=ALU.mult)))
            x0 = xp[:, j * 512:j * 512 + 512]
            x1 = xp[:, j * 512 + 1:j * 512 + 513]
            x2 = xp[:, j * 512 + 2:j * 512 + 514]
            cl = self.cols
            o.append(op("dve", xall + [("col", cb), ("col", cb + 3)], [k2], lambda e, x0=x0: e.tensor_scalar(out=t2, in0=x0, scalar1=cl[:, cb:cb + 1], scalar2=cl[:, cb + 3:cb + 4], op0=ALU.mult, op1=ALU.add)))
            o.append(op("dve", xall + [k2, ("col", cb + 1)], [k2], lambda e, x1=x1: e.scalar_tensor_tensor(out=t2, in0=x1, scalar=cl[:, cb + 1:cb + 2], in1=t2, op0=ALU.mult, op1=ALU.add)))
            o.append(op("dve", xall + [k2, ("col", cb + 2)], [k2], lambda e, x2=x2: e.scalar_tensor_tensor(out=t2, in0=x2, scalar=cl[:, cb + 2:cb + 3], in1=t2, op0=ALU.mult, op1=ALU.add)))
            o += self.proj_fm(s, 2, j)
            o.append(op("dve", [("ps", PJ), k2], [k2], lambda e: e.tensor_tensor(out=t2, in0=self.bank(PJ), in1=t2, op=ALU.mult)))
            o += self.proj_fm(s, 3, j, pbank=PB)
            o += self.gate_ops(PB, j % 2)
            zk = self.ak("ZS", j % 2)
            Z = self.ZS[j % 2]
            o.append(op("dve", [k2, zk], [yk], lambda e, Y=Y, Z=Z: e.scalar_tensor_tensor(out=Y, in0=t2, scalar=0.5, in1=Z, op0=ALU.mult, op1=ALU.mult)))
        return o

    def fox_prologue(self):
        op = self.op
        T = self.T
        o = []
        WF = self.WF
        wfk = self.ak("WF")
        o.append(op("pool", [], [wfk], lambda e: e.memset(WF, 0.0)))
        for r0 in (0, 32, 64):
            o.append(op("pool", [wfk], [wfk], lambda e, r0=r0: e.dma_start(out=WF[:, :, r0:r0 + 12], in_=T["o_w_in"][:, 6144:6156].rearrange("(c p) j -> p c j", p=128)), dma=True))
            o.append(op("sp", [], [("col", 22)], lambda e, r0=r0: e.dma_start(out=self.cols[r0:r0 + 12, 22:23], in_=T["o_c_forget_b"].rearrange("(p o) -> p o", o=1)), dma=True))
        o.append(op("dve", [("col", 22)], [("col", 23)], lambda e: e.tensor_scalar(out=self.cols[:, 23:24], in0=self.cols[:, 22:23], scalar1=-1.0, scalar2=None, op0=ALU.mult)))
        f0, f1, f2 = self.fx
        kf = [self.ak("fx", i) for i in range(3)]
        hb, mb = self.fxb
        kh, km = self.ak("fxb", 0), self.ak("fxb", 1)
        nck = self.ak("WMh")
        ok1 = self.ak("onesf")
        o.append(op("pool", [], [ok1], lambda e: e.memset(self.onesf, 1.0)))
        ncum = self.ncum
        R = 76
        for b in range(NB):
            pf = self.bank(PJ, R)
            for c in range(8):
                o.append(op("pe", [wfk] + [("UT", 4 * b + i) for i in range(4)], [("ps", PJ)],
                            lambda e, c=c, b=b: e.matmul(pf, lhsT=WF[:, c, 0:R], rhs=self.UT[:, c, b * 512:(b + 1) * 512], start=(c == 0), stop=(c == 7))))
            o.append(op("act", [("ps", PJ), ("col", 23)], [kf[0]], lambda e: e.activation(out=f0[0:R], in_=pf, func=AF.Exp, bias=self.cols[0:R, 23:24], scale=-1.0)))
            o.append(op("act", [kf[0]], [kf[0]], lambda e: e.activation(out=f0[0:R], in_=f0[0:R], func=AF.Ln, bias=1.0)))
            init = 0.0 if b == 0 else ncum[0:R, b * 512 - 1:b * 512]
            o.append(op("dve", [kf[0], ok1, nck], [nck], lambda e, b=b, init=init: e.tensor_tensor_scan(out=ncum[0:R, b * 512:(b + 1) * 512], data0=self.onesf[0:R, :], data1=f0[0:R], initial=init, op0=ALU.mult, op1=ALU.add)))
            nb = ncum[0:R, b * 512:(b + 1) * 512]
            o.append(op("dve", [nck], [kf[1]], lambda e, nb=nb: e.tensor_scalar(out=f1[0:R], in0=nb, scalar1=-1.0, scalar2=None, op0=ALU.mult)))
            o.append(op("dve", [kf[1]], [kh], lambda e: e.tensor_copy(out=hb[0:R], in_=f1[0:R])))
            o.append(op("dve", [kf[1], kh], [kf[2]], lambda e: e.tensor_tensor(out=f2[0:R], in0=f1[0:R], in1=hb[0:R], op=ALU.subtract)))
            o.append(op("dve", [kf[2]], [km], lambda e: e.tensor_copy(out=mb[0:R], in_=f2[0:R])))
            o.append(op("dve", [kf[2], km], [kf[2]], lambda e: e.tensor_tensor(out=f2[0:R], in0=f2[0:R], in1=mb[0:R], op=ALU.subtract)))
            cs = self.Csplit
            o.append(op("dve", [kh], [("Csplit",)], lambda e, b=b: e.tensor_copy(out=cs[0:32, b * 512:(b + 1) * 512], in_=hb[0:32])))
            o.append(op("dve", [km], [("Csplit",)], lambda e, b=b: e.tensor_copy(out=cs[32:64, b * 512:(b + 1) * 512], in_=mb[32:64])))
            o.append(op("dve", [kf[2]], [("Csplit",)], lambda e, b=b: e.tensor_copy(out=cs[64:R, b * 512:(b + 1) * 512], in_=f2[64:R])))
        pt = self.ps[:, A0, 0:192]
        for t in range(NT):
            o.append(op("pe", [nck, "identf"], [("ps", A0)], lambda e, t=t: e.transpose(out=self.ps[:, A0, t * 12:(t + 1) * 12], in_=ncum[0:12, t * 128:(t + 1) * 128], identity=self.identf[0:12, 0:12])))
        o.append(op("dve", [("ps", A0)], [("kb",)], lambda e: e.tensor_copy(out=self.kb[:, :], in_=pt)))
        return o

    def main_phase(self, L):
        op = self.op
        T = self.T
        o = []
        wname = "e_w_in" if L == 0 else "o_w_in"
        if L == 0:
            chunks = [("attn", h) for h in range(8)] + [("conv", c) for c in range(4)] + [("xattn", x) for x in range(4)]
        else:
            chunks = [("attn", h) for h in range(12)] + [("xattn", x) for x in range(4)]

        def wloads(ci):
            kind, idx = chunks[ci]
            s = ci % 2
            if kind == "attn":
                if L == 0:
                    offs = [idx * 128, 1024 + idx * 128, 2048 + idx * 128, 3072 + idx * 128]
                else:
                    offs = [idx * 128, 1536 + idx * 128, 3072 + idx * 128, 4608 + idx * 128]
                return [self.load_w(wname, offs[k], s, k) for k in range(4)]
            if kind == "conv":
                offs = [4096 + idx * 128, 4608 + idx * 128, 5120 + idx * 128, 5632 + idx * 128]
                return [self.load_w(wname, offs[k], s, k) for k in range(4)]
            if L == 0:
                offs = {0: 6144 + idx * 128, 3: 6656 + idx * 128}
            else:
                offs = {0: 6156 + idx * 128, 3: 6668 + idx * 128}
            return [self.load_w(wname, offs[k], s, k) for k in (0, 3)]

        if L == 0:
            for j in range(NB):
                o.append(op("pool", [], [self.ak("Qaug"), self.ak("Q", j)], lambda e, j=j: e.memset(self.QB[j][0:64, :], 0.0)))
                o.append(op("sp", [], [self.ak("Qaug")], lambda e, j=j: e.dma_start(out=self.QA[j][64:68, :], in_=T["c_qa"][:, j * 512:(j + 1) * 512]), dma=True))
                o.append(op("sp", [self.ak("Qaug")], [self.ak("Qaug")], lambda e, j=j: e.dma_start(out=self.QB[j][0:4, :], in_=T["c_qa"][:, j * 512:(j + 1) * 512]), dma=True))
        o += wloads(0)
        for ci, (kind, idx) in enumerate(chunks):
            s = ci % 2
            slot = ci % 4
            if ci + 1 < len(chunks):
                o += wloads(ci + 1)
            if kind == "conv":
                o += self.conv_chunk(idx, slot, s)
            else:
                if kind == "attn":
                    kv, blocks = self.attn_head(L, idx, slot, s)
                else:
                    kv, blocks = self.xattn_head(L, idx, slot, s)
                for g in kv:
                    o += g
                for (qz, ugroups, post) in blocks:
                    for g in qz:
                        o += g
                    for g in ugroups:
                        o += g
                    o += post
            if slot == 3:
                o += self.outproj(L, ci // 4)
        return o

    def store_out(self):
        o = []
        T = self.T
        for t in range(NT):
            o.append(self.op("sp", self.hk(t), [("out", t)], lambda e, t=t: e.dma_start(out=T["out"][t * 128:(t + 1) * 128, :], in_=self.h[:, t, :]), dma=True))
        o.append(Op("sp", None, reads=[("out", t) for t in range(NT)]))
        return o


_CACHE = {}


def _get_prog(layers):
    key = tuple(layers)
    if key not in _CACHE:
        g = Gen(layers)
        nc = g.build()
        _CACHE[key] = (nc, g.in_names)
    return _CACHE[key]


def _run(layers, xin, mem, params, consts):
    nc, in_names = _get_prog(layers)
    in_maps = []
    for b in range(8):
        m = {}
        for n in in_names:
            if n == "x":
                m[n] = np.ascontiguousarray(xin[b])
            elif n == "mem":
                m[n] = np.ascontiguousarray(mem[b])
            elif n in consts:
                m[n] = consts[n]
            else:
                m[n] = params[n]
        in_maps.append(m)
    res = run_bass_kernel_spmd(nc, in_maps, core_ids=list(range(8)))
    return np.stack([np.asarray(r["out"]) for r in res.results], 0)


FUSED = True


def kernel(**inputs):
    x = np.asarray(inputs["x"], np.float32)
    mem = np.asarray(inputs["mem"], np.float32)
    params = {}
    for n in L0_NAMES + L1_NAMES:
        a = np.asarray(inputs[n], np.float32)
        params[n] = np.ascontiguousarray(a[0])
    consts = _consts()
    if FUSED:
        return _run((0, 1), x, mem, params, consts).astype(np.float32)
    h1 = _run((0,), x, mem, params, consts)
    return _run((1,), h1, mem, params, consts).astype(np.float32)
```

```python
import contextlib
import math
import numpy as np
import ml_dtypes
import concourse.bass as bass
import concourse.mybir as mybir
from concourse.bass_utils import run_bass_kernel_spmd

F32 = mybir.dt.float32
BF16 = mybir.dt.bfloat16
AF = mybir.ActivationFunctionType
ALU = mybir.AluOpType
AX = mybir.AxisListType

ENGS = ("pe", "act", "dve", "pool", "sp")
EPS = 1e-6
S_LEN = 2048
D = 1024
NT = 16
NB = 4


class Op:
    __slots__ = ("eng", "fn", "reads", "writes", "dma", "idx", "sig", "deps", "dsem", "dtarget", "dprev")

    def __init__(self, eng, fn, reads=(), writes=(), dma=False):
        self.eng = eng
        self.fn = fn
        self.reads = tuple(reads)
        self.writes = tuple(writes)
        self.dma = dma
        self.sig = None
        self.deps = ()
        self.dsem = None


class Sched:
    NDMA_SEMS = 8

    def __init__(self, same_engine_sync=True):
        self.ops = []
        self.same_engine_sync = same_engine_sync

    def add(self, ops):
        if isinstance(ops, Op):
            self.ops.append(ops)
        else:
            for o in ops:
                self.add(o)

    def plan(self):
        last_w = {}
        readers = {}
        ops = self.ops
        for i, op in enumerate(ops):
            op.idx = i
            deps = set()
            for r in op.reads:
                w = last_w.get(r)
                if w is not None:
                    deps.add(w)
            for wr in op.writes:
                w = last_w.get(wr)
                if w is not None:
                    deps.add(w)
                rs = readers.get(wr)
                if rs:
                    deps.update(rs)
            deps.discard(i)
            for r in op.reads:
                readers.setdefault(r, []).append(i)
            for wr in op.writes:
                last_w[wr] = i
                readers[wr] = []
            latest = {}
            dmadeps = []
            for dix in deps:
                d = ops[dix]
                if d.dma:
                    dmadeps.append(dix)
                else:
                    if d.eng == op.eng and not op.dma:
                        if op.eng == "pe" or not self.same_engine_sync:
                            continue
                    if d.eng not in latest or latest[d.eng] < dix:
                        latest[d.eng] = dix
            op.deps = tuple(sorted(list(latest.values()) + dmadeps))
        need = set()
        for op in ops:
            for dix in op.deps:
                if not ops[dix].dma:
                    need.add(dix)
        cnt = {e: 0 for e in ENGS}
        for op in ops:
            if op.dma:
                continue
            if op.idx in need:
                cnt[op.eng] += 1
                op.sig = cnt[op.eng]
        dcount = {e: 0 for e in ENGS}
        for op in ops:
            if op.dma:
                n = dcount[op.eng]
                dcount[op.eng] += 1
                op.dsem = (op.eng, n % self.NDMA_SEMS)
                op.dtarget = 16 * (n // self.NDMA_SEMS + 1)
                op.dprev = 16 * (n // self.NDMA_SEMS)
        self.sigcount = cnt
        self.dmacount = dcount

    def emit(self, block, esems, dsems):
        handles = {"pe": block.tensor, "act": block.scalar, "dve": block.vector, "pool": block.gpsimd,
                   "sp": block.sync}
        ops = self.ops
        for eng in ENGS:
            mine = [op for op in ops if op.eng == eng]

            def body(e, mine=mine, eng=eng):
                waited = {}

                def wait(key, sem, val):
                    if waited.get(key, 0) >= val:
                        return
                    waited[key] = val
                    e.wait_ge(sem, val)

                for op in mine:
                    for dix in op.deps:
                        d = ops[dix]
                        if d.dma:
                            wait(d.dsem, dsems[d.dsem], d.dtarget)
                        else:
                            wait(d.eng, esems[d.eng], d.sig)
                    if op.dma and op.dprev > 0:
                        wait(op.dsem, dsems[op.dsem], op.dprev)
                    if op.fn is None:
                        continue
                    ins = op.fn(e)
                    if op.dma:
                        ins.then_inc(dsems[op.dsem], 16)
                    elif op.sig is not None:
                        ins.then_inc(esems[eng], 1)

            handles[eng](body)


def interleave(units, fillers):
    out = []
    nu, nf = len(units), len(fillers)
    if nu == 0:
        for f in fillers:
            out.extend(f)
        return out
    fi = 0
    for u in range(nu):
        out.extend(units[u])
        tgt = ((u + 1) * nf) // nu
        while fi < tgt:
            out.extend(fillers[fi])
            fi += 1
    return out


def _consts():
    bf = ml_dtypes.bfloat16
    tri = (np.arange(128)[None, :] >= np.arange(128)[:, None]).astype(np.float32)
    bd = np.zeros((128, 128), np.float32)
    bd[:64, :64] = 1.0
    bd[64:, 64:] = 1.0
    ident = np.eye(128, dtype=np.float32)
    ones = np.ones((128, 128), np.float32)
    cst = np.concatenate([tri, bd, ident, ones], axis=1).astype(bf)
    pos = np.arange(S_LEN)
    qa = np.stack([(pos % 128).astype(np.float32), (pos // 128).astype(np.float32),
                   np.ones(S_LEN, np.float32), np.ones(S_LEN, np.float32)], 0)
    ka = np.zeros((8, 4, S_LEN), np.float32)
    for h in range(8):
        sl = 2.0 ** (-(h + 1))
        ka[h, 0] = -sl
        ka[h, 1] = -sl * 128.0
        ka[h, 2] = sl * (pos % 128)
        ka[h, 3] = sl * 128.0 * (pos // 128)
    assert np.array_equal(ka.astype(bf).astype(np.float32), ka)
    assert np.array_equal(qa.astype(bf).astype(np.float32), qa)
    sel = np.zeros((76, 12 * 128), np.float32)
    for h in range(12):
        for r in (h, 32 + h, 64 + h):
            sel[r, h * 128:(h + 1) * 128] = 1.0
    identf = np.eye(128, dtype=np.float32)
    return {"c_cst": cst, "c_qa": qa.astype(bf), "c_ka": ka.astype(bf), "c_sel": sel.astype(bf),
            "c_identf": identf}


L0_NAMES = ["e_norm_g", "e_w_in", "e_w_out", "e_a_q_norm_g", "e_a_k_norm_g", "e_a_lambda", "e_a_out_norm_g",
            "e_b_conv_w", "e_b_conv_b", "e_x_q_norm_g", "e_x_k_norm_g", "e_mem_norm_g", "e_w_mem_kv"]
L1_NAMES = ["o_norm_g", "o_w_in", "o_w_out", "o_c_q_norm_g", "o_c_k_norm_g", "o_c_forget_b", "o_x_q_norm_g",
            "o_x_k_norm_g", "o_mem_norm_g", "o_w_mem_kv"]
SHAPES = {
    "e_norm_g": [1024], "e_w_in": [1024, 7168], "e_w_out": [2048, 1024], "e_a_q_norm_g": [64],
    "e_a_k_norm_g": [64], "e_a_lambda": [4, 64], "e_a_out_norm_g": [128], "e_b_conv_w": [3, 512],
    "e_b_conv_b": [512], "e_x_q_norm_g": [128], "e_x_k_norm_g": [128], "e_mem_norm_g": [1024],
    "e_w_mem_kv": [1024, 1024],
    "o_norm_g": [1024], "o_w_in": [1024, 7180], "o_w_out": [2048, 1024], "o_c_q_norm_g": [128],
    "o_c_k_norm_g": [128], "o_c_forget_b": [12], "o_x_q_norm_g": [128], "o_x_k_norm_g": [128],
    "o_mem_norm_g": [1024], "o_w_mem_kv": [1024, 1024],
}

PJ, PB, S0, S1, A0, A1, A2, A3 = range(8)


class Gen:
    def __init__(self, layers, interleave_on=True):
        self.layers = tuple(layers)
        self.il = interleave_on
        self.nc = bass.Bass("TRN2", target_bir_lowering=False)
        self.S = Sched()
        self.stack = contextlib.ExitStack()
        self.sb_bytes = 0
        self.arena_keys = set()
        self.wctr = 0
        self.pctr = 0
        self.sctr = 0

    def sb(self, name, shape, dt):
        n = 1
        for s in shape[1:]:
            n *= s
        self.sb_bytes += n * (4 if dt == F32 else 2)
        return self.stack.enter_context(self.nc.sbuf_tensor(name, shape, dt))

    def carve(self, nbytes):
        off = self.ar_off
        self.ar_off += nbytes
        assert self.ar_off <= self.AR_BYTES, (self.ar_off, self.AR_BYTES)
        return off

    def arv(self, off, shape, dt):
        n = 1
        for s in shape:
            n *= s
        nb = n * (4 if dt == F32 else 2)
        v = self.AR[:, off // 2:(off + nb) // 2]
        if dt == F32:
            v = v.bitcast(F32)
        if len(shape) == 2:
            v = v.rearrange("p (a b) -> p a b", a=shape[0])
        elif len(shape) == 3:
            v = v.rearrange("p (a b c) -> p a b c", a=shape[0], b=shape[1])
        return v

    def ak(self, *key):
        self.arena_keys.add(key)
        return key

    def op(self, eng, reads, writes, fn, dma=False):
        return Op(eng, fn, reads, writes, dma)

    def fence(self):
        keys = sorted(self.arena_keys, key=repr)
        d = self.dummy
        return [Op("pool", lambda e: e.memset(d[:], 0.0), reads=(), writes=keys)]

    def build(self):
        nc = self.nc
        T = {}
        T["x"] = nc.dram_tensor("x", [S_LEN, D], F32, kind="ExternalInput").ap()
        T["mem"] = nc.dram_tensor("mem", [256, D], F32, kind="ExternalInput").ap()
        names = []
        if 0 in self.layers:
            names += L0_NAMES
        if 1 in self.layers:
            names += L1_NAMES
        for n in names:
            T[n] = nc.dram_tensor(n, SHAPES[n], F32, kind="ExternalInput").ap()
        T["c_cst"] = nc.dram_tensor("c_cst", [128, 512], BF16, kind="ExternalInput").ap()
        T["c_qa"] = nc.dram_tensor("c_qa", [4, S_LEN], BF16, kind="ExternalInput").ap()
        T["c_ka"] = nc.dram_tensor("c_ka", [8, 4, S_LEN], BF16, kind="ExternalInput").ap()
        T["c_sel"] = nc.dram_tensor("c_sel", [76, 1536], BF16, kind="ExternalInput").ap()
        T["c_identf"] = nc.dram_tensor("c_identf", [128, 128], F32, kind="ExternalInput").ap()
        T["out"] = nc.dram_tensor("out", [S_LEN, D], F32, kind="ExternalOutput").ap()
        self.T = T
        self.in_names = ["x", "mem"] + names + ["c_cst", "c_qa", "c_ka", "c_sel", "c_identf"]

        st = self.stack
        sb = self.sb
        self.h = sb("h", [128, NT, D], F32)
        self.UT = sb("UT", [128, 8, S_LEN], BF16)
        self.KA = [sb(f"KA{s}", [128, S_LEN], BF16) for s in range(2)]
        self.KB = [sb(f"KB{s}", [128, S_LEN], BF16) for s in range(2)]
        self.V = [sb(f"V{s}", [128, NT, 128], BF16) for s in range(2)]
        self.W = [[sb(f"W{s}_{k}", [128, 8, 128], BF16) for k in range(4)] for s in range(2)]
        self.WO = sb("WO", [128, 4, D], BF16)
        self.cols = sb("cols", [128, 64], F32)
        self.cst = sb("cst", [128, 512], BF16)
        self.MKT = sb("MKT", [128, 4, 256], BF16)
        self.MV = sb("MV", [128, 2, 512], BF16)
        self.Csplit = sb("Csplit", [128, S_LEN], BF16)
        self.kb = sb("kb", [128, 192], F32)
        self.sel = sb("sel", [128, 1536], BF16)
        self.identf = sb("identf", [128, 128], F32)
        self.dummy = sb("dmy_t", [128, 2], F32)
        self.small = sb("small", [128, 64], F32)
        self.AR_BYTES = 45056
        self.AR = sb("AR", [128, self.AR_BYTES // 2], BF16)
        assert self.sb_bytes <= 210000, self.sb_bytes
        self.ar_off = 0
        c = self.carve
        self.YT = self.arv(c(16384), [4, S_LEN], BF16)
        self.QA = [self.arv(c(1024), [512], BF16) for _ in range(4)]
        self.QB = [self.arv(c(1024), [512], BF16) for _ in range(4)]
        self.ZS = [self.arv(c(2048), [512], F32) for _ in range(2)]
        self.t1 = self.arv(c(2048), [512], F32)
        self.t2 = self.arv(c(2048), [512], F32)
        xoff = self.ar_off
        self.P = [self.arv(c(1024), [512], BF16) for _ in range(4)]
        self.t3 = self.arv(c(2048), [512], F32)
        self.sq = [self.arv(c(1024), [512], BF16) for _ in range(2)]
        self.sse = self.arv(c(2048), [512], F32)
        self.rstd = self.arv(c(2048), [512], F32)
        assert self.ar_off <= self.AR_BYTES
        self.xpad = self.arv(xoff, [2064], F32)
        self.ar_off = 0
        self.gbc = self.arv(c(4096), [D], F32)
        self.Usc = [self.arv(c(2048), [D], BF16) for _ in range(2)]
        woff = self.ar_off
        self.WMh = self.arv(c(8192), [8, 512], BF16)
        self.ncum = self.arv(woff, [S_LEN], F32)
        self.memf = self.arv(c(4096), [D], F32)
        self.memUT = self.arv(c(4096), [8, 256], BF16)
        foff = self.ar_off
        self.fx = [self.arv(c(2048), [512], F32) for _ in range(3)]
        self.junk = self.arv(foff, [D], BF16)
        self.fxb = [self.arv(c(1024), [512], BF16) for _ in range(2)]
        self.WF = self.arv(c(2048), [8, 128], BF16)
        self.onesf = self.arv(c(2048), [512], F32)
        assert self.ar_off <= 38912, self.ar_off

        self.ps = st.enter_context(nc.psum_tensor("ps", [128, 8, 512], F32))
        self.esems = {e: st.enter_context(nc.semaphore(f"s_{e}")) for e in ENGS}
        self.dsems = {(e, k): st.enter_context(nc.semaphore(f"d_{e}{k}"))
                      for e in ("sp", "pool") for k in range(Sched.NDMA_SEMS)}

        S = self.S
        segs = [self.setup_ops()]
        for li, L in enumerate(self.layers):
            segs.append("fence")
            segs.append(self.prologue(L))
            segs.append("fence")
            segs.append(self.main_phase(L))
        segs.append(self.store_out())
        for sg in segs:
            S.add(self.fence() if isinstance(sg, str) else sg)
        S.plan()
        with nc.Block() as block:
            S.emit(block, self.esems, self.dsems)
        self.stack.close()
        return nc

    def bank(self, b, rows=128, c0=0, c1=512):
        return self.ps[0:rows, b, c0:c1]

    def col(self, j, rows=128, r0=0):
        return self.cols[r0:r0 + rows, j:j + 1]

    def hk(self, t):
        return [("h", t, 0), ("h", t, 1)]

    def setup_ops(self):
        T = self.T
        o = []
        op = self.op
        h = self.h
        for t in range(NT):
            o.append(op("sp", [], self.hk(t), lambda e, t=t: e.dma_start(out=h[:, t, :], in_=T["x"][t * 128:(t + 1) * 128, :]), dma=True))
        o.append(op("sp", [], ["cst"], lambda e: e.dma_start(out=self.cst[:], in_=T["c_cst"]), dma=True))
        o.append(op("sp", [], ["sel"], lambda e: e.dma_start(out=self.sel[0:76, :], in_=T["c_sel"]), dma=True))
        o.append(op("sp", [], ["identf"], lambda e: e.dma_start(out=self.identf[:], in_=T["c_identf"]), dma=True))
        o.append(op("pool", [], ["dummy"], lambda e: e.memset(self.dummy[:], 0.0)))
        o.append(op("pool", [], [("col", j) for j in range(64)], lambda e: e.memset(self.cols[:], 0.0)))
        for s in range(2):
            o.append(op("pool", [], [("K", s, b) for b in range(NB)] + [("Kaug", s)], lambda e, s=s: e.memset(self.KB[s][0:64, :], 0.0)))
        return o

    def consts(self):
        cst = self.cst
        return {"tri": cst[:, 0:128], "bd": cst[:, 128:256], "ident": cst[:, 256:384], "ones": cst[:, 384:512]}

    def small_rstd(self, src_key, src_ap, dst_key, dst_ap, n, scale, eps):
        o = []
        tmpk = ("small_tmp",)
        tmp = self.small[:, 32:32 + n]
        o.append(self.op("act", [src_key], [tmpk], lambda e: e.activation(out=tmp, in_=src_ap, func=AF.Ln, bias=float(eps), scale=float(scale))))
        o.append(self.op("act", [tmpk], [dst_key], lambda e: e.activation(out=dst_ap, in_=tmp, func=AF.Exp, scale=-0.5)))
        return o

    def norm_rows_to_T(self, src_key_list, src_ap, rstd_col_key, rstd_col, ui, dst_writes, dst_ap_fn, tb):
        o = []
        C = self.consts()
        uk = self.ak("Usc", ui)
        U = self.Usc[ui]
        o.append(self.op("dve", list(src_key_list) + [rstd_col_key, self.ak("gbc")], [uk],
                         lambda e: e.scalar_tensor_tensor(out=U, in0=src_ap, scalar=rstd_col, in1=self.gbc, op0=ALU.mult, op1=ALU.mult)))
        psT = self.ps[:, tb, :].bitcast(BF16)
        for c in range(8):
            o.append(self.op("pe", [uk, "cst"], [("ps", tb)],
                             lambda e, c=c: e.transpose(out=psT[:, c * 128:(c + 1) * 128], in_=U[:, c * 128:(c + 1) * 128], identity=C["ident"])))
        o.append(self.op("act", [("ps", tb)], dst_writes,
                         lambda e: e.activation(out=dst_ap_fn(), in_=psT.rearrange("p (c j) -> p c j", c=8), func=AF.Copy)))
        return o

    def prologue(self, L):
        T = self.T
        op = self.op
        pre = "e_" if L == 0 else "o_"
        o = []
        C = self.consts()
        h = self.h
        cols = self.cols
        def vec_col(name, j, n=128, r0=0, src_off=0):
            src = T[name]
            ap = src[src_off:src_off + n].rearrange("(p o) -> p o", o=1)
            return op("sp", [], [("col", j)], lambda e: e.dma_start(out=cols[r0:r0 + n, j:j + 1], in_=ap), dma=True)

        def scale_col(j, f):
            return op("dve", [("col", j)], [("col", j)], lambda e: e.tensor_scalar(out=cols[:, j:j + 1], in0=cols[:, j:j + 1], scalar1=float(f), scalar2=None, op0=ALU.mult))

        if L == 0:
            for r0 in (0, 64):
                o.append(vec_col("e_a_q_norm_g", 0, 64, r0))
                o.append(vec_col("e_a_k_norm_g", 1, 64, r0))
            o.append(scale_col(1, 8.0))
            o.append(vec_col("e_a_out_norm_g", 2))
            lam_init = 0.8 - 0.6 * math.exp(-0.3 * 0)
            o.append(scale_col(2, math.sqrt(128.0) * (1.0 - lam_init)))
            lpb = self.fx[0][:, 0:256]
            lpk = self.ak("fx", 0)
            o.append(op("sp", [], [lpk], lambda e: e.dma_start(out=lpb, in_=T["e_a_lambda"].rearrange("a b -> (a b)").rearrange("(o n) -> o n", o=1).partition_broadcast(128)), dma=True))
            prod = self.fx[1][:, 0:128]
            pk = self.ak("fx", 1)
            lp4 = lpb.rearrange("p (a b) -> p a b", a=4)
            for q in range(2):
                o.append(op("dve", [lpk], [pk], lambda e, q=q: e.tensor_tensor(out=prod[:, q * 64:(q + 1) * 64], in0=lp4[:, 2 * q, :], in1=lp4[:, 2 * q + 1, :], op=ALU.mult)))
            sm = self.small
            o.append(op("dve", [pk], [("sm", 0)], lambda e: e.reduce_sum(out=sm[:, 0:2], in_=prod.rearrange("p (a b) -> p a b", a=2), axis=AX.X)))
            o.append(op("act", [("sm", 0)], [("sm", 1)], lambda e: e.activation(out=sm[:, 2:4], in_=sm[:, 0:2], func=AF.Exp)))
            o.append(op("dve", [("sm", 1)], [("sm", 2)], lambda e: e.tensor_tensor(out=sm[:, 4:5], in0=sm[:, 2:3], in1=sm[:, 3:4], op=ALU.subtract)))
            o.append(op("dve", [("sm", 2)], [("col", 5)], lambda e: e.tensor_scalar(out=cols[:, 5:6], in0=sm[:, 4:5], scalar1=lam_init, scalar2=-1.0, op0=ALU.add, op1=ALU.mult)))
            for cc in range(4):
                for k in range(3):
                    o.append(op("sp", [], [("col", 6 + cc * 4 + k)], lambda e, cc=cc, k=k: e.dma_start(out=cols[:, 6 + cc * 4 + k:7 + cc * 4 + k], in_=T["e_b_conv_w"][k, cc * 128:(cc + 1) * 128].rearrange("(p o) -> p o", o=1)), dma=True))
                o.append(op("sp", [], [("col", 6 + cc * 4 + 3)], lambda e, cc=cc: e.dma_start(out=cols[:, 9 + cc * 4:10 + cc * 4], in_=T["e_b_conv_b"][cc * 128:(cc + 1) * 128].rearrange("(p o) -> p o", o=1)), dma=True))
        else:
            o.append(vec_col("o_c_q_norm_g", 0))
            o.append(vec_col("o_c_k_norm_g", 1))
            o.append(scale_col(1, math.sqrt(128.0)))
        o.append(vec_col(pre + "x_q_norm_g", 3))
        o.append(vec_col(pre + "x_k_norm_g", 4))
        o.append(scale_col(4, math.sqrt(128.0)))

        gk = self.ak("gbc")
        o.append(op("sp", [], [gk], lambda e: e.dma_start(out=self.gbc, in_=T[pre + "mem_norm_g"].rearrange("(o n) -> o n", o=1).partition_broadcast(128)), dma=True))
        mfk = self.ak("memf")
        mutk = self.ak("memUT")
        sm = self.small
        for mt in range(2):
            o.append(op("sp", [], [mfk], lambda e, mt=mt: e.dma_start(out=self.memf, in_=T["mem"][mt * 128:(mt + 1) * 128, :]), dma=True))
            jk = self.ak("fx", 0)
            o.append(op("act", [mfk], [jk, ("sm", 10)], lambda e: e.activation(out=self.junk, in_=self.memf, func=AF.Square, accum_out=sm[:, 10:11])))
            o += self.small_rstd(("sm", 10), sm[:, 10:11], ("sm", 11), sm[:, 11:12], 1, 1.0 / D, EPS)
            tb = A0 + mt
            o += self.norm_rows_to_T([mfk], self.memf, ("sm", 11), sm[:, 11:12], mt, [mutk],
                                     lambda mt=mt: self.memUT[:, :, mt * 128:(mt + 1) * 128], tb)
        wmk = self.ak("WMh")
        wsrc = T[pre + "w_mem_kv"]
        o.append(op("pool", [], [wmk], lambda e: e.dma_start(out=self.WMh, in_=wsrc[:, 0:512].rearrange("(c p) j -> p c j", p=128)), dma=True))
        for hx in range(4):
            pj = self.bank(PJ, 128, 0, 256)
            for c in range(8):
                o.append(op("pe", [wmk, mutk], [("ps", PJ)], lambda e, c=c, hx=hx: e.matmul(pj, lhsT=self.WMh[:, c, hx * 128:(hx + 1) * 128], rhs=self.memUT[:, c, :], start=(c == 0), stop=(c == 7))))
            o += self.norm_chain(PJ, 256, C["ones"], 128, 4, [("MKT", hx)],
                                 [(0, 128, self.MKT[:, hx, :])], 0)
        o.append(op("pool", [], [wmk], lambda e: e.dma_start(out=self.WMh, in_=wsrc[:, 512:1024].rearrange("(c p) j -> p c j", p=128)), dma=True))
        for mt in range(2):
            pv = self.bank(PJ)
            for c in range(8):
                o.append(op("pe", [wmk, mutk], [("ps", PJ)], lambda e, c=c, mt=mt: e.matmul(pv, lhsT=self.memUT[:, c, mt * 128:(mt + 1) * 128], rhs=self.WMh[:, c, :], start=(c == 0), stop=(c == 7))))
            o.append(op("act", [("ps", PJ)], [("MV", mt)], lambda e, mt=mt: e.activation(out=self.MV[:, mt, :], in_=pv, func=AF.Copy)))

        o.append(op("sp", [], [gk], lambda e: e.dma_start(out=self.gbc, in_=T[pre + "norm_g"].rearrange("(o n) -> o n", o=1).partition_broadcast(128)), dma=True))
        jk = self.ak("fx", 0)
        for t in range(NT):
            o.append(op("act", self.hk(t), [jk, ("hss", t)], lambda e, t=t: e.activation(out=self.junk, in_=h[:, t, :], func=AF.Square, accum_out=sm[:, 12 + t:13 + t])))
        o.append(op("act", [("hss", t) for t in range(NT)], [("small_tmp",)], lambda e: e.activation(out=sm[:, 32:48], in_=sm[:, 12:28], func=AF.Ln, bias=float(EPS), scale=1.0 / D)))
        o.append(op("act", [("small_tmp",)], [("hrstd",)], lambda e: e.activation(out=sm[:, 48:64], in_=sm[:, 32:48], func=AF.Exp, scale=-0.5)))
        for t in range(NT):
            o += self.norm_rows_to_T(self.hk(t), h[:, t, :], ("hrstd",), sm[:, 48 + t:49 + t], t % 2, [("UT", t)],
                                     lambda t=t: self.UT[:, :, t * 128:(t + 1) * 128], A0 + (t % 4))
        if L == 1:
            o += self.fox_prologue()
        return o

    def norm_chain(self, pbank, n, ones_ap, dsz, gcol, dst_writes, dsts, si):
        op = self.op
        o = []
        pj = self.bank(pbank, 128, 0, n)
        sq = self.sq[si][:, 0:n]
        sse = self.sse[:, 0:n]
        rstd = self.rstd[:, 0:n]
        sqk, ssek, rk = self.ak("sq", si), self.ak("sse"), self.ak("rstd")
        pb = self.bank(PB, 128, 0, n)
        o.append(op("act", [("ps", pbank)], [sqk], lambda e: e.activation(out=sq, in_=pj, func=AF.Square)))
        o.append(op("pe", [sqk, "cst"], [("ps", PB)], lambda e: e.matmul(pb, lhsT=ones_ap, rhs=sq, start=True, stop=True)))
        o.append(op("act", [("ps", PB)], [ssek], lambda e: e.activation(out=sse, in_=pb, func=AF.Ln, bias=float(dsz * EPS))))
        o.append(op("act", [ssek], [rk], lambda e: e.activation(out=rstd, in_=sse, func=AF.Exp, scale=-0.5)))
        for (r0, nr, dst) in dsts:
            o.append(op("dve", [("ps", pbank), rk, ("col", gcol)], dst_writes,
                        lambda e, r0=r0, nr=nr, dst=dst: e.scalar_tensor_tensor(out=dst, in0=self.ps[r0:r0 + nr, pbank, 0:n], scalar=self.cols[r0:r0 + nr, gcol:gcol + 1], in1=self.rstd[r0:r0 + nr, 0:n], op0=ALU.mult, op1=ALU.mult)))
        return o

    def load_w(self, wname, col0, s, k, ncols=128):
        src = self.T[wname][:, col0:col0 + ncols].rearrange("(c p) j -> p c j", p=128)
        dst = self.W[s][k]
        return self.op("pool", [], [("W", s, k)], lambda e: e.dma_start(out=dst[:, :, 0:ncols], in_=src), dma=True)

    def proj_fm(self, s, k, b, pbank=PJ, n=512):
        o = []
        W = self.W[s][k]
        out = self.bank(pbank)
        for c in range(8):
            o.append(self.op("pe", [("W", s, k)] + [("UT", 4 * b + i) for i in range(4)], [("ps", pbank)],
                             lambda e, c=c: e.matmul(out, lhsT=W[:, c, :], rhs=self.UT[:, c, b * 512:(b + 1) * 512], start=(c == 0), stop=(c == 7))))
        return o

    def gate_ops(self, pbank, zi):
        o = []
        zk = self.ak("ZS", zi)
        Z = self.ZS[zi]
        pj = self.bank(pbank)
        o.append(self.op("act", [("ps", pbank)], [zk], lambda e: e.activation(out=Z, in_=pj, func=AF.Exp, scale=-1.0)))
        o.append(self.op("act", [zk], [zk], lambda e: e.activation(out=Z, in_=Z, func=AF.Ln, bias=1.0)))
        o.append(self.op("act", [zk], [zk], lambda e: e.activation(out=Z, in_=Z, func=AF.Exp, scale=-1.0)))
        o.append(self.op("dve", [("ps", pbank), zk], [zk], lambda e: e.tensor_tensor(out=Z, in0=pj, in1=Z, op=ALU.mult)))
        return o

    def attn_block(self, units, post):
        LA = 2
        groups = []
        n = len(units)
        for u in range(n + LA):
            g = []
            if u < n:
                g += units[u]["qk"] + units[u]["exp"]
            if u - LA >= 0:
                g += units[u - LA]["pv"]
            groups.append(g)
        return groups, post

    def next_p(self):
        i = self.pctr % 4
        self.pctr += 1
        return i

    def next_s(self):
        i = self.sctr % 2
        self.sctr += 1
        return S0 + i

    def make_unit(self, kT, k_reads, qT, q_reads, n0, extra_mm, bias, bias_reads, diag, v_ap, v_reads, obank, lbank, first, last):
        op = self.op
        C = self.consts()
        sb = self.next_s()
        pi = self.next_p()
        P = self.P[pi]
        pk = self.ak("P", pi)
        sc = self.bank(sb, 128, n0, 512)
        qk = []
        qk.append(op("pe", k_reads + q_reads, [("ps", sb)], lambda e: e.matmul(sc, lhsT=kT, rhs=qT, start=True, stop=(extra_mm is None))))
        if extra_mm is not None:
            l2, r2, rd2 = extra_mm
            qk.append(op("pe", rd2, [("ps", sb)], lambda e: e.matmul(sc, lhsT=l2, rhs=r2, start=False, stop=True)))
        ex = []
        if bias is None:
            ex.append(op("act", [("ps", sb)], [pk], lambda e: e.activation(out=P[:, n0:512], in_=sc, func=AF.Exp)))
        else:
            ex.append(op("act", [("ps", sb)] + bias_reads, [pk], lambda e: e.activation(out=P[:, n0:512], in_=sc, func=AF.Exp, bias=bias)))
        if diag:
            ex.append(op("pool", [pk, "cst"], [pk], lambda e: e.tensor_tensor(out=P[:, n0:n0 + 128], in0=P[:, n0:n0 + 128], in1=C["tri"], op=ALU.mult)))
        pv = []
        ob = self.bank(obank, 128, n0, 512)
        lb = self.bank(lbank, 128, n0, 512)
        pv.append(op("pe", [pk] + v_reads, [("ps", obank)], lambda e: e.matmul(ob, lhsT=v_ap, rhs=P[:, n0:512], start=first, stop=last)))
        pv.append(op("pe", [pk, "cst"], [("ps", lbank)], lambda e: e.matmul(lb, lhsT=C["ones"], rhs=P[:, n0:512], start=first, stop=last)))
        return {"qk": qk, "exp": ex, "pv": pv}

    def outproj(self, L, g):
        op = self.op
        o = []
        wname = "e_w_out" if L == 0 else "o_w_out"
        src = self.T[wname][g * 512:(g + 1) * 512, :].rearrange("(c p) j -> p c j", p=128)
        o.append(op("pool", [], [("WO",)], lambda e: e.dma_start(out=self.WO[:], in_=src), dma=True))
        banks = [A0, A1, A2, A3]
        i = 0
        for t in range(NT):
            for n in range(2):
                bnk = banks[i % 4]
                i += 1
                out = self.bank(bnk)
                for c in range(4):
                    o.append(op("pe", [("WO",), self.ak("YT", c, t // 4)], [("ps", bnk)],
                                lambda e, c=c, t=t, n=n, out=out: e.matmul(out, lhsT=self.YT[:, c, t * 128:(t + 1) * 128], rhs=self.WO[:, c, n * 512:(n + 1) * 512], start=(c == 0), stop=(c == 3))))
                hv = self.h[:, t, n * 512:(n + 1) * 512]
                o.append(op("dve", [("ps", bnk), ("h", t, n)], [("h", t, n)], lambda e, hv=hv, out=out: e.tensor_tensor(out=hv, in0=out, in1=hv, op=ALU.add)))
        return o

    def prep_kv(self, L, s, ones_ap, dsz, split):
        op = self.op
        groups = []
        for b in range(NB):
            g1 = self.proj_fm(s, 1, b)
            if split:
                dsts = [(0, 64, self.KA[s][0:64, b * 512:(b + 1) * 512]), (64, 64, self.KB[s][64:128, b * 512:(b + 1) * 512])]
            else:
                dsts = [(0, 128, self.KA[s][:, b * 512:(b + 1) * 512])]
            g2 = self.norm_chain(PJ, 512, ones_ap, dsz, 1, [("K", s, b)], dsts, b % 2)
            groups.append(g1 + g2[:1])
            groups.append(g2[1:])
        Wv = self.W[s][2]
        for tg in range(4):
            g = []
            pv = self.bank(PJ)
            for ti in range(4):
                t = tg * 4 + ti
                for c in range(8):
                    g.append(op("pe", [("W", s, 2), ("UT", t)], [("ps", PJ)],
                                lambda e, c=c, t=t, ti=ti: e.matmul(self.ps[:, PJ, ti * 128:(ti + 1) * 128], lhsT=self.UT[:, c, t * 128:(t + 1) * 128], rhs=Wv[:, c, :], start=(c == 0), stop=(c == 7))))
            g.append(op("act", [("ps", PJ)], [("V", s, tg * 4 + ti) for ti in range(4)],
                        lambda e, tg=tg: e.activation(out=self.V[s][:, tg * 4:(tg + 1) * 4, :], in_=pv.rearrange("p (a b) -> p a b", a=4), func=AF.Copy)))
            groups.append(g)
        return groups

    def prep_q(self, L, s, j, ones_ap, dsz, split, gcol=0):
        g1 = self.proj_fm(s, 0, j)
        if split:
            dsts = [(0, 64, self.QA[j][0:64, :]), (64, 64, self.QB[j][64:128, :])]
        else:
            dsts = [(0, 128, self.QA[j][:, :])]
        g2 = self.norm_chain(PJ, 512, ones_ap, dsz, gcol, [self.ak("Q", j)], dsts, j % 2)
        return [g1 + g2[:1], g2[1:]]

    def prep_z(self, s, k, j):
        return [self.proj_fm(s, k, j), self.gate_ops(PJ, j % 2)]

    def attn_head(self, L, hd, slot, s):
        op = self.op
        C = self.consts()
        split = (L == 0)
        ones_ap = C["bd"] if split else C["ones"]
        dsz = 64 if split else 128
        kv = self.prep_kv(L, s, ones_ap, dsz, split)
        if L == 0:
            kv = [[op("sp", [], [("Kaug", s)], lambda e: e.dma_start(out=self.KA[s][64:68, :], in_=self.T["c_ka"][hd]), dma=True),
                   op("sp", [], [("Kaug", s)], lambda e: e.dma_start(out=self.KB[s][0:4, :], in_=self.T["c_ka"][hd]), dma=True)]] + kv
        blocks = []
        for j in range(NB):
            qz = self.prep_q(L, s, j, ones_ap, dsz, split) + self.prep_z(s, 3, j)
            units = []
            ntile = 4 * j + 4
            subs = (0, 1) if L == 0 else (0,)
            for i in range(ntile):
                n0 = max(0, i - 4 * j) * 128
                diag = i >= 4 * j
                for sub in subs:
                    if L == 0:
                        if sub == 0:
                            kT = self.KA[s][0:68, i * 128:(i + 1) * 128]
                            qT = self.QA[j][0:68, n0:512]
                        else:
                            kT = self.KB[s][:, i * 128:(i + 1) * 128]
                            qT = self.QB[j][:, n0:512]
                        obank, lbank = (A0, A2) if sub == 0 else (A1, A3)
                        u = self.make_unit(kT, [("K", s, i // 4), ("Kaug", s)], qT, [self.ak("Q", j), self.ak("Qaug")], n0, None, None, [], diag,
                                           self.V[s][:, i, :], [("V", s, i)], obank, lbank, i == 0, i == ntile - 1)
                    else:
                        kT = self.KA[s][:, i * 128:(i + 1) * 128]
                        qT = self.QA[j][:, n0:512]
                        extra = (self.sel[0:76, hd * 128:(hd + 1) * 128], self.Csplit[0:76, j * 512 + n0:(j + 1) * 512], ["sel", ("Csplit",)])
                        bias = self.kb[:, i * 12 + hd:i * 12 + hd + 1]
                        u = self.make_unit(kT, [("K", s, i // 4)], qT, [self.ak("Q", j)], n0, extra, bias, [("kb",)], diag,
                                           self.V[s][:, i, :], [("V", s, i)], A0, A1, i == 0, i == ntile - 1)
                    units.append(u)
            ugroups, _ = self.attn_block(units, None)
            post = []
            yk = self.ak("YT", slot, j)
            Y = self.YT[:, slot, j * 512:(j + 1) * 512]
            zk = self.ak("ZS", j % 2)
            Z = self.ZS[j % 2]
            t1, t2, t3 = self.t1, self.t2, self.t3
            k1, k2, k3 = self.ak("t1"), self.ak("t2"), self.ak("t3")
            if L == 1:
                post.append(op("act", [("ps", A1)], [k1], lambda e: e.activation(out=t1, in_=self.bank(A1), func=AF.Ln)))
                post.append(op("act", [k1], [k1], lambda e: e.activation(out=t1, in_=t1, func=AF.Exp, scale=-1.0)))
                post.append(op("dve", [("ps", A0), k1], [k1], lambda e: e.tensor_tensor(out=t1, in0=self.bank(A0), in1=t1, op=ALU.mult)))
                post.append(op("dve", [k1, zk], [yk], lambda e, Y=Y, Z=Z: e.tensor_tensor(out=Y, in0=t1, in1=Z, op=ALU.mult)))
            else:
                post.append(op("act", [("ps", A2)], [k1], lambda e: e.activation(out=t1, in_=self.bank(A2), func=AF.Ln)))
                post.append(op("act", [("ps", A3)], [k2], lambda e: e.activation(out=t2, in_=self.bank(A3), func=AF.Ln)))
                post.append(op("act", [k1], [k1], lambda e: e.activation(out=t1, in_=t1, func=AF.Exp, scale=-1.0)))
                post.append(op("act", [k2], [k2], lambda e: e.activation(out=t2, in_=t2, func=AF.Exp, scale=-1.0)))
                post.append(op("dve", [("ps", A0), k1], [k1], lambda e: e.tensor_tensor(out=t1, in0=self.bank(A0), in1=t1, op=ALU.mult)))
                post.append(op("dve", [("ps", A1), k2], [k2], lambda e: e.tensor_tensor(out=t2, in0=self.bank(A1), in1=t2, op=ALU.mult)))
                post.append(op("dve", [k1, k2, ("col", 5)], [k1], lambda e: e.scalar_tensor_tensor(out=t1, in0=t2, scalar=self.cols[:, 5:6], in1=t1, op0=ALU.mult, op1=ALU.add)))
                sqk, ssek, rk = self.ak("sq", 0), self.ak("sse"), self.ak("rstd")
                sq, sse, rstd = self.sq[0], self.sse, self.rstd
                pb = self.bank(PB)
                post.append(op("act", [k1], [sqk], lambda e: e.activation(out=sq, in_=t1, func=AF.Square)))
                post.append(op("pe", [sqk, "cst"], [("ps", PB)], lambda e: e.matmul(pb, lhsT=C["ones"], rhs=sq, start=True, stop=True)))
                post.append(op("act", [("ps", PB)], [ssek], lambda e: e.activation(out=sse, in_=pb, func=AF.Ln, bias=float(128 * EPS))))
                post.append(op("act", [ssek], [rk], lambda e: e.activation(out=rstd, in_=sse, func=AF.Exp, scale=-0.5)))
                post.append(op("dve", [k1, rk, ("col", 2)], [k1], lambda e: e.scalar_tensor_tensor(out=t1, in0=t1, scalar=self.cols[:, 2:3], in1=rstd, op0=ALU.mult, op1=ALU.mult)))
                post.append(op("dve", [k1, zk], [yk], lambda e, Y=Y, Z=Z: e.tensor_tensor(out=Y, in0=t1, in1=Z, op=ALU.mult)))
            blocks.append((qz, ugroups, post))
        return kv, blocks

    def xattn_head(self, L, hx, slot, s):
        op = self.op
        C = self.consts()
        blocks = []
        for j in range(NB):
            qz = self.prep_q(L, s, j, C["ones"], 128, False, gcol=3) + self.prep_z(s, 3, j)
            units = []
            for mt in range(2):
                kT = self.MKT[:, hx, mt * 128:(mt + 1) * 128]
                qT = self.QA[j][:, :]
                u = self.make_unit(kT, [("MKT", hx)], qT, [self.ak("Q", j)], 0, None, None, [], False,
                                   self.MV[:, mt, hx * 128:(hx + 1) * 128], [("MV", mt)], A0, A1, mt == 0, mt == 1)
                units.append(u)
            ugroups, _ = self.attn_block(units, None)
            post = []
            yk = self.ak("YT", slot, j)
            Y = self.YT[:, slot, j * 512:(j + 1) * 512]
            zk = self.ak("ZS", j % 2)
            Z = self.ZS[j % 2]
            t1 = self.t1
            k1 = self.ak("t1")
            post.append(op("act", [("ps", A1)], [k1], lambda e: e.activation(out=t1, in_=self.bank(A1), func=AF.Ln)))
            post.append(op("act", [k1], [k1], lambda e: e.activation(out=t1, in_=t1, func=AF.Exp, scale=-1.0)))
            post.append(op("dve", [("ps", A0), k1], [k1], lambda e: e.tensor_tensor(out=t1, in0=self.bank(A0), in1=t1, op=ALU.mult)))
            post.append(op("dve", [k1, zk], [yk], lambda e, Y=Y, Z=Z: e.tensor_tensor(out=Y, in0=t1, in1=Z, op=ALU.mult)))
            blocks.append((qz, ugroups, post))
        return [], blocks

    def conv_chunk(self, cc, slot, s):
        op = self.op
        o = []
        xk = self.ak("xpad")
        xall = [xk] + [self.ak("P", i) for i in range(4)] + [self.ak("t3"), self.ak("sq", 0), self.ak("sq", 1), self.ak("sse")]
        xp = self.xpad
        cb = 6 + cc * 4
        o.append(op("pool", [], xall, lambda e: e.memset(xp[:, 0:2], 0.0)))
        t1, t2, t3 = self.t1, self.t2, self.t3
        k1, k2, k3 = self.ak("t1"), self.ak("t2"), self.ak("t3")
        for j in range(NB):
            yk = self.ak("YT", slot, j)
            Y = self.YT[:, slot, j * 512:(j + 1) * 512]
            o += self.proj_fm(s, 1, j)
            o.append(op("act", [("ps", PJ)], [k1], lambda e: e.activation(out=t1, in_=self.bank(PJ), func=AF.Copy)))
            o += self.proj_fm(s, 0, j, pbank=PB)
            o.append(op("dve", [("ps", PB), k1], xall, lambda e, j=j: e.tensor_tensor(out=xp[:, 2 + j * 512:2 + (j + 1) * 512], in0=self.bank(PB), in1=t1, op=ALU.mult)))
            x0 = xp[:, j * 512:j * 512 + 512]
            x1 = xp[:, j * 512 + 1:j * 512 + 513]
            x2 = xp[:, j * 512 + 2:j * 512 + 514]
            cl = self.cols
            o.append(op("dve", xall + [("col", cb), ("col", cb + 3)], [k2], lambda e, x0=x0: e.tensor_scalar(out=t2, in0=x0, scalar1=cl[:, cb:cb + 1], scalar2=cl[:, cb + 3:cb + 4], op0=ALU.mult, op1=ALU.add)))
            o.append(op("dve", xall + [k2, ("col", cb + 1)], [k2], lambda e, x1=x1: e.scalar_tensor_tensor(out=t2, in0=x1, scalar=cl[:, cb + 1:cb + 2], in1=t2, op0=ALU.mult, op1=ALU.add)))
            o.append(op("dve", xall + [k2, ("col", cb + 2)], [k2], lambda e, x2=x2: e.scalar_tensor_tensor(out=t2, in0=x2, scalar=cl[:, cb + 2:cb + 3], in1=t2, op0=ALU.mult, op1=ALU.add)))
            o += self.proj_fm(s, 2, j)
            o.append(op("dve", [("ps", PJ), k2], [k2], lambda e: e.tensor_tensor(out=t2, in0=self.bank(PJ), in1=t2, op=ALU.mult)))
            o += self.proj_fm(s, 3, j, pbank=PB)
            o += self.gate_ops(PB, j % 2)
            zk = self.ak("ZS", j % 2)
            Z = self.ZS[j % 2]
            o.append(op("dve", [k2, zk], [yk], lambda e, Y=Y, Z=Z: e.tensor_tensor(out=Y, in0=t2, in1=Z, op=ALU.mult)))
        return o

    def fox_prologue(self):
        op = self.op
        T = self.T
        o = []
        WF = self.WF
        wfk = self.ak("WF")
        o.append(op("pool", [], [wfk], lambda e: e.memset(WF, 0.0)))
        for r0 in (0, 32, 64):
            o.append(op("pool", [wfk], [wfk], lambda e, r0=r0: e.dma_start(out=WF[:, :, r0:r0 + 12], in_=T["o_w_in"][:, 6144:6156].rearrange("(c p) j -> p c j", p=128)), dma=True))
            o.append(op("sp", [], [("col", 22)], lambda e, r0=r0: e.dma_start(out=self.cols[r0:r0 + 12, 22:23], in_=T["o_c_forget_b"].rearrange("(p o) -> p o", o=1)), dma=True))
        o.append(op("dve", [("col", 22)], [("col", 23)], lambda e: e.tensor_scalar(out=self.cols[:, 23:24], in0=self.cols[:, 22:23], scalar1=-1.0, scalar2=None, op0=ALU.mult)))
        f0, f1, f2 = self.fx
        kf = [self.ak("fx", i) for i in range(3)]
        hb, mb = self.fxb
        kh, km = self.ak("fxb", 0), self.ak("fxb", 1)
        nck = self.ak("WMh")
        ok1 = self.ak("onesf")
        o.append(op("pool", [], [ok1], lambda e: e.memset(self.onesf, 1.0)))
        ncum = self.ncum
        R = 76
        for b in range(NB):
            pf = self.bank(PJ, R)
            for c in range(8):
                o.append(op("pe", [wfk] + [("UT", 4 * b + i) for i in range(4)], [("ps", PJ)],
                            lambda e, c=c, b=b: e.matmul(pf, lhsT=WF[:, c, 0:R], rhs=self.UT[:, c, b * 512:(b + 1) * 512], start=(c == 0), stop=(c == 7))))
            o.append(op("act", [("ps", PJ), ("col", 23)], [kf[0]], lambda e: e.activation(out=f0[0:R], in_=pf, func=AF.Exp, bias=self.cols[0:R, 23:24], scale=-1.0)))
            o.append(op("act", [kf[0]], [kf[0]], lambda e: e.activation(out=f0[0:R], in_=f0[0:R], func=AF.Ln, bias=1.0)))
            init = 0.0 if b == 0 else ncum[0:R, b * 512 - 1:b * 512]
            o.append(op("dve", [kf[0], ok1, nck], [nck], lambda e, b=b, init=init: e.tensor_tensor_scan(out=ncum[0:R, b * 512:(b + 1) * 512], data0=self.onesf[0:R, :], data1=f0[0:R], initial=init, op0=ALU.mult, op1=ALU.add)))
            nb = ncum[0:R, b * 512:(b + 1) * 512]
            o.append(op("dve", [nck], [kf[1]], lambda e, nb=nb: e.tensor_scalar(out=f1[0:R], in0=nb, scalar1=-1.0, scalar2=None, op0=ALU.mult)))
            o.append(op("dve", [kf[1]], [kh], lambda e: e.tensor_copy(out=hb[0:R], in_=f1[0:R])))
            o.append(op("dve", [kf[1], kh], [kf[2]], lambda e: e.tensor_tensor(out=f2[0:R], in0=f1[0:R], in1=hb[0:R], op=ALU.subtract)))
            o.append(op("dve", [kf[2]], [km], lambda e: e.tensor_copy(out=mb[0:R], in_=f2[0:R])))
            o.append(op("dve", [kf[2], km], [kf[2]], lambda e: e.tensor_tensor(out=f2[0:R], in0=f2[0:R], in1=mb[0:R], op=ALU.subtract)))
            cs = self.Csplit
            o.append(op("dve", [kh], [("Csplit",)], lambda e, b=b: e.tensor_copy(out=cs[0:32, b * 512:(b + 1) * 512], in_=hb[0:32])))
            o.append(op("dve", [km], [("Csplit",)], lambda e, b=b: e.tensor_copy(out=cs[32:64, b * 512:(b + 1) * 512], in_=mb[32:64])))
            o.append(op("dve", [kf[2]], [("Csplit",)], lambda e, b=b: e.tensor_copy(out=cs[64:R, b * 512:(b + 1) * 512], in_=f2[64:R])))
        pt = self.ps[:, A0, 0:192]
        for t in range(NT):
            o.append(op("pe", [nck, "identf"], [("ps", A0)], lambda e, t=t: e.transpose(out=self.ps[:, A0, t * 12:(t + 1) * 12], in_=ncum[0:12, t * 128:(t + 1) * 128], identity=self.identf[0:12, 0:12])))
        o.append(op("dve", [("ps", A0)], [("kb",)], lambda e: e.tensor_copy(out=self.kb[:, :], in_=pt)))
        return o

    def main_phase(self, L):
        op = self.op
        T = self.T
        o = []
        wname = "e_w_in" if L == 0 else "o_w_in"
        if L == 0:
            chunks = [("attn", h) for h in range(8)] + [("conv", c) for c in range(4)] + [("xattn", x) for x in range(4)]
        else:
            chunks = [("attn", h) for h in range(12)] + [("xattn", x) for x in range(4)]

        def wloads(ci):
            kind, idx = chunks[ci]
            s = ci % 2
            if kind == "attn":
                if L == 0:
                    offs = [idx * 128, 1024 + idx * 128, 2048 + idx * 128, 3072 + idx * 128]
                else:
                    offs = [idx * 128, 1536 + idx * 128, 3072 + idx * 128, 4608 + idx * 128]
                return [self.load_w(wname, offs[k], s, k) for k in range(4)]
            if kind == "conv":
                offs = [4096 + idx * 128, 4608 + idx * 128, 5120 + idx * 128, 5632 + idx * 128]
                return [self.load_w(wname, offs[k], s, k) for k in range(4)]
            if L == 0:
                offs = {0: 6144 + idx * 128, 3: 6656 + idx * 128}
            else:
                offs = {0: 6156 + idx * 128, 3: 6668 + idx * 128}
            return [self.load_w(wname, offs[k], s, k) for k in (0, 3)]

        if L == 0:
            for j in range(NB):
                o.append(op("pool", [], [self.ak("Qaug"), self.ak("Q", j)], lambda e, j=j: e.memset(self.QB[j][0:64, :], 0.0)))
                o.append(op("sp", [], [self.ak("Qaug")], lambda e, j=j: e.dma_start(out=self.QA[j][64:68, :], in_=T["c_qa"][:, j * 512:(j + 1) * 512]), dma=True))
                o.append(op("sp", [self.ak("Qaug")], [self.ak("Qaug")], lambda e, j=j: e.dma_start(out=self.QB[j][0:4, :], in_=T["c_qa"][:, j * 512:(j + 1) * 512]), dma=True))
        o += wloads(0)
        for ci, (kind, idx) in enumerate(chunks):
            s = ci % 2
            slot = ci % 4
            if ci + 1 < len(chunks):
                o += wloads(ci + 1)
            if kind == "conv":
                o += self.conv_chunk(idx, slot, s)
            else:
                if kind == "attn":
                    kv, blocks = self.attn_head(L, idx, slot, s)
                else:
                    kv, blocks = self.xattn_head(L, idx, slot, s)
                for g in kv:
                    o += g
                for (qz, ugroups, post) in blocks:
                    for g in qz:
                        o += g
                    for g in ugroups:
                        o += g
                    o += post
            if slot == 3:
                o += self.outproj(L, ci // 4)
        return o

    def store_out(self):
        o = []
        T = self.T
        for t in range(NT):
            o.append(self.op("sp", self.hk(t), [("out", t)], lambda e, t=t: e.dma_start(out=T["out"][t * 128:(t + 1) * 128, :], in_=self.h[:, t, :]), dma=True))
        o.append(Op("sp", None, reads=[("out", t) for t in range(NT)]))
        return o


_CACHE = {}


def _get_prog(layers):
    key = tuple(layers)
    if key not in _CACHE:
        g = Gen(layers)
        nc = g.build()
        _CACHE[key] = (nc, g.in_names)
    return _CACHE[key]


def _run(layers, xin, mem, params, consts):
    nc, in_names = _get_prog(layers)
    in_maps = []
    for b in range(8):
        m = {}
        for n in in_names:
            if n == "x":
                m[n] = np.ascontiguousarray(xin[b])
            elif n == "mem":
                m[n] = np.ascontiguousarray(mem[b])
            elif n in consts:
                m[n] = consts[n]
            else:
                m[n] = params[n]
        in_maps.append(m)
    res = run_bass_kernel_spmd(nc, in_maps, core_ids=list(range(8)))
    return np.stack([np.asarray(r["out"]) for r in res.results], 0)


FUSED = True


def kernel(**inputs):
    x = np.asarray(inputs["x"], np.float32)
    mem = np.asarray(inputs["mem"], np.float32)
    params = {}
    for n in L0_NAMES + L1_NAMES:
        a = np.asarray(inputs[n], np.float32)
        params[n] = np.ascontiguousarray(a[0])
    consts = _consts()
    if FUSED:
        return _run((0, 1), x, mem, params, consts).astype(np.float32)
    h1 = _run((0,), x, mem, params, consts)
    return _run((1,), h1, mem, params, consts).astype(np.float32)
```

```python
import contextlib
import math
import numpy as np
import ml_dtypes
import concourse.bass as bass
import concourse.mybir as mybir
from concourse.bass_utils import run_bass_kernel_spmd

F32 = mybir.dt.float32
BF16 = mybir.dt.bfloat16
AF = mybir.ActivationFunctionType
ALU = mybir.AluOpType
AX = mybir.AxisListType

ENGS = ("pe", "act", "dve", "pool", "sp")
EPS = 1e-6
S_LEN = 2048
D = 1024
NT = 16
NB = 4


class Op:
    __slots__ = ("eng", "fn", "reads", "writes", "dma", "idx", "sig", "deps", "dsem", "dtarget", "dprev", "glue")

    def __init__(self, eng, fn, reads=(), writes=(), dma=False):
        self.eng = eng
        self.fn = fn
        self.reads = tuple(reads)
        self.writes = tuple(writes)
        self.dma = dma
        self.sig = None
        self.deps = ()
        self.dsem = None
        self.glue = False


class Sched:
    NDMA_SEMS = 8

    def __init__(self, same_engine_sync=True):
        self.ops = []
        self.same_engine_sync = same_engine_sync

    def add(self, ops):
        if isinstance(ops, Op):
            self.ops.append(ops)
        else:
            for o in ops:
                self.add(o)

    def plan(self):
        last_w = {}
        readers = {}
        ops = self.ops
        for i, op in enumerate(ops):
            op.idx = i
            deps = set()
            for r in op.reads:
                w = last_w.get(r)
                if w is not None:
                    deps.add(w)
            for wr in op.writes:
                w = last_w.get(wr)
                if w is not None:
                    deps.add(w)
                rs = readers.get(wr)
                if rs:
                    deps.update(rs)
            deps.discard(i)
            for r in op.reads:
                readers.setdefault(r, []).append(i)
            for wr in op.writes:
                last_w[wr] = i
                readers[wr] = []
            latest = {}
            dmadeps = []
            for dix in deps:
                d = ops[dix]
                if d.dma:
                    dmadeps.append(dix)
                else:
                    if d.eng == op.eng and not op.dma:
                        if op.eng == "pe" or not self.same_engine_sync:
                            continue
                    if d.eng not in latest or latest[d.eng] < dix:
                        latest[d.eng] = dix
            op.deps = tuple(sorted(list(latest.values()) + dmadeps))
        need = set()
        for op in ops:
            for dix in op.deps:
                if not ops[dix].dma:
                    need.add(dix)
        cnt = {e: 0 for e in ENGS}
        for op in ops:
            if op.dma:
                continue
            if op.idx in need:
                cnt[op.eng] += 1
                op.sig = cnt[op.eng]
        dcount = {e: 0 for e in ENGS}
        for op in ops:
            if op.dma:
                n = dcount[op.eng]
                dcount[op.eng] += 1
                op.dsem = (op.eng, n % self.NDMA_SEMS)
                op.dtarget = 16 * (n // self.NDMA_SEMS + 1)
                op.dprev = 16 * (n // self.NDMA_SEMS)
        self.sigcount = cnt
        self.dmacount = dcount

    def emit(self, block, esems, dsems):
        handles = {"pe": block.tensor, "act": block.scalar, "dve": block.vector, "pool": block.gpsimd,
                   "sp": block.sync}
        ops = self.ops
        for eng in ENGS:
            mine = [op for op in ops if op.eng == eng]

            def body(e, mine=mine, eng=eng):
                waited = {}

                def wait(key, sem, val):
                    if waited.get(key, 0) >= val:
                        return
                    waited[key] = val
                    e.wait_ge(sem, val)

                for op in mine:
                    for dix in op.deps:
                        d = ops[dix]
                        if d.dma:
                            wait(d.dsem, dsems[d.dsem], d.dtarget)
                        else:
                            wait(d.eng, esems[d.eng], d.sig)
                    if op.dma and op.dprev > 0:
                        wait(op.dsem, dsems[op.dsem], op.dprev)
                    if op.fn is None:
                        continue
                    ins = op.fn(e)
                    if op.dma:
                        ins.then_inc(dsems[op.dsem], 16)
                    elif op.sig is not None:
                        ins.then_inc(esems[eng], 1)

            handles[eng](body)


def interleave(units, fillers):
    out = []
    nu, nf = len(units), len(fillers)
    if nu == 0:
        for f in fillers:
            out.extend(f)
        return out
    fi = 0
    for u in range(nu):
        out.extend(units[u])
        tgt = ((u + 1) * nf) // nu
        while fi < tgt:
            out.extend(fillers[fi])
            fi += 1
    return out


def _consts():
    bf = ml_dtypes.bfloat16
    tri = (np.arange(128)[None, :] >= np.arange(128)[:, None]).astype(np.float32)
    bd = np.zeros((128, 128), np.float32)
    bd[:64, :64] = 1.0
    bd[64:, 64:] = 1.0
    ident = np.eye(128, dtype=np.float32)
    ones = np.ones((128, 128), np.float32)
    cst = np.concatenate([tri, bd, ident, ones], axis=1).astype(bf)
    pos = np.arange(S_LEN)
    qa = np.stack([(pos % 128).astype(np.float32), (pos // 128).astype(np.float32),
                   np.ones(S_LEN, np.float32), np.ones(S_LEN, np.float32)], 0)
    ka = np.zeros((8, 4, S_LEN), np.float32)
    for h in range(8):
        sl = 2.0 ** (-(h + 1))
        ka[h, 0] = -sl
        ka[h, 1] = -sl * 128.0
        ka[h, 2] = sl * (pos % 128)
        ka[h, 3] = sl * 128.0 * (pos // 128)
    assert np.array_equal(ka.astype(bf).astype(np.float32), ka)
    assert np.array_equal(qa.astype(bf).astype(np.float32), qa)
    sel = np.zeros((76, 12 * 128), np.float32)
    for h in range(12):
        for r in (h, 32 + h, 64 + h):
            sel[r, h * 128:(h + 1) * 128] = 1.0
    identf = np.eye(128, dtype=np.float32)
    return {"c_cst": cst, "c_qa": qa.astype(bf), "c_ka": ka.astype(bf), "c_sel": sel.astype(bf),
            "c_identf": identf}


L0_NAMES = ["e_norm_g", "e_w_in", "e_w_out", "e_a_q_norm_g", "e_a_k_norm_g", "e_a_lambda", "e_a_out_norm_g",
            "e_b_conv_w", "e_b_conv_b", "e_x_q_norm_g", "e_x_k_norm_g", "e_mem_norm_g", "e_w_mem_kv"]
L1_NAMES = ["o_norm_g", "o_w_in", "o_w_out", "o_c_q_norm_g", "o_c_k_norm_g", "o_c_forget_b", "o_x_q_norm_g",
            "o_x_k_norm_g", "o_mem_norm_g", "o_w_mem_kv"]
SHAPES = {
    "e_norm_g": [1024], "e_w_in": [1024, 7168], "e_w_out": [2048, 1024], "e_a_q_norm_g": [64],
    "e_a_k_norm_g": [64], "e_a_lambda": [4, 64], "e_a_out_norm_g": [128], "e_b_conv_w": [3, 512],
    "e_b_conv_b": [512], "e_x_q_norm_g": [128], "e_x_k_norm_g": [128], "e_mem_norm_g": [1024],
    "e_w_mem_kv": [1024, 1024],
    "o_norm_g": [1024], "o_w_in": [1024, 7180], "o_w_out": [2048, 1024], "o_c_q_norm_g": [128],
    "o_c_k_norm_g": [128], "o_c_forget_b": [12], "o_x_q_norm_g": [128], "o_x_k_norm_g": [128],
    "o_mem_norm_g": [1024], "o_w_mem_kv": [1024, 1024],
}

PJ, PB, S0, S1, A0, A1, A2, A3 = range(8)


class Gen:
    def __init__(self, layers, interleave_on=True):
        self.layers = tuple(layers)
        self.il = interleave_on
        self.nc = bass.Bass("TRN2", target_bir_lowering=False)
        self.S = Sched()
        self.stack = contextlib.ExitStack()
        self.sb_bytes = 0
        self.arena_keys = set()
        self.wctr = 0
        self.pctr = 0
        self.sctr = 0

    def sb(self, name, shape, dt):
        n = 1
        for s in shape[1:]:
            n *= s
        self.sb_bytes += n * (4 if dt == F32 else 2)
        return self.stack.enter_context(self.nc.sbuf_tensor(name, shape, dt))

    def carve(self, nbytes):
        off = self.ar_off
        self.ar_off += nbytes
        assert self.ar_off <= self.AR_BYTES, (self.ar_off, self.AR_BYTES)
        return off

    def arv(self, off, shape, dt, base=None):
        n = 1
        for s in shape:
            n *= s
        nb = n * (4 if dt == F32 else 2)
        v = (self.AR if base is None else base)[:, off // 2:(off + nb) // 2]
        if dt == F32:
            v = v.bitcast(F32)
        if len(shape) == 2:
            v = v.rearrange("p (a b) -> p a b", a=shape[0])
        elif len(shape) == 3:
            v = v.rearrange("p (a b c) -> p a b c", a=shape[0], b=shape[1])
        return v

    def ak(self, *key):
        self.arena_keys.add(key)
        return key

    def op(self, eng, reads, writes, fn, dma=False):
        return Op(eng, fn, reads, writes, dma)

    def fence(self):
        keys = sorted(self.arena_keys, key=repr)
        d = self.dummy
        return [Op("pool", lambda e: e.memset(d[:], 0.0), reads=(), writes=keys)]

    def build(self):
        nc = self.nc
        T = {}
        T["x"] = nc.dram_tensor("x", [S_LEN, D], F32, kind="ExternalInput").ap()
        T["mem"] = nc.dram_tensor("mem", [256, D], F32, kind="ExternalInput").ap()
        names = []
        if 0 in self.layers:
            names += L0_NAMES
        if 1 in self.layers:
            names += L1_NAMES
        for n in names:
            T[n] = nc.dram_tensor(n, SHAPES[n], F32, kind="ExternalInput").ap()
        T["c_cst"] = nc.dram_tensor("c_cst", [128, 512], BF16, kind="ExternalInput").ap()
        T["c_qa"] = nc.dram_tensor("c_qa", [4, S_LEN], BF16, kind="ExternalInput").ap()
        T["c_ka"] = nc.dram_tensor("c_ka", [8, 4, S_LEN], BF16, kind="ExternalInput").ap()
        T["c_sel"] = nc.dram_tensor("c_sel", [76, 1536], BF16, kind="ExternalInput").ap()
        T["c_identf"] = nc.dram_tensor("c_identf", [128, 128], F32, kind="ExternalInput").ap()
        T["out"] = nc.dram_tensor("out", [S_LEN, D], F32, kind="ExternalOutput").ap()
        import os as _os
        self.ydbg = None
        if _os.environ.get("MKYDBG"):
            self.ydbg = nc.dram_tensor("ydbg", [len(self.layers), 2048, S_LEN], BF16, kind="ExternalOutput").ap()
        self.T = T
        self.in_names = ["x", "mem"] + names + ["c_cst", "c_qa", "c_ka", "c_sel", "c_identf"]

        st = self.stack
        sb = self.sb
        self.h = sb("h", [128, NT, D], F32)
        self.UT = sb("UT", [128, 8, S_LEN], BF16)
        self.KA = [sb(f"KA{s}", [128, S_LEN], BF16) for s in range(2)]
        self.V = [sb(f"V{s}", [128, NT, 128], BF16) for s in range(2)]
        self.W = [[sb(f"W{s}_{k}", [128, 8, 128], BF16) for k in range(4)] for s in range(2)]
        self.WO = sb("WO", [128, 4, D], BF16)
        self.cols = sb("cols", [128, 64], F32)
        self.cst = sb("cst", [128, 512], BF16)
        self.MKT = sb("MKT", [128, 4, 256], BF16)
        self.MV = sb("MV", [128, 2, 512], BF16)
        self.AR2 = sb("AR2", [128, 16448 // 2], BF16)
        self.KB = [self.arv(4096 * i, [S_LEN], BF16, base=self.AR2) for i in range(2)]
        self.xpad = self.arv(8192, [2064], F32, base=self.AR2)
        self.Csplit = self.arv(0, [S_LEN], BF16, base=self.AR2)
        self.sel = self.arv(4096, [1536], BF16, base=self.AR2)
        self.kb = self.arv(7168, [192], F32, base=self.AR2)
        for s_ in range(2):
            self.arena_keys.add(("Kaug", s_))
            for b_ in range(NB):
                self.arena_keys.add(("K", s_, b_))
        self.arena_keys.update([("Csplit",), "sel", ("kb",), ("xpad",)])
        self.identf = sb("identf", [128, 128], F32)
        self.dummy = sb("dmy_t", [128, 2], F32)
        self.small = sb("small", [128, 64], F32)
        self.AR_BYTES = 47104
        self.AR = sb("AR", [128, self.AR_BYTES // 2], BF16)
        assert self.sb_bytes <= 212000, self.sb_bytes
        self.ar_off = 0
        c = self.carve
        self.YT = self.arv(c(16384), [4, S_LEN], BF16)
        self.QA = [self.arv(c(1024), [512], BF16) for _ in range(4)]
        self.QB = [self.arv(c(1024), [512], BF16) for _ in range(4)]
        self.ZS = [self.arv(c(1024), [512], BF16) for _ in range(4)]
        self.t1 = self.arv(c(2048), [512], F32)
        self.t2 = self.arv(c(2048), [512], F32)
        self.P = [self.arv(c(1024), [512], BF16) for _ in range(4)]
        self.t3 = self.arv(c(2048), [512], F32)
        self.sq = [self.arv(c(1024), [512], BF16) for _ in range(2)]
        self.sse = self.arv(c(2048), [512], F32)
        self.rstd = self.arv(c(2048), [512], F32)
        self.QX = self.arv(c(1024), [512], BF16)
        self.PX = self.arv(c(1024), [512], BF16)
        assert self.ar_off <= self.AR_BYTES
        self.ar_off = 0
        self.gbc = self.arv(c(4096), [D], F32)
        self.Usc = [self.arv(c(2048), [D], BF16) for _ in range(2)]
        woff = self.ar_off
        self.WMh = self.arv(c(8192), [8, 512], BF16)
        self.ncum = self.arv(woff, [S_LEN], F32)
        self.memf = self.arv(c(4096), [D], F32)
        self.memUT = self.arv(c(4096), [8, 256], BF16)
        foff = self.ar_off
        self.fx = [self.arv(c(2048), [512], F32) for _ in range(3)]
        self.junk = self.arv(foff, [D], BF16)
        self.fxb = [self.arv(c(1024), [512], BF16) for _ in range(2)]
        self.WF = self.arv(c(2048), [8, 128], BF16)
        self.onesf = self.arv(c(2048), [512], F32)
        assert self.ar_off <= 38912, self.ar_off

        self.ps = st.enter_context(nc.psum_tensor("ps", [128, 8, 512], F32))
        self.esems = {e: st.enter_context(nc.semaphore(f"s_{e}")) for e in ENGS}
        self.dsems = {(e, k): st.enter_context(nc.semaphore(f"d_{e}{k}"))
                      for e in ("sp", "pool") for k in range(Sched.NDMA_SEMS)}

        S = self.S
        segs = [self.setup_ops()]
        for li, L in enumerate(self.layers):
            segs.append("fence")
            segs.append(self.prologue(L))
            segs.append("fence")
            segs.append(self.main_phase(L))
        segs.append(self.store_out())
        for sg in segs:
            S.add(self.fence() if isinstance(sg, str) else sg)
        S.plan()
        with nc.Block() as block:
            S.emit(block, self.esems, self.dsems)
        self.stack.close()
        return nc

    def bank(self, b, rows=128, c0=0, c1=512):
        return self.ps[0:rows, b, c0:c1]

    def col(self, j, rows=128, r0=0):
        return self.cols[r0:r0 + rows, j:j + 1]

    def hk(self, t):
        return [("h", t, 0), ("h", t, 1)]

    def setup_ops(self):
        T = self.T
        o = []
        op = self.op
        h = self.h
        for t in range(NT):
            o.append(op("sp", [], self.hk(t), lambda e, t=t: e.dma_start(out=h[:, t, :], in_=T["x"][t * 128:(t + 1) * 128, :]), dma=True))
        o.append(op("sp", [], ["cst"], lambda e: e.dma_start(out=self.cst[:], in_=T["c_cst"]), dma=True))
        o.append(op("sp", [], ["identf"], lambda e: e.dma_start(out=self.identf[:], in_=T["c_identf"]), dma=True))
        o.append(op("pool", [], ["dummy"], lambda e: e.memset(self.dummy[:], 0.0)))
        o.append(op("pool", [], [("col", j) for j in range(64)], lambda e: e.memset(self.cols[:], 0.0)))
        if 0 in self.layers:
            for s in range(2):
                o.append(op("pool", [], [("K", s, b) for b in range(NB)] + [("Kaug", s)], lambda e, s=s: e.memset(self.KB[s][0:64, :], 0.0)))
        return o

    def consts(self):
        cst = self.cst
        return {"tri": cst[:, 0:128], "bd": cst[:, 128:256], "ident": cst[:, 256:384], "ones": cst[:, 384:512]}

    def small_rstd(self, src_key, src_ap, dst_key, dst_ap, n, scale, eps):
        o = []
        tmpk = ("small_tmp",)
        tmp = self.small[:, 32:32 + n]
        o.append(self.op("act", [src_key], [tmpk], lambda e: e.activation(out=tmp, in_=src_ap, func=AF.Ln, bias=float(eps), scale=float(scale))))
        o.append(self.op("act", [tmpk], [dst_key], lambda e: e.activation(out=dst_ap, in_=tmp, func=AF.Exp, scale=-0.5)))
        return o

    def norm_rows_to_T(self, src_key_list, src_ap, rstd_col_key, rstd_col, ui, dst_writes, dst_ap_fn, tb):
        o = []
        C = self.consts()
        uk = self.ak("Usc", ui)
        U = self.Usc[ui]
        o.append(self.op("dve", list(src_key_list) + [rstd_col_key, self.ak("gbc")], [uk],
                         lambda e: e.scalar_tensor_tensor(out=U, in0=src_ap, scalar=rstd_col, in1=self.gbc, op0=ALU.mult, op1=ALU.mult)))
        psT = self.ps[:, tb, :].bitcast(BF16)
        for c in range(8):
            o.append(self.op("pe", [uk, "cst"], [("ps", tb)],
                             lambda e, c=c: e.transpose(out=psT[:, c * 128:(c + 1) * 128], in_=U[:, c * 128:(c + 1) * 128], identity=C["ident"])))
        o.append(self.op("act", [("ps", tb)], dst_writes,
                         lambda e: e.activation(out=dst_ap_fn(), in_=psT.rearrange("p (c j) -> p c j", c=8), func=AF.Copy)))
        return o

    def prologue(self, L):
        T = self.T
        op = self.op
        pre = "e_" if L == 0 else "o_"
        o = []
        C = self.consts()
        h = self.h
        cols = self.cols
        def vec_col(name, j, n=128, r0=0, src_off=0):
            src = T[name]
            ap = src[src_off:src_off + n].rearrange("(p o) -> p o", o=1)
            return op("sp", [], [("col", j)], lambda e: e.dma_start(out=cols[r0:r0 + n, j:j + 1], in_=ap), dma=True)

        def scale_col(j, f):
            return op("dve", [("col", j)], [("col", j)], lambda e: e.tensor_scalar(out=cols[:, j:j + 1], in0=cols[:, j:j + 1], scalar1=float(f), scalar2=None, op0=ALU.mult))

        if L == 0:
            for r0 in (0, 64):
                o.append(vec_col("e_a_q_norm_g", 0, 64, r0))
                o.append(vec_col("e_a_k_norm_g", 1, 64, r0))
            o.append(scale_col(1, 8.0))
            o.append(vec_col("e_a_out_norm_g", 2))
            lam_init = 0.8 - 0.6 * math.exp(-0.3 * 0)
            o.append(scale_col(2, math.sqrt(128.0) * (1.0 - lam_init)))
            lpb = self.fx[0][:, 0:256]
            lpk = self.ak("fx", 0)
            o.append(op("sp", [], [lpk], lambda e: e.dma_start(out=lpb, in_=T["e_a_lambda"].rearrange("a b -> (a b)").rearrange("(o n) -> o n", o=1).partition_broadcast(128)), dma=True))
            prod = self.fx[1][:, 0:128]
            pk = self.ak("fx", 1)
            lp4 = lpb.rearrange("p (a b) -> p a b", a=4)
            for q in range(2):
                o.append(op("dve", [lpk], [pk], lambda e, q=q: e.tensor_tensor(out=prod[:, q * 64:(q + 1) * 64], in0=lp4[:, 2 * q, :], in1=lp4[:, 2 * q + 1, :], op=ALU.mult)))
            sm = self.small
            o.append(op("dve", [pk], [("sm", 0)], lambda e: e.reduce_sum(out=sm[:, 0:2], in_=prod.rearrange("p (a b) -> p a b", a=2), axis=AX.X)))
            o.append(op("act", [("sm", 0)], [("sm", 1)], lambda e: e.activation(out=sm[:, 2:4], in_=sm[:, 0:2], func=AF.Exp)))
            o.append(op("dve", [("sm", 1)], [("sm", 2)], lambda e: e.tensor_tensor(out=sm[:, 4:5], in0=sm[:, 2:3], in1=sm[:, 3:4], op=ALU.subtract)))
            o.append(op("dve", [("sm", 2)], [("col", 5)], lambda e: e.tensor_scalar(out=cols[:, 5:6], in0=sm[:, 4:5], scalar1=lam_init, scalar2=-1.0, op0=ALU.add, op1=ALU.mult)))
            for cc in range(4):
                for k in range(3):
                    o.append(op("sp", [], [("col", 6 + cc * 4 + k)], lambda e, cc=cc, k=k: e.dma_start(out=cols[:, 6 + cc * 4 + k:7 + cc * 4 + k], in_=T["e_b_conv_w"][k, cc * 128:(cc + 1) * 128].rearrange("(p o) -> p o", o=1)), dma=True))
                o.append(op("sp", [], [("col", 6 + cc * 4 + 3)], lambda e, cc=cc: e.dma_start(out=cols[:, 9 + cc * 4:10 + cc * 4], in_=T["e_b_conv_b"][cc * 128:(cc + 1) * 128].rearrange("(p o) -> p o", o=1)), dma=True))
        else:
            o.append(vec_col("o_c_q_norm_g", 0))
            o.append(vec_col("o_c_k_norm_g", 1))
            o.append(scale_col(1, math.sqrt(128.0)))
        o.append(vec_col(pre + "x_q_norm_g", 3))
        o.append(vec_col(pre + "x_k_norm_g", 4))
        o.append(scale_col(4, math.sqrt(128.0)))

        gk = self.ak("gbc")
        o.append(op("sp", [], [gk], lambda e: e.dma_start(out=self.gbc, in_=T[pre + "mem_norm_g"].rearrange("(o n) -> o n", o=1).partition_broadcast(128)), dma=True))
        mfk = self.ak("memf")
        mutk = self.ak("memUT")
        sm = self.small
        for mt in range(2):
            o.append(op("sp", [], [mfk], lambda e, mt=mt: e.dma_start(out=self.memf, in_=T["mem"][mt * 128:(mt + 1) * 128, :]), dma=True))
            jk = self.ak("fx", 0)
            o.append(op("act", [mfk], [jk, ("sm", 10)], lambda e: e.activation(out=self.junk, in_=self.memf, func=AF.Square, accum_out=sm[:, 10:11])))
            o += self.small_rstd(("sm", 10), sm[:, 10:11], ("sm", 11), sm[:, 11:12], 1, 1.0 / D, EPS)
            tb = A0 + mt
            o += self.norm_rows_to_T([mfk], self.memf, ("sm", 11), sm[:, 11:12], mt, [mutk],
                                     lambda mt=mt: self.memUT[:, :, mt * 128:(mt + 1) * 128], tb)
        wmk = self.ak("WMh")
        wsrc = T[pre + "w_mem_kv"]
        o.append(op("pool", [], [wmk], lambda e: e.dma_start(out=self.WMh, in_=wsrc[:, 0:512].rearrange("(c p) j -> p c j", p=128)), dma=True))
        for hx in range(4):
            pj = self.bank(PJ, 128, 0, 256)
            for c in range(8):
                o.append(op("pe", [wmk, mutk], [("ps", PJ)], lambda e, c=c, hx=hx: e.matmul(pj, lhsT=self.WMh[:, c, hx * 128:(hx + 1) * 128], rhs=self.memUT[:, c, :], start=(c == 0), stop=(c == 7))))
            o += self.norm_chain(PJ, 256, C["ones"], 128, 4, [("MKT", hx)],
                                 [(0, 128, self.MKT[:, hx, :])], 0)
        o.append(op("pool", [], [wmk], lambda e: e.dma_start(out=self.WMh, in_=wsrc[:, 512:1024].rearrange("(c p) j -> p c j", p=128)), dma=True))
        for mt in range(2):
            pv = self.bank(PJ)
            for c in range(8):
                o.append(op("pe", [wmk, mutk], [("ps", PJ)], lambda e, c=c, mt=mt: e.matmul(pv, lhsT=self.memUT[:, c, mt * 128:(mt + 1) * 128], rhs=self.WMh[:, c, :], start=(c == 0), stop=(c == 7))))
            o.append(op("act", [("ps", PJ)], [("MV", mt)], lambda e, mt=mt: e.activation(out=self.MV[:, mt, :], in_=pv, func=AF.Copy)))

        o.append(op("sp", [], [gk], lambda e: e.dma_start(out=self.gbc, in_=T[pre + "norm_g"].rearrange("(o n) -> o n", o=1).partition_broadcast(128)), dma=True))
        jk = self.ak("fx", 0)
        for t in range(NT):
            o.append(op("act", self.hk(t), [jk, ("hss", t)], lambda e, t=t: e.activation(out=self.junk, in_=h[:, t, :], func=AF.Square, accum_out=sm[:, 12 + t:13 + t])))
        o.append(op("act", [("hss", t) for t in range(NT)], [("small_tmp",)], lambda e: e.activation(out=sm[:, 32:48], in_=sm[:, 12:28], func=AF.Ln, bias=float(EPS), scale=1.0 / D)))
        o.append(op("act", [("small_tmp",)], [("hrstd",)], lambda e: e.activation(out=sm[:, 48:64], in_=sm[:, 32:48], func=AF.Exp, scale=-0.5)))
        for t in range(NT):
            o += self.norm_rows_to_T(self.hk(t), h[:, t, :], ("hrstd",), sm[:, 48 + t:49 + t], t % 2, [("UT", t)],
                                     lambda t=t: self.UT[:, :, t * 128:(t + 1) * 128], A0 + (t % 4))
        if L == 1:
            o += self.fox_prologue()
        return o

    def norm_chain(self, pbank, n, ones_ap, dsz, gcol, dst_writes, dsts, si):
        op = self.op
        o = []
        pj = self.bank(pbank, 128, 0, n)
        sq = self.sq[si][:, 0:n]
        sse = self.sse[:, 0:n]
        rstd = self.rstd[:, 0:n]
        sqk, ssek, rk = self.ak("sq", si), self.ak("sse"), self.ak("rstd")
        pb = self.bank(PB, 128, 0, n)
        o.append(op("act", [("ps", pbank)], [sqk], lambda e: e.activation(out=sq, in_=pj, func=AF.Square)))
        o.append(op("pe", [sqk, "cst"], [("ps", PB)], lambda e: e.matmul(pb, lhsT=ones_ap, rhs=sq, start=True, stop=True)))
        o.append(op("act", [("ps", PB)], [ssek], lambda e: e.activation(out=sse, in_=pb, func=AF.Ln, bias=float(dsz * EPS))))
        o.append(op("act", [ssek], [rk], lambda e: e.activation(out=rstd, in_=sse, func=AF.Exp, scale=-0.5)))
        for (r0, nr, dst) in dsts:
            o.append(op("dve", [("ps", pbank), rk, ("col", gcol)], dst_writes,
                        lambda e, r0=r0, nr=nr, dst=dst: e.scalar_tensor_tensor(out=dst, in0=self.ps[r0:r0 + nr, pbank, 0:n], scalar=self.cols[r0:r0 + nr, gcol:gcol + 1], in1=self.rstd[r0:r0 + nr, 0:n], op0=ALU.mult, op1=ALU.mult)))
        return o

    def load_w(self, wname, col0, s, k, ncols=128):
        src = self.T[wname][:, col0:col0 + ncols].rearrange("(c p) j -> p c j", p=128)
        dst = self.W[s][k]
        return self.op("pool", [], [("W", s, k)], lambda e: e.dma_start(out=dst[:, :, 0:ncols], in_=src), dma=True)

    def proj_fm(self, s, k, b, pbank=PJ, n=512):
        o = []
        W = self.W[s][k]
        out = self.bank(pbank)
        for c in range(8):
            o.append(self.op("pe", [("W", s, k)] + [("UT", 4 * b + i) for i in range(4)], [("ps", pbank)],
                             lambda e, c=c: e.matmul(out, lhsT=W[:, c, :], rhs=self.UT[:, c, b * 512:(b + 1) * 512], start=(c == 0), stop=(c == 7))))
        return o

    def gate_ops(self, pbank, zi):
        o = []
        zk = self.ak("ZS", zi)
        Z = self.ZS[zi]
        pj = self.bank(pbank)
        o.append(self.op("act", [("ps", pbank)], [zk], lambda e: e.activation(out=Z, in_=pj, func=AF.Exp, scale=-1.0)))
        o.append(self.op("act", [zk], [zk], lambda e: e.activation(out=Z, in_=Z, func=AF.Ln, bias=1.0)))
        o.append(self.op("act", [zk], [zk], lambda e: e.activation(out=Z, in_=Z, func=AF.Exp, scale=-1.0)))
        o.append(self.op("dve", [("ps", pbank), zk], [zk], lambda e: e.tensor_tensor(out=Z, in0=pj, in1=Z, op=ALU.mult)))
        return o

    def attn_block(self, units, post):
        LA = 2
        groups = []
        n = len(units)
        for u in range(n + LA):
            g = []
            if u < n:
                g += units[u]["qk"] + units[u]["exp"]
            if u - LA >= 0:
                g += units[u - LA]["pv"]
            groups.append(g)
        return groups, post

    def stage_slices(self, fl):
        slices = []
        cur = []
        wr = {}
        rd = {}
        for op_ in fl:
            cut = False
            if cur and not cur[-1].glue:
                for r in op_.reads:
                    e = wr.get(r)
                    if e is not None and e != op_.eng:
                        cut = True
                        break
                if not cut:
                    for w in op_.writes:
                        e = wr.get(w)
                        if e is not None and e != op_.eng:
                            cut = True
                            break
                        es = rd.get(w)
                        if es and (len(es) > 1 or op_.eng not in es):
                            cut = True
                            break
            if cut:
                slices.append(cur)
                cur = []
                wr = {}
                rd = {}
            cur.append(op_)
            weng = "dmaq" if op_.dma else op_.eng
            for r in op_.reads:
                rd.setdefault(r, set()).add(weng)
            for w in op_.writes:
                wr[w] = weng
        if cur:
            slices.append(cur)
        return slices

    def next_p(self):
        i = self.pctr % 4
        self.pctr += 1
        return i

    def next_s(self):
        i = self.sctr % 2
        self.sctr += 1
        return S0 + i

    def make_unit(self, kT, k_reads, qT, q_reads, n0, extra_mm, bias, bias_reads, diag, v_ap, v_reads, obank, lbank, first, last, px=False):
        op = self.op
        C = self.consts()
        sb = self.next_s()
        if px:
            P = self.PX
            pk = self.ak("PX")
        else:
            pi = self.next_p()
            P = self.P[pi]
            pk = self.ak("P", pi)
        sc = self.bank(sb, 128, n0, 512)
        qk = []
        qk.append(op("pe", k_reads + q_reads, [("ps", sb)], lambda e: e.matmul(sc, lhsT=kT, rhs=qT, start=True, stop=(extra_mm is None))))
        if extra_mm is not None:
            l2, r2, rd2 = extra_mm
            qk.append(op("pe", rd2, [("ps", sb)], lambda e: e.matmul(sc, lhsT=l2, rhs=r2, start=False, stop=True)))
        ex = []
        if bias is None:
            ex.append(op("act", [("ps", sb)], [pk], lambda e: e.activation(out=P[:, n0:512], in_=sc, func=AF.Exp)))
        else:
            ex.append(op("act", [("ps", sb)] + bias_reads, [pk], lambda e: e.activation(out=P[:, n0:512], in_=sc, func=AF.Exp, bias=bias)))
        if diag:
            ex.append(op("pool", [pk, "cst"], [pk], lambda e: e.tensor_tensor(out=P[:, n0:n0 + 128], in0=P[:, n0:n0 + 128], in1=C["tri"], op=ALU.mult)))
        pv = []
        ob = self.bank(obank, 128, n0, 512)
        lb = self.bank(lbank, 128, n0, 512)
        pv.append(op("pe", [pk] + v_reads, [("ps", obank)], lambda e: e.matmul(ob, lhsT=v_ap, rhs=P[:, n0:512], start=first, stop=last)))
        pv.append(op("pe", [pk, "cst"], [("ps", lbank)], lambda e: e.matmul(lb, lhsT=C["ones"], rhs=P[:, n0:512], start=first, stop=last)))
        return {"qk": qk, "exp": ex, "pv": pv}

    def outproj(self, L, rowblocks, banks):
        op = self.op
        wname = "e_w_out" if L == 0 else "o_w_out"
        ld = []
        for c, rb in enumerate(rowblocks):
            src = self.T[wname][rb * 128:(rb + 1) * 128, :]
            ld.append(op("pool", [], [("WO", c)], lambda e, c=c, src=src: e.dma_start(out=self.WO[:, c, :], in_=src), dma=True))
        o = []
        i = 0
        for t in range(NT):
            for n in range(2):
                bnk = banks[i % len(banks)]
                i += 1
                out = self.bank(bnk)
                for c in range(4):
                    o.append(op("pe", [("WO", c), self.ak("YT", c, t // 4)], [("ps", bnk)],
                                lambda e, c=c, t=t, n=n, out=out: e.matmul(out, lhsT=self.YT[:, c, t * 128:(t + 1) * 128], rhs=self.WO[:, c, n * 512:(n + 1) * 512], start=(c == 0), stop=(c == 3))))
                hv = self.h[:, t, n * 512:(n + 1) * 512]
                o.append(op("dve", [("ps", bnk), ("h", t, n)], [("h", t, n)], lambda e, hv=hv, out=out: e.tensor_tensor(out=hv, in0=out, in1=hv, op=ALU.add)))
        return ld, o

    def prep_kv(self, L, s, ones_ap, dsz, split):
        op = self.op
        groups = []
        for b in range(NB):
            g1 = self.proj_fm(s, 1, b)
            if split:
                dsts = [(0, 64, self.KA[s][0:64, b * 512:(b + 1) * 512]), (64, 64, self.KB[s][64:128, b * 512:(b + 1) * 512])]
            else:
                dsts = [(0, 128, self.KA[s][:, b * 512:(b + 1) * 512])]
            g2 = self.norm_chain(PJ, 512, ones_ap, dsz, 1, [("K", s, b)], dsts, b % 2)
            groups.append(g1 + g2)
        Wv = self.W[s][2]
        for tg in range(4):
            g = []
            pv = self.bank(PJ)
            for ti in range(4):
                t = tg * 4 + ti
                for c in range(8):
                    g.append(op("pe", [("W", s, 2), ("UT", t)], [("ps", PJ)],
                                lambda e, c=c, t=t, ti=ti: e.matmul(self.ps[:, PJ, ti * 128:(ti + 1) * 128], lhsT=self.UT[:, c, t * 128:(t + 1) * 128], rhs=Wv[:, c, :], start=(c == 0), stop=(c == 7))))
            g.append(op("act", [("ps", PJ)], [("V", s, tg * 4 + ti) for ti in range(4)],
                        lambda e, tg=tg: e.activation(out=self.V[s][:, tg * 4:(tg + 1) * 4, :], in_=pv.rearrange("p (a b) -> p a b", a=4), func=AF.Copy)))
            groups.append(g)
        return groups

    def prep_q(self, L, s, j, ones_ap, dsz, split, gcol=0, wk=0, qx=False):
        g1 = self.proj_fm(s, wk, j)
        if qx:
            dsts = [(0, 128, self.QX[:, :])]
            qkey = self.ak("QX")
        elif split:
            dsts = [(0, 64, self.QA[j][0:64, :]), (64, 64, self.QB[j][64:128, :])]
            qkey = self.ak("Q", j)
        else:
            dsts = [(0, 128, self.QA[j][:, :])]
            qkey = self.ak("Q", j)
        g2 = self.norm_chain(PJ, 512, ones_ap, dsz, gcol, [qkey], dsts, j % 2)
        return [g1 + g2]

    def prep_z(self, s, k, j, zi):
        return [self.proj_fm(s, k, j) + self.gate_ops(PJ, zi)]

    def attn_head(self, L, hd, slot, s, zbase):
        op = self.op
        C = self.consts()
        split = (L == 0)
        ones_ap = C["bd"] if split else C["ones"]
        dsz = 64 if split else 128
        kv = self.prep_kv(L, s, ones_ap, dsz, split)
        if L == 0:
            kv = [[op("sp", [], [("Kaug", s)], lambda e: e.dma_start(out=self.KA[s][64:68, :], in_=self.T["c_ka"][hd]), dma=True),
                   op("sp", [], [("Kaug", s)], lambda e: e.dma_start(out=self.KB[s][0:4, :], in_=self.T["c_ka"][hd]), dma=True)]] + kv
        blocks = []
        for j in range(NB):
            qz = self.prep_q(L, s, j, ones_ap, dsz, split) + self.prep_z(s, 3, j, zbase + j % 2)
            units = []
            ntile = 4 * j + 4
            subs = (0, 1) if L == 0 else (0,)
            for i in range(ntile):
                n0 = max(0, i - 4 * j) * 128
                diag = i >= 4 * j
                for sub in subs:
                    if L == 0:
                        if sub == 0:
                            kT = self.KA[s][0:68, i * 128:(i + 1) * 128]
                            qT = self.QA[j][0:68, n0:512]
                        else:
                            kT = self.KB[s][:, i * 128:(i + 1) * 128]
                            qT = self.QB[j][:, n0:512]
                        obank, lbank = (A0, A2) if sub == 0 else (A1, A3)
                        u = self.make_unit(kT, [("K", s, i // 4), ("Kaug", s)], qT, [self.ak("Q", j), self.ak("Qaug")], n0, None, None, [], diag,
                                           self.V[s][:, i, :], [("V", s, i)], obank, lbank, i == 0, i == ntile - 1)
                    else:
                        kT = self.KA[s][:, i * 128:(i + 1) * 128]
                        qT = self.QA[j][:, n0:512]
                        extra = (self.sel[0:76, hd * 128:(hd + 1) * 128], self.Csplit[0:76, j * 512 + n0:(j + 1) * 512], ["sel", ("Csplit",)])
                        bias = self.kb[:, i * 12 + hd:i * 12 + hd + 1]
                        u = self.make_unit(kT, [("K", s, i // 4)], qT, [self.ak("Q", j)], n0, extra, bias, [("kb",)], diag,
                                           self.V[s][:, i, :], [("V", s, i)], A0, A1, i == 0, i == ntile - 1)
                    units.append(u)
            ugroups, _ = self.attn_block(units, None)
            post = []
            yk = self.ak("YT", slot, j)
            Y = self.YT[:, slot, j * 512:(j + 1) * 512]
            zk = self.ak("ZS", zbase + j % 2)
            Z = self.ZS[zbase + j % 2]
            t1, t2, t3 = self.t1, self.t2, self.t3
            k1, k2, k3 = self.ak("t1"), self.ak("t2"), self.ak("t3")
            if L == 1:
                post.append(op("act", [("ps", A1)], [k1], lambda e: e.activation(out=t1, in_=self.bank(A1), func=AF.Ln)))
                post.append(op("act", [k1], [k1], lambda e: e.activation(out=t1, in_=t1, func=AF.Exp, scale=-1.0)))
                post.append(op("dve", [("ps", A0), k1], [k1], lambda e: e.tensor_tensor(out=t1, in0=self.bank(A0), in1=t1, op=ALU.mult)))
                post.append(op("dve", [k1, zk], [yk], lambda e, Y=Y, Z=Z: e.tensor_tensor(out=Y, in0=t1, in1=Z, op=ALU.mult)))
            else:
                post.append(op("act", [("ps", A2)], [k1], lambda e: e.activation(out=t1, in_=self.bank(A2), func=AF.Ln)))
                post.append(op("act", [("ps", A3)], [k2], lambda e: e.activation(out=t2, in_=self.bank(A3), func=AF.Ln)))
                post.append(op("act", [k1], [k1], lambda e: e.activation(out=t1, in_=t1, func=AF.Exp, scale=-1.0)))
                post.append(op("act", [k2], [k2], lambda e: e.activation(out=t2, in_=t2, func=AF.Exp, scale=-1.0)))
                post.append(op("dve", [("ps", A0), k1], [k1], lambda e: e.tensor_tensor(out=t1, in0=self.bank(A0), in1=t1, op=ALU.mult)))
                post.append(op("dve", [("ps", A1), k2], [k2], lambda e: e.tensor_tensor(out=t2, in0=self.bank(A1), in1=t2, op=ALU.mult)))
                post.append(op("dve", [k1, k2, ("col", 5)], [k1], lambda e: e.scalar_tensor_tensor(out=t1, in0=t2, scalar=self.cols[:, 5:6], in1=t1, op0=ALU.mult, op1=ALU.add)))
                sqk, ssek, rk = self.ak("sq", 0), self.ak("sse"), self.ak("rstd")
                sq, sse, rstd = self.sq[0], self.sse, self.rstd
                pb = self.bank(PB)
                post.append(op("act", [k1], [sqk], lambda e: e.activation(out=sq, in_=t1, func=AF.Square)))
                post.append(op("pe", [sqk, "cst"], [("ps", PB)], lambda e: e.matmul(pb, lhsT=C["ones"], rhs=sq, start=True, stop=True)))
                post.append(op("act", [("ps", PB)], [ssek], lambda e: e.activation(out=sse, in_=pb, func=AF.Ln, bias=float(128 * EPS))))
                post.append(op("act", [ssek], [rk], lambda e: e.activation(out=rstd, in_=sse, func=AF.Exp, scale=-0.5)))
                post.append(op("dve", [k1, rk, ("col", 2)], [k1], lambda e: e.scalar_tensor_tensor(out=t1, in0=t1, scalar=self.cols[:, 2:3], in1=rstd, op0=ALU.mult, op1=ALU.mult)))
                post.append(op("dve", [k1, zk], [yk], lambda e, Y=Y, Z=Z: e.tensor_tensor(out=Y, in0=t1, in1=Z, op=ALU.mult)))
            blocks.append((qz, ugroups, post))
        return kv, blocks

    def xattn_head(self, L, hx, slot, s, zi):
        op = self.op
        C = self.consts()
        groups = []
        t3 = self.t3
        k3 = self.ak("t3")
        for j in range(NB):
            g = []
            g += self.prep_q(L, s, j, C["ones"], 128, False, gcol=3, wk=1, qx=True)[0]
            g += self.prep_z(s, 2, j, zi)[0]
            for mt in range(2):
                kT = self.MKT[:, hx, mt * 128:(mt + 1) * 128]
                qT = self.QX[:, :]
                u = self.make_unit(kT, [("MKT", hx)], qT, [self.ak("QX")], 0, None, None, [], False,
                                   self.MV[:, mt, hx * 128:(hx + 1) * 128], [("MV", mt)], PJ, PB, mt == 0, mt == 1, px=True)
                for q_ in u["qk"]:
                    q_.glue = True
                g += u["qk"] + u["exp"] + u["pv"]
            yk = self.ak("YT", slot, j)
            Y = self.YT[:, slot, j * 512:(j + 1) * 512]
            zk = self.ak("ZS", zi)
            Z = self.ZS[zi]
            g.append(op("act", [("ps", PB)], [k3], lambda e: e.activation(out=t3, in_=self.bank(PB), func=AF.Ln)))
            g.append(op("act", [k3], [k3], lambda e: e.activation(out=t3, in_=t3, func=AF.Exp, scale=-1.0)))
            g.append(op("dve", [("ps", PJ), k3], [k3], lambda e: e.tensor_tensor(out=t3, in0=self.bank(PJ), in1=t3, op=ALU.mult)))
            g.append(op("dve", [k3, zk], [yk], lambda e, Y=Y, Z=Z: e.tensor_tensor(out=Y, in0=t3, in1=Z, op=ALU.mult)))
            groups.append(g)
        return groups

    def conv_chunk(self, cc, slot, s, zi, wname):
        op = self.op
        groups = []
        xk = ("xpad",)
        xp = self.xpad
        cb = 6 + cc * 4
        t3 = self.t3
        k3 = self.ak("t3")
        cl = self.cols
        groups.append([op("pool", [], [xk], lambda e: e.memset(xp[:, 0:2], 0.0))])
        for j in range(NB):
            g = []
            g += self.proj_fm(s, 1, j)
            g.append(op("act", [("ps", PJ)], [k3], lambda e: e.activation(out=t3, in_=self.bank(PJ), func=AF.Copy)))
            g += self.proj_fm(s, 2, j, pbank=PB)
            g.append(op("dve", [("ps", PB), k3], [xk], lambda e, j=j: e.tensor_tensor(out=xp[:, 2 + j * 512:2 + (j + 1) * 512], in0=self.bank(PB), in1=t3, op=ALU.mult)))
            groups.append(g)
        groups.append([self.load_w(wname, 5120 + cc * 128, s, 1), self.load_w(wname, 5632 + cc * 128, s, 2)])
        for j in range(NB):
            g = []
            yk = self.ak("YT", slot, j)
            Y = self.YT[:, slot, j * 512:(j + 1) * 512]
            x0 = xp[:, j * 512:j * 512 + 512]
            x1 = xp[:, j * 512 + 1:j * 512 + 513]
            x2 = xp[:, j * 512 + 2:j * 512 + 514]
            g.append(op("dve", [xk, ("col", cb), ("col", cb + 3)], [k3], lambda e, x0=x0: e.tensor_scalar(out=t3, in0=x0, scalar1=cl[:, cb:cb + 1], scalar2=cl[:, cb + 3:cb + 4], op0=ALU.mult, op1=ALU.add)))
            g.append(op("dve", [xk, k3, ("col", cb + 1)], [k3], lambda e, x1=x1: e.scalar_tensor_tensor(out=t3, in0=x1, scalar=cl[:, cb + 1:cb + 2], in1=t3, op0=ALU.mult, op1=ALU.add)))
            g.append(op("dve", [xk, k3, ("col", cb + 2)], [k3], lambda e, x2=x2: e.scalar_tensor_tensor(out=t3, in0=x2, scalar=cl[:, cb + 2:cb + 3], in1=t3, op0=ALU.mult, op1=ALU.add)))
            g += self.proj_fm(s, 1, j)
            g.append(op("dve", [("ps", PJ), k3], [k3], lambda e: e.tensor_tensor(out=t3, in0=self.bank(PJ), in1=t3, op=ALU.mult)))
            g += self.proj_fm(s, 2, j, pbank=PB)
            g += self.gate_ops(PB, zi)
            zk = self.ak("ZS", zi)
            Z = self.ZS[zi]
            g.append(op("dve", [k3, zk], [yk], lambda e, Y=Y, Z=Z: e.tensor_tensor(out=Y, in0=t3, in1=Z, op=ALU.mult)))
            groups.append(g)
        return groups

    def fox_prologue(self):
        op = self.op
        T = self.T
        o = []
        WF = self.WF
        wfk = self.ak("WF")
        o.append(op("pool", [], [wfk], lambda e: e.memset(WF, 0.0)))
        for r0 in (0, 32, 64):
            o.append(op("pool", [wfk], [wfk], lambda e, r0=r0: e.dma_start(out=WF[:, :, r0:r0 + 12], in_=T["o_w_in"][:, 6144:6156].rearrange("(c p) j -> p c j", p=128)), dma=True))
            o.append(op("sp", [], [("col", 22)], lambda e, r0=r0: e.dma_start(out=self.cols[r0:r0 + 12, 22:23], in_=T["o_c_forget_b"].rearrange("(p o) -> p o", o=1)), dma=True))
        o.append(op("dve", [("col", 22)], [("col", 23)], lambda e: e.tensor_scalar(out=self.cols[:, 23:24], in0=self.cols[:, 22:23], scalar1=-1.0, scalar2=None, op0=ALU.mult)))
        f0, f1, f2 = self.fx
        kf = [self.ak("fx", i) for i in range(3)]
        hb, mb = self.fxb
        kh, km = self.ak("fxb", 0), self.ak("fxb", 1)
        nck = self.ak("WMh")
        o.append(op("sp", [], ["sel"], lambda e: e.dma_start(out=self.sel[0:76, :], in_=T["c_sel"]), dma=True))
        ok1 = self.ak("onesf")
        o.append(op("pool", [], [ok1], lambda e: e.memset(self.onesf, 1.0)))
        ncum = self.ncum
        R = 76
        for b in range(NB):
            pf = self.bank(PJ, R)
            for c in range(8):
                o.append(op("pe", [wfk] + [("UT", 4 * b + i) for i in range(4)], [("ps", PJ)],
                            lambda e, c=c, b=b: e.matmul(pf, lhsT=WF[:, c, 0:R], rhs=self.UT[:, c, b * 512:(b + 1) * 512], start=(c == 0), stop=(c == 7))))
            o.append(op("act", [("ps", PJ), ("col", 23)], [kf[0]], lambda e: e.activation(out=f0[0:R], in_=pf, func=AF.Exp, bias=self.cols[0:R, 23:24], scale=-1.0)))
            o.append(op("act", [kf[0]], [kf[0]], lambda e: e.activation(out=f0[0:R], in_=f0[0:R], func=AF.Ln, bias=1.0)))
            init = 0.0 if b == 0 else ncum[0:R, b * 512 - 1:b * 512]
            o.append(op("dve", [kf[0], ok1, nck], [nck], lambda e, b=b, init=init: e.tensor_tensor_scan(out=ncum[0:R, b * 512:(b + 1) * 512], data0=self.onesf[0:R, :], data1=f0[0:R], initial=init, op0=ALU.mult, op1=ALU.add)))
            nb = ncum[0:R, b * 512:(b + 1) * 512]
            o.append(op("dve", [nck], [kf[1]], lambda e, nb=nb: e.tensor_scalar(out=f1[0:R], in0=nb, scalar1=-1.0, scalar2=None, op0=ALU.mult)))
            o.append(op("dve", [kf[1]], [kh], lambda e: e.tensor_copy(out=hb[0:R], in_=f1[0:R])))
            o.append(op("dve", [kf[1], kh], [kf[2]], lambda e: e.tensor_tensor(out=f2[0:R], in0=f1[0:R], in1=hb[0:R], op=ALU.subtract)))
            o.append(op("dve", [kf[2]], [km], lambda e: e.tensor_copy(out=mb[0:R], in_=f2[0:R])))
            o.append(op("dve", [kf[2], km], [kf[2]], lambda e: e.tensor_tensor(out=f2[0:R], in0=f2[0:R], in1=mb[0:R], op=ALU.subtract)))
            cs = self.Csplit
            o.append(op("dve", [kh], [("Csplit",)], lambda e, b=b: e.tensor_copy(out=cs[0:32, b * 512:(b + 1) * 512], in_=hb[0:32])))
            o.append(op("dve", [km], [("Csplit",)], lambda e, b=b: e.tensor_copy(out=cs[32:64, b * 512:(b + 1) * 512], in_=mb[32:64])))
            o.append(op("dve", [kf[2]], [("Csplit",)], lambda e, b=b: e.tensor_copy(out=cs[64:R, b * 512:(b + 1) * 512], in_=f2[64:R])))
        pt = self.ps[:, A0, 0:192]
        for t in range(NT):
            o.append(op("pe", [nck, "identf"], [("ps", A0)], lambda e, t=t: e.transpose(out=self.ps[:, A0, t * 12:(t + 1) * 12], in_=ncum[0:12, t * 128:(t + 1) * 128], identity=self.identf[0:12, 0:12])))
        o.append(op("dve", [("ps", A0)], [("kb",)], lambda e: e.tensor_copy(out=self.kb[:, :], in_=pt)))
        return o

    def main_phase(self, L):
        op = self.op
        T = self.T
        o = []
        wname = "e_w_in" if L == 0 else "o_w_in"
        if L == 0:
            A = [("attn", h, h) for h in range(8)]
            Cv = [("conv", c, 8 + c) for c in range(4)]
            X = [("xattn", x, 12 + x) for x in range(4)]
            chunks = [A[0], Cv[0], A[1], Cv[1], A[2], Cv[2], A[3], Cv[3], A[4], X[0], A[5], X[1], A[6], X[2], A[7], X[3]]
        else:
            A = [("attn", h, h) for h in range(12)]
            X = [("xattn", x, 12 + x) for x in range(4)]
            chunks = [A[0], A[1], A[2], X[0], A[3], A[4], A[5], X[1], A[6], A[7], A[8], X[2], A[9], A[10], A[11], X[3]]
        nch = len(chunks)
        hidx = []
        hi = -1
        for (kind, idx, rb) in chunks:
            if kind == "attn":
                hi += 1
            hidx.append(hi)

        def wloads(ci):
            kind, idx, _ = chunks[ci]
            s = hidx[ci] % 2
            if kind == "attn":
                if L == 0:
                    offs = [idx * 128, 1024 + idx * 128, 2048 + idx * 128, 3072 + idx * 128]
                else:
                    offs = [idx * 128, 1536 + idx * 128, 3072 + idx * 128, 4608 + idx * 128]
                return [self.load_w(wname, offs[k], s, k) for k in range(4)]
            if kind == "conv":
                return [self.load_w(wname, 4608 + idx * 128, s, 1), self.load_w(wname, 4096 + idx * 128, s, 2)]
            if L == 0:
                return [self.load_w(wname, 6144 + idx * 128, s, 1), self.load_w(wname, 6656 + idx * 128, s, 2)]
            return [self.load_w(wname, 6156 + idx * 128, s, 1), self.load_w(wname, 6668 + idx * 128, s, 2)]

        if L == 0:
            for j in range(NB):
                o.append(op("pool", [], [self.ak("Qaug"), self.ak("Q", j)], lambda e, j=j: e.memset(self.QB[j][0:64, :], 0.0)))
                o.append(op("sp", [], [self.ak("Qaug")], lambda e, j=j: e.dma_start(out=self.QA[j][64:68, :], in_=T["c_qa"][:, j * 512:(j + 1) * 512]), dma=True))
                o.append(op("sp", [self.ak("Qaug")], [self.ak("Qaug")], lambda e, j=j: e.dma_start(out=self.QB[j][0:4, :], in_=T["c_qa"][:, j * 512:(j + 1) * 512]), dma=True))

        def flat(groups):
            r = []
            for g in groups:
                r.extend(g)
            return r

        gen = []
        for ci, (kind, idx, rb) in enumerate(chunks):
            s = hidx[ci] % 2
            slot = ci % 4
            zl = 2 * ((hidx[ci] + 1) % 2) + 1
            if kind == "conv":
                gen.append(("light", self.conv_chunk(idx, slot, s, zl, wname)))
            elif kind == "xattn":
                gen.append(("light", self.xattn_head(L, idx, slot, s, zl)))
            else:
                kv, blocks = self.attn_head(L, idx, slot, s, 2 * (hidx[ci] % 2))
                gen.append(("heavy", kv, blocks))

        if self.ydbg is not None:
            li = self.layers.index(L)
            for ci, (kind, idx, rb) in enumerate(chunks):
                slot = ci % 4
                dop = op("sp", [self.ak("YT", slot, j) for j in range(NB)], [("ydbg", ci)],
                         lambda e, slot=slot, rb=rb: e.dma_start(out=self.ydbg[li, rb * 128:(rb + 1) * 128, :], in_=self.YT[:, slot, :]), dma=True)
                if gen[ci][0] == "light":
                    gen[ci][1][-1].append(dop)
                else:
                    gen[ci][2][NB - 1][2].append(dop)
        fill_banks = [PJ, PB]
        o += wloads(0)
        assert gen[0][0] == "heavy"
        o += flat(gen[0][1]) + flat(gen[0][2][0][0])
        pending_out = None
        ci = 0
        while ci < nch:
            assert gen[ci][0] == "heavy"
            blocks = gen[ci][2]
            ahead = []
            cj = ci + 1
            loads = []
            import os as _os
            dbg = _os.environ.get("MKDBG", "")
            seq_l, seq_k = [], []
            while cj < nch and gen[cj][0] == "light":
                loads += wloads(cj)
                if "L" in dbg:
                    seq_l += gen[cj][1]
                else:
                    ahead += gen[cj][1]
                cj += 1
            tail = []
            if cj < nch:
                loads += wloads(cj)
                if "K" in dbg:
                    seq_k += gen[cj][1]
                else:
                    ahead += gen[cj][1]
                tail = gen[cj][2][0][0]
            o += loads
            import os as _os
            dbg = _os.environ.get("MKDBG", "")
            post_seq = seq_l + seq_k
            if "T" in dbg:
                post_seq += tail
                tail = []
            if "A" in dbg:
                post_seq = ahead + post_seq
                ahead = []
            tot = sum(len(g) for g in ahead) + sum(len(g) for g in tail)
            parts = [[], [], []]
            acc = 0
            for g in ahead + tail:
                k = min(2, (acc * 3) // max(tot, 1))
                parts[k].extend(g)
                acc += len(g)
            for j in range(NB):
                qz, ug, post = blocks[j]
                fl = []
                if j == 0 and pending_out is not None:
                    fl += pending_out
                    pending_out = None
                if j + 1 < NB:
                    fl += flat(blocks[j + 1][0])
                if j >= 1:
                    fl += parts[j - 1]
                nu = len(ug)
                slices = self.stage_slices(fl)
                ns = len(slices)
                si_ = 0
                for u in range(nu):
                    o += ug[u]
                    tgt = ((u + 1) * ns) // nu
                    while si_ < tgt:
                        o += slices[si_]
                        si_ += 1
                o += post
            o += flat(post_seq)
            for ck in range(ci, cj):
                if ck % 4 == 3:
                    g = ck // 4
                    rbs = [chunks[g * 4 + c][2] for c in range(4)]
                    last = not (cj < nch)
                    ld, oo = self.outproj(L, rbs, [A0, A1, A2, A3] if last else fill_banks)
                    o += ld
                    if not last and ck == cj - 1:
                        pending_out = oo
                    else:
                        o += oo
            ci = cj
        assert pending_out is None
        return o

    def store_out(self):
        o = []
        T = self.T
        for t in range(NT):
            o.append(self.op("sp", self.hk(t), [("out", t)], lambda e, t=t: e.dma_start(out=T["out"][t * 128:(t + 1) * 128, :], in_=self.h[:, t, :]), dma=True))
        o.append(Op("sp", None, reads=[("out", t) for t in range(NT)]))
        return o


_CACHE = {}


def _get_prog(layers):
    key = tuple(layers)
    if key not in _CACHE:
        g = Gen(layers)
        nc = g.build()
        _CACHE[key] = (nc, g.in_names)
    return _CACHE[key]


def _run(layers, xin, mem, params, consts):
    nc, in_names = _get_prog(layers)
    in_maps = []
    for b in range(8):
        m = {}
        for n in in_names:
            if n == "x":
                m[n] = np.ascontiguousarray(xin[b])
            elif n == "mem":
                m[n] = np.ascontiguousarray(mem[b])
            elif n in consts:
                m[n] = consts[n]
            else:
                m[n] = params[n]
        in_maps.append(m)
    res = run_bass_kernel_spmd(nc, in_maps, core_ids=list(range(8)))
    return np.stack([np.asarray(r["out"]) for r in res.results], 0)


FUSED = True


def kernel(**inputs):
    x = np.asarray(inputs["x"], np.float32)
    mem = np.asarray(inputs["mem"], np.float32)
    params = {}
    for n in L0_NAMES + L1_NAMES:
        a = np.asarray(inputs[n], np.float32)
        params[n] = np.ascontiguousarray(a[0])
    consts = _consts()
    if FUSED:
        return _run((0, 1), x, mem, params, consts).astype(np.float32)
    h1 = _run((0,), x, mem, params, consts)
    return _run((1,), h1, mem, params, consts).astype(np.float32)
```

```python
import contextlib
import math
import numpy as np
import ml_dtypes
import concourse.bass as bass
import concourse.mybir as mybir
from concourse.bass_utils import run_bass_kernel_spmd

F32 = mybir.dt.float32
BF16 = mybir.dt.bfloat16
AF = mybir.ActivationFunctionType
ALU = mybir.AluOpType
AX = mybir.AxisListType

ENGS = ("pe", "act", "dve", "pool", "sp")
EPS = 1e-6
S_LEN = 2048
D = 1024
NT = 16
NB = 4


class Op:
    __slots__ = ("eng", "fn", "reads", "writes", "dma", "idx", "sig", "deps", "dsem", "dtarget", "dprev", "glue")

    def __init__(self, eng, fn, reads=(), writes=(), dma=False):
        self.eng = eng
        self.fn = fn
        self.reads = tuple(reads)
        self.writes = tuple(writes)
        self.dma = dma
        self.sig = None
        self.deps = ()
        self.dsem = None
        self.glue = False


class Sched:
    NDMA_SEMS = 8

    def __init__(self, same_engine_sync=True):
        self.ops = []
        self.same_engine_sync = same_engine_sync

    def add(self, ops):
        if isinstance(ops, Op):
            self.ops.append(ops)
        else:
            for o in ops:
                self.add(o)

    def plan(self):
        last_w = {}
        readers = {}
        ops = self.ops
        for i, op in enumerate(ops):
            op.idx = i
            deps = set()
            for r in op.reads:
                w = last_w.get(r)
                if w is not None:
                    deps.add(w)
            for wr in op.writes:
                w = last_w.get(wr)
                if w is not None:
                    deps.add(w)
                rs = readers.get(wr)
                if rs:
                    deps.update(rs)
            deps.discard(i)
            for r in op.reads:
                readers.setdefault(r, []).append(i)
            for wr in op.writes:
                last_w[wr] = i
                readers[wr] = []
            latest = {}
            dmadeps = []
            for dix in deps:
                d = ops[dix]
                if d.dma:
                    dmadeps.append(dix)
                else:
                    if d.eng == op.eng and not op.dma:
                        if op.eng == "pe" or not self.same_engine_sync:
                            continue
                    if d.eng not in latest or latest[d.eng] < dix:
                        latest[d.eng] = dix
            op.deps = tuple(sorted(list(latest.values()) + dmadeps))
        need = set()
        for op in ops:
            for dix in op.deps:
                if not ops[dix].dma:
                    need.add(dix)
        cnt = {e: 0 for e in ENGS}
        for op in ops:
            if op.dma:
                continue
            if op.idx in need:
                cnt[op.eng] += 1
                op.sig = cnt[op.eng]
        dcount = {e: 0 for e in ENGS}
        for op in ops:
            if op.dma:
                n = dcount[op.eng]
                dcount[op.eng] += 1
                op.dsem = (op.eng, n % self.NDMA_SEMS)
                op.dtarget = 16 * (n // self.NDMA_SEMS + 1)
                op.dprev = 16 * (n // self.NDMA_SEMS)
        self.sigcount = cnt
        self.dmacount = dcount

    def emit(self, block, esems, dsems):
        handles = {"pe": block.tensor, "act": block.scalar, "dve": block.vector, "pool": block.gpsimd,
                   "sp": block.sync}
        ops = self.ops
        for eng in ENGS:
            mine = [op for op in ops if op.eng == eng]

            def body(e, mine=mine, eng=eng):
                waited = {}

                def wait(key, sem, val):
                    if waited.get(key, 0) >= val:
                        return
                    waited[key] = val
                    e.wait_ge(sem, val)

                for op in mine:
                    for dix in op.deps:
                        d = ops[dix]
                        if d.dma:
                            wait(d.dsem, dsems[d.dsem], d.dtarget)
                        else:
                            wait(d.eng, esems[d.eng], d.sig)
                    if op.dma and op.dprev > 0:
                        wait(op.dsem, dsems[op.dsem], op.dprev)
                    if op.fn is None:
                        continue
                    ins = op.fn(e)
                    if op.dma:
                        ins.then_inc(dsems[op.dsem], 16)
                    elif op.sig is not None:
                        ins.then_inc(esems[eng], 1)

            handles[eng](body)


def interleave(units, fillers):
    out = []
    nu, nf = len(units), len(fillers)
    if nu == 0:
        for f in fillers:
            out.extend(f)
        return out
    fi = 0
    for u in range(nu):
        out.extend(units[u])
        tgt = ((u + 1) * nf) // nu
        while fi < tgt:
            out.extend(fillers[fi])
            fi += 1
    return out


def _consts():
    bf = ml_dtypes.bfloat16
    tri = (np.arange(128)[None, :] >= np.arange(128)[:, None]).astype(np.float32)
    bd = np.zeros((128, 128), np.float32)
    bd[:64, :64] = 1.0
    bd[64:, 64:] = 1.0
    ident = np.eye(128, dtype=np.float32)
    ones = np.ones((128, 128), np.float32)
    cst = np.concatenate([tri, bd, ident, ones], axis=1).astype(bf)
    pos = np.arange(S_LEN)
    qa = np.stack([(pos % 128).astype(np.float32), (pos // 128).astype(np.float32),
                   np.ones(S_LEN, np.float32), np.ones(S_LEN, np.float32)], 0)
    ka = np.zeros((8, 4, S_LEN), np.float32)
    for h in range(8):
        sl = 2.0 ** (-(h + 1))
        ka[h, 0] = -sl
        ka[h, 1] = -sl * 128.0
        ka[h, 2] = sl * (pos % 128)
        ka[h, 3] = sl * 128.0 * (pos // 128)
    assert np.array_equal(ka.astype(bf).astype(np.float32), ka)
    assert np.array_equal(qa.astype(bf).astype(np.float32), qa)
    sel = np.zeros((76, 12 * 128), np.float32)
    for h in range(12):
        for r in (h, 32 + h, 64 + h):
            sel[r, h * 128:(h + 1) * 128] = 1.0
    identf = np.eye(128, dtype=np.float32)
    return {"c_cst": cst, "c_qa": qa.astype(bf), "c_ka": ka.astype(bf), "c_sel": sel.astype(bf),
            "c_identf": identf}


L0_NAMES = ["e_norm_g", "e_w_in", "e_w_out", "e_a_q_norm_g", "e_a_k_norm_g", "e_a_lambda", "e_a_out_norm_g",
            "e_b_conv_w", "e_b_conv_b", "e_x_q_norm_g", "e_x_k_norm_g", "e_mem_norm_g", "e_w_mem_kv"]
L1_NAMES = ["o_norm_g", "o_w_in", "o_w_out", "o_c_q_norm_g", "o_c_k_norm_g", "o_c_forget_b", "o_x_q_norm_g",
            "o_x_k_norm_g", "o_mem_norm_g", "o_w_mem_kv"]
SHAPES = {
    "e_norm_g": [1024], "e_w_in": [1024, 7168], "e_w_out": [2048, 1024], "e_a_q_norm_g": [64],
    "e_a_k_norm_g": [64], "e_a_lambda": [4, 64], "e_a_out_norm_g": [128], "e_b_conv_w": [3, 512],
    "e_b_conv_b": [512], "e_x_q_norm_g": [128], "e_x_k_norm_g": [128], "e_mem_norm_g": [1024],
    "e_w_mem_kv": [1024, 1024],
    "o_norm_g": [1024], "o_w_in": [1024, 7180], "o_w_out": [2048, 1024], "o_c_q_norm_g": [128],
    "o_c_k_norm_g": [128], "o_c_forget_b": [12], "o_x_q_norm_g": [128], "o_x_k_norm_g": [128],
    "o_mem_norm_g": [1024], "o_w_mem_kv": [1024, 1024],
}

PJ, PB, S0, S1, A0, A1, A2, A3 = range(8)


class Gen:
    def __init__(self, layers, interleave_on=True):
        self.layers = tuple(layers)
        self.il = interleave_on
        self.nc = bass.Bass("TRN2", target_bir_lowering=False)
        self.S = Sched()
        self.stack = contextlib.ExitStack()
        self.sb_bytes = 0
        self.arena_keys = set()
        self.wctr = 0
        self.pctr = 0
        self.sctr = 0

    def sb(self, name, shape, dt):
        n = 1
        for s in shape[1:]:
            n *= s
        self.sb_bytes += n * (4 if dt == F32 else 2)
        return self.stack.enter_context(self.nc.sbuf_tensor(name, shape, dt))

    def carve(self, nbytes):
        off = self.ar_off
        self.ar_off += nbytes
        assert self.ar_off <= self.AR_BYTES, (self.ar_off, self.AR_BYTES)
        return off

    def arv(self, off, shape, dt, base=None):
        n = 1
        for s in shape:
            n *= s
        nb = n * (4 if dt == F32 else 2)
        v = (self.AR if base is None else base)[:, off // 2:(off + nb) // 2]
        if dt == F32:
            v = v.bitcast(F32)
        if len(shape) == 2:
            v = v.rearrange("p (a b) -> p a b", a=shape[0])
        elif len(shape) == 3:
            v = v.rearrange("p (a b c) -> p a b c", a=shape[0], b=shape[1])
        return v

    def ak(self, *key):
        self.arena_keys.add(key)
        return key

    def op(self, eng, reads, writes, fn, dma=False):
        return Op(eng, fn, reads, writes, dma)

    def fence(self):
        keys = sorted(self.arena_keys, key=repr)
        d = self.dummy
        return [Op("pool", lambda e: e.memset(d[:], 0.0), reads=(), writes=keys)]

    def build(self):
        nc = self.nc
        T = {}
        T["x"] = nc.dram_tensor("x", [S_LEN, D], F32, kind="ExternalInput").ap()
        T["mem"] = nc.dram_tensor("mem", [256, D], F32, kind="ExternalInput").ap()
        names = []
        if 0 in self.layers:
            names += L0_NAMES
        if 1 in self.layers:
            names += L1_NAMES
        for n in names:
            T[n] = nc.dram_tensor(n, SHAPES[n], F32, kind="ExternalInput").ap()
        T["c_cst"] = nc.dram_tensor("c_cst", [128, 512], BF16, kind="ExternalInput").ap()
        T["c_qa"] = nc.dram_tensor("c_qa", [4, S_LEN], BF16, kind="ExternalInput").ap()
        T["c_ka"] = nc.dram_tensor("c_ka", [8, 4, S_LEN], BF16, kind="ExternalInput").ap()
        T["c_sel"] = nc.dram_tensor("c_sel", [76, 1536], BF16, kind="ExternalInput").ap()
        T["c_identf"] = nc.dram_tensor("c_identf", [128, 128], F32, kind="ExternalInput").ap()
        T["out"] = nc.dram_tensor("out", [S_LEN, D], F32, kind="ExternalOutput").ap()
        import os as _os
        self.ydbg = None
        if _os.environ.get("MKYDBG"):
            self.ydbg = nc.dram_tensor("ydbg", [len(self.layers), 2048, S_LEN], BF16, kind="ExternalOutput").ap()
        self.T = T
        self.in_names = ["x", "mem"] + names + ["c_cst", "c_qa", "c_ka", "c_sel", "c_identf"]

        st = self.stack
        sb = self.sb
        self.h = sb("h", [128, NT, D], F32)
        self.UT = sb("UT", [128, 8, S_LEN], BF16)
        self.KA = [sb(f"KA{s}", [128, S_LEN], BF16) for s in range(2)]
        self.V = [sb(f"V{s}", [128, NT, 128], BF16) for s in range(2)]
        self.W = [[sb(f"W{s}_{k}", [128, 8, 128], BF16) for k in range(4)] for s in range(2)]
        self.WO = sb("WO", [128, 4, D], BF16)
        self.cols = sb("cols", [128, 64], F32)
        self.cst = sb("cst", [128, 512], BF16)
        self.MKT = sb("MKT", [128, 4, 256], BF16)
        self.MV = sb("MV", [128, 2, 512], BF16)
        self.AR2 = sb("AR2", [128, 16448 // 2], BF16)
        self.KB = [self.arv(4096 * i, [S_LEN], BF16, base=self.AR2) for i in range(2)]
        self.xpad = self.arv(8192, [2064], F32, base=self.AR2)
        self.Csplit = self.arv(0, [S_LEN], BF16, base=self.AR2)
        self.sel = self.arv(4096, [1536], BF16, base=self.AR2)
        self.kb = self.arv(7168, [192], F32, base=self.AR2)
        for s_ in range(2):
            self.arena_keys.add(("Kaug", s_))
            for b_ in range(NB):
                self.arena_keys.add(("K", s_, b_))
        self.arena_keys.update([("Csplit",), "sel", ("kb",), ("xpad",)])
        self.identf = sb("identf", [128, 128], F32)
        self.dummy = sb("dmy_t", [128, 2], F32)
        self.small = sb("small", [128, 64], F32)
        self.AR_BYTES = 49152
        self.AR = sb("AR", [128, self.AR_BYTES // 2], BF16)
        assert self.sb_bytes <= 212000, self.sb_bytes
        self.ar_off = 0
        c = self.carve
        self.YT = self.arv(c(16384), [4, S_LEN], BF16)
        self.QA = [self.arv(c(1024), [512], BF16) for _ in range(4)]
        self.QB = [self.arv(c(1024), [512], BF16) for _ in range(4)]
        self.ZS = [self.arv(c(1024), [512], BF16) for _ in range(4)]
        self.t1 = self.arv(c(2048), [512], F32)
        self.t2 = self.arv(c(2048), [512], F32)
        self.P = [self.arv(c(1024), [512], BF16) for _ in range(4)]
        self.t3 = self.arv(c(2048), [512], F32)
        self.sq = [self.arv(c(1024), [512], BF16) for _ in range(2)]
        self.sse = self.arv(c(2048), [512], F32)
        self.rstd = self.arv(c(2048), [512], F32)
        self.QX = self.arv(c(1024), [512], BF16)
        self.PX = self.arv(c(1024), [512], BF16)
        self.OC = [self.arv(c(1024), [512], BF16) for _ in range(2)]
        assert self.ar_off <= self.AR_BYTES
        self.ar_off = 0
        self.gbc = self.arv(c(4096), [D], F32)
        self.Usc = [self.arv(c(2048), [D], BF16) for _ in range(2)]
        woff = self.ar_off
        self.WMh = self.arv(c(8192), [8, 512], BF16)
        self.ncum = self.arv(woff, [S_LEN], F32)
        self.memf = self.arv(c(4096), [D], F32)
        self.memUT = self.arv(c(4096), [8, 256], BF16)
        foff = self.ar_off
        self.fx = [self.arv(c(2048), [512], F32) for _ in range(3)]
        self.junk = self.arv(foff, [D], BF16)
        self.fxb = [self.arv(c(1024), [512], BF16) for _ in range(2)]
        self.WF = self.arv(c(2048), [8, 128], BF16)
        self.onesf = self.arv(c(2048), [512], F32)
        assert self.ar_off <= 38912, self.ar_off

        self.ps = st.enter_context(nc.psum_tensor("ps", [128, 8, 512], F32))
        self.esems = {e: st.enter_context(nc.semaphore(f"s_{e}")) for e in ENGS}
        self.dsems = {(e, k): st.enter_context(nc.semaphore(f"d_{e}{k}"))
                      for e in ("sp", "pool") for k in range(Sched.NDMA_SEMS)}

        S = self.S
        segs = [self.setup_ops()]
        for li, L in enumerate(self.layers):
            segs.append("fence")
            segs.append(self.prologue(L))
            segs.append("fence")
            segs.append(self.main_phase(L))
        segs.append(self.store_out())
        for sg in segs:
            S.add(self.fence() if isinstance(sg, str) else sg)
        S.plan()
        with nc.Block() as block:
            S.emit(block, self.esems, self.dsems)
        self.stack.close()
        return nc

    def bank(self, b, rows=128, c0=0, c1=512):
        return self.ps[0:rows, b, c0:c1]

    def col(self, j, rows=128, r0=0):
        return self.cols[r0:r0 + rows, j:j + 1]

    def hk(self, t):
        return [("h", t, 0), ("h", t, 1)]

    def setup_ops(self):
        T = self.T
        o = []
        op = self.op
        h = self.h
        for t in range(NT):
            o.append(op("sp", [], self.hk(t), lambda e, t=t: e.dma_start(out=h[:, t, :], in_=T["x"][t * 128:(t + 1) * 128, :]), dma=True))
        o.append(op("sp", [], ["cst"], lambda e: e.dma_start(out=self.cst[:], in_=T["c_cst"]), dma=True))
        o.append(op("sp", [], ["identf"], lambda e: e.dma_start(out=self.identf[:], in_=T["c_identf"]), dma=True))
        o.append(op("pool", [], ["dummy"], lambda e: e.memset(self.dummy[:], 0.0)))
        o.append(op("pool", [], [("col", j) for j in range(64)], lambda e: e.memset(self.cols[:], 0.0)))
        if 0 in self.layers:
            for s in range(2):
                o.append(op("pool", [], [("K", s, b) for b in range(NB)] + [("Kaug", s)], lambda e, s=s: e.memset(self.KB[s][0:64, :], 0.0)))
        return o

    def consts(self):
        cst = self.cst
        return {"tri": cst[:, 0:128], "bd": cst[:, 128:256], "ident": cst[:, 256:384], "ones": cst[:, 384:512]}

    def small_rstd(self, src_key, src_ap, dst_key, dst_ap, n, scale, eps):
        o = []
        tmpk = ("small_tmp",)
        tmp = self.small[:, 32:32 + n]
        o.append(self.op("act", [src_key], [tmpk], lambda e: e.activation(out=tmp, in_=src_ap, func=AF.Ln, bias=float(eps), scale=float(scale))))
        o.append(self.op("act", [tmpk], [dst_key], lambda e: e.activation(out=dst_ap, in_=tmp, func=AF.Exp, scale=-0.5)))
        return o

    def norm_rows_to_T(self, src_key_list, src_ap, rstd_col_key, rstd_col, ui, dst_writes, dst_ap_fn, tb):
        o = []
        C = self.consts()
        uk = self.ak("Usc", ui)
        U = self.Usc[ui]
        o.append(self.op("dve", list(src_key_list) + [rstd_col_key, self.ak("gbc")], [uk],
                         lambda e: e.scalar_tensor_tensor(out=U, in0=src_ap, scalar=rstd_col, in1=self.gbc, op0=ALU.mult, op1=ALU.mult)))
        psT = self.ps[:, tb, :].bitcast(BF16)
        for c in range(8):
            o.append(self.op("pe", [uk, "cst"], [("ps", tb)],
                             lambda e, c=c: e.transpose(out=psT[:, c * 128:(c + 1) * 128], in_=U[:, c * 128:(c + 1) * 128], identity=C["ident"])))
        o.append(self.op("act", [("ps", tb)], dst_writes,
                         lambda e: e.activation(out=dst_ap_fn(), in_=psT.rearrange("p (c j) -> p c j", c=8), func=AF.Copy)))
        return o

    def prologue(self, L):
        T = self.T
        op = self.op
        pre = "e_" if L == 0 else "o_"
        o = []
        C = self.consts()
        h = self.h
        cols = self.cols
        def vec_col(name, j, n=128, r0=0, src_off=0):
            src = T[name]
            ap = src[src_off:src_off + n].rearrange("(p o) -> p o", o=1)
            return op("sp", [], [("col", j)], lambda e: e.dma_start(out=cols[r0:r0 + n, j:j + 1], in_=ap), dma=True)

        def scale_col(j, f):
            return op("dve", [("col", j)], [("col", j)], lambda e: e.tensor_scalar(out=cols[:, j:j + 1], in0=cols[:, j:j + 1], scalar1=float(f), scalar2=None, op0=ALU.mult))

        if L == 0:
            for r0 in (0, 64):
                o.append(vec_col("e_a_q_norm_g", 0, 64, r0))
                o.append(vec_col("e_a_k_norm_g", 1, 64, r0))
            o.append(scale_col(1, 8.0))
            o.append(vec_col("e_a_out_norm_g", 2))
            lam_init = 0.8 - 0.6 * math.exp(-0.3 * 0)
            o.append(scale_col(2, math.sqrt(128.0) * (1.0 - lam_init)))
            lpb = self.fx[0][:, 0:256]
            lpk = self.ak("fx", 0)
            o.append(op("sp", [], [lpk], lambda e: e.dma_start(out=lpb, in_=T["e_a_lambda"].rearrange("a b -> (a b)").rearrange("(o n) -> o n", o=1).partition_broadcast(128)), dma=True))
            prod = self.fx[1][:, 0:128]
            pk = self.ak("fx", 1)
            lp4 = lpb.rearrange("p (a b) -> p a b", a=4)
            for q in range(2):
                o.append(op("dve", [lpk], [pk], lambda e, q=q: e.tensor_tensor(out=prod[:, q * 64:(q + 1) * 64], in0=lp4[:, 2 * q, :], in1=lp4[:, 2 * q + 1, :], op=ALU.mult)))
            sm = self.small
            o.append(op("dve", [pk], [("sm", 0)], lambda e: e.reduce_sum(out=sm[:, 0:2], in_=prod.rearrange("p (a b) -> p a b", a=2), axis=AX.X)))
            o.append(op("act", [("sm", 0)], [("sm", 1)], lambda e: e.activation(out=sm[:, 2:4], in_=sm[:, 0:2], func=AF.Exp)))
            o.append(op("dve", [("sm", 1)], [("sm", 2)], lambda e: e.tensor_tensor(out=sm[:, 4:5], in0=sm[:, 2:3], in1=sm[:, 3:4], op=ALU.subtract)))
            o.append(op("dve", [("sm", 2)], [("col", 5)], lambda e: e.tensor_scalar(out=cols[:, 5:6], in0=sm[:, 4:5], scalar1=lam_init, scalar2=-1.0, op0=ALU.add, op1=ALU.mult)))
            for cc in range(4):
                for k in range(3):
                    o.append(op("sp", [], [("col", 6 + cc * 4 + k)], lambda e, cc=cc, k=k: e.dma_start(out=cols[:, 6 + cc * 4 + k:7 + cc * 4 + k], in_=T["e_b_conv_w"][k, cc * 128:(cc + 1) * 128].rearrange("(p o) -> p o", o=1)), dma=True))
                o.append(op("sp", [], [("col", 6 + cc * 4 + 3)], lambda e, cc=cc: e.dma_start(out=cols[:, 9 + cc * 4:10 + cc * 4], in_=T["e_b_conv_b"][cc * 128:(cc + 1) * 128].rearrange("(p o) -> p o", o=1)), dma=True))
        else:
            o.append(vec_col("o_c_q_norm_g", 0))
            o.append(vec_col("o_c_k_norm_g", 1))
            o.append(scale_col(1, math.sqrt(128.0)))
        o.append(vec_col(pre + "x_q_norm_g", 3))
        o.append(vec_col(pre + "x_k_norm_g", 4))
        o.append(scale_col(4, math.sqrt(128.0)))

        gk = self.ak("gbc")
        o.append(op("sp", [], [gk], lambda e: e.dma_start(out=self.gbc, in_=T[pre + "mem_norm_g"].rearrange("(o n) -> o n", o=1).partition_broadcast(128)), dma=True))
        mfk = self.ak("memf")
        mutk = self.ak("memUT")
        sm = self.small
        for mt in range(2):
            o.append(op("sp", [], [mfk], lambda e, mt=mt: e.dma_start(out=self.memf, in_=T["mem"][mt * 128:(mt + 1) * 128, :]), dma=True))
            jk = self.ak("fx", 0)
            o.append(op("act", [mfk], [jk, ("sm", 10)], lambda e: e.activation(out=self.junk, in_=self.memf, func=AF.Square, accum_out=sm[:, 10:11])))
            o += self.small_rstd(("sm", 10), sm[:, 10:11], ("sm", 11), sm[:, 11:12], 1, 1.0 / D, EPS)
            tb = A0 + mt
            o += self.norm_rows_to_T([mfk], self.memf, ("sm", 11), sm[:, 11:12], mt, [mutk],
                                     lambda mt=mt: self.memUT[:, :, mt * 128:(mt + 1) * 128], tb)
        wmk = self.ak("WMh")
        wsrc = T[pre + "w_mem_kv"]
        o.append(op("pool", [], [wmk], lambda e: e.dma_start(out=self.WMh, in_=wsrc[:, 0:512].rearrange("(c p) j -> p c j", p=128)), dma=True))
        for hx in range(4):
            pj = self.bank(PJ, 128, 0, 256)
            for c in range(8):
                o.append(op("pe", [wmk, mutk], [("ps", PJ)], lambda e, c=c, hx=hx: e.matmul(pj, lhsT=self.WMh[:, c, hx * 128:(hx + 1) * 128], rhs=self.memUT[:, c, :], start=(c == 0), stop=(c == 7))))
            o += self.norm_chain(PJ, 256, C["ones"], 128, 4, [("MKT", hx)],
                                 [(0, 128, self.MKT[:, hx, :])], 0)
        o.append(op("pool", [], [wmk], lambda e: e.dma_start(out=self.WMh, in_=wsrc[:, 512:1024].rearrange("(c p) j -> p c j", p=128)), dma=True))
        for mt in range(2):
            pv = self.bank(PJ)
            for c in range(8):
                o.append(op("pe", [wmk, mutk], [("ps", PJ)], lambda e, c=c, mt=mt: e.matmul(pv, lhsT=self.memUT[:, c, mt * 128:(mt + 1) * 128], rhs=self.WMh[:, c, :], start=(c == 0), stop=(c == 7))))
            o.append(op("act", [("ps", PJ)], [("MV", mt)], lambda e, mt=mt: e.activation(out=self.MV[:, mt, :], in_=pv, func=AF.Copy)))

        o.append(op("sp", [], [gk], lambda e: e.dma_start(out=self.gbc, in_=T[pre + "norm_g"].rearrange("(o n) -> o n", o=1).partition_broadcast(128)), dma=True))
        jk = self.ak("fx", 0)
        for t in range(NT):
            o.append(op("act", self.hk(t), [jk, ("hss", t)], lambda e, t=t: e.activation(out=self.junk, in_=h[:, t, :], func=AF.Square, accum_out=sm[:, 12 + t:13 + t])))
        o.append(op("act", [("hss", t) for t in range(NT)], [("small_tmp",)], lambda e: e.activation(out=sm[:, 32:48], in_=sm[:, 12:28], func=AF.Ln, bias=float(EPS), scale=1.0 / D)))
        o.append(op("act", [("small_tmp",)], [("hrstd",)], lambda e: e.activation(out=sm[:, 48:64], in_=sm[:, 32:48], func=AF.Exp, scale=-0.5)))
        for t in range(NT):
            o += self.norm_rows_to_T(self.hk(t), h[:, t, :], ("hrstd",), sm[:, 48 + t:49 + t], t % 2, [("UT", t)],
                                     lambda t=t: self.UT[:, :, t * 128:(t + 1) * 128], A0 + (t % 4))
        if L == 1:
            o += self.fox_prologue()
        return o

    def norm_chain(self, pbank, n, ones_ap, dsz, gcol, dst_writes, dsts, si):
        op = self.op
        o = []
        pj = self.bank(pbank, 128, 0, n)
        sq = self.sq[si][:, 0:n]
        sse = self.sse[:, 0:n]
        rstd = self.rstd[:, 0:n]
        sqk, ssek, rk = self.ak("sq", si), self.ak("sse"), self.ak("rstd")
        pb = self.bank(PB, 128, 0, n)
        o.append(op("act", [("ps", pbank)], [sqk], lambda e: e.activation(out=sq, in_=pj, func=AF.Square)))
        o.append(op("pe", [sqk, "cst"], [("ps", PB)], lambda e: e.matmul(pb, lhsT=ones_ap, rhs=sq, start=True, stop=True)))
        o.append(op("act", [("ps", PB)], [ssek], lambda e: e.activation(out=sse, in_=pb, func=AF.Ln, bias=float(dsz * EPS))))
        o.append(op("act", [ssek], [rk], lambda e: e.activation(out=rstd, in_=sse, func=AF.Exp, scale=-0.5)))
        for (r0, nr, dst) in dsts:
            o.append(op("dve", [("ps", pbank), rk, ("col", gcol)], dst_writes,
                        lambda e, r0=r0, nr=nr, dst=dst: e.scalar_tensor_tensor(out=dst, in0=self.ps[r0:r0 + nr, pbank, 0:n], scalar=self.cols[r0:r0 + nr, gcol:gcol + 1], in1=self.rstd[r0:r0 + nr, 0:n], op0=ALU.mult, op1=ALU.mult)))
        return o

    def load_w(self, wname, col0, s, k, ncols=128):
        src = self.T[wname][:, col0:col0 + ncols].rearrange("(c p) j -> p c j", p=128)
        dst = self.W[s][k]
        return self.op("pool", [], [("W", s, k)], lambda e: e.dma_start(out=dst[:, :, 0:ncols], in_=src), dma=True)

    def proj_fm(self, s, k, b, pbank=PJ, n=512):
        o = []
        W = self.W[s][k]
        out = self.bank(pbank)
        for c in range(8):
            o.append(self.op("pe", [("W", s, k)] + [("UT", 4 * b + i) for i in range(4)], [("ps", pbank)],
                             lambda e, c=c: e.matmul(out, lhsT=W[:, c, :], rhs=self.UT[:, c, b * 512:(b + 1) * 512], start=(c == 0), stop=(c == 7))))
        return o

    def gate_ops(self, pbank, zi):
        o = []
        zk = self.ak("ZS", zi)
        Z = self.ZS[zi]
        pj = self.bank(pbank)
        o.append(self.op("act", [("ps", pbank)], [zk], lambda e: e.activation(out=Z, in_=pj, func=AF.Exp, scale=-1.0)))
        o.append(self.op("act", [zk], [zk], lambda e: e.activation(out=Z, in_=Z, func=AF.Ln, bias=1.0)))
        o.append(self.op("act", [zk], [zk], lambda e: e.activation(out=Z, in_=Z, func=AF.Exp, scale=-1.0)))
        o.append(self.op("dve", [("ps", pbank), zk], [zk], lambda e: e.tensor_tensor(out=Z, in0=pj, in1=Z, op=ALU.mult)))
        return o

    def attn_block(self, units, post):
        LA = 2
        groups = []
        n = len(units)
        for u in range(n + LA):
            g = []
            if u < n:
                g += units[u]["qk"] + units[u]["exp"]
            if u - LA >= 0:
                g += units[u - LA]["pv"]
            groups.append(g)
        return groups, post

    def stage_slices(self, fl):
        slices = []
        cur = []
        wr = {}
        rd = {}
        for op_ in fl:
            cut = False
            if cur and not cur[-1].glue:
                for r in op_.reads:
                    e = wr.get(r)
                    if e is not None and e != op_.eng:
                        cut = True
                        break
                if not cut:
                    for w in op_.writes:
                        e = wr.get(w)
                        if e is not None and e != op_.eng:
                            cut = True
                            break
                        es = rd.get(w)
                        if es and (len(es) > 1 or op_.eng not in es):
                            cut = True
                            break
            if cut:
                slices.append(cur)
                cur = []
                wr = {}
                rd = {}
            cur.append(op_)
            weng = "dmaq" if op_.dma else op_.eng
            for r in op_.reads:
                rd.setdefault(r, set()).add(weng)
            for w in op_.writes:
                wr[w] = weng
        if cur:
            slices.append(cur)
        return slices

    def next_p(self):
        i = self.pctr % 4
        self.pctr += 1
        return i

    def next_s(self):
        i = self.sctr % 2
        self.sctr += 1
        return S0 + i

    def make_unit(self, kT, k_reads, qT, q_reads, n0, extra_mm, bias, bias_reads, diag, v_ap, v_reads, obank, lbank, first, last, px=False):
        op = self.op
        C = self.consts()
        sb = self.next_s()
        if px:
            P = self.PX
            pk = self.ak("PX")
        else:
            pi = self.next_p()
            P = self.P[pi]
            pk = self.ak("P", pi)
        sc = self.bank(sb, 128, n0, 512)
        qk = []
        qk.append(op("pe", k_reads + q_reads, [("ps", sb)], lambda e: e.matmul(sc, lhsT=kT, rhs=qT, start=True, stop=(extra_mm is None))))
        if extra_mm is not None:
            l2, r2, rd2 = extra_mm
            qk.append(op("pe", rd2, [("ps", sb)], lambda e: e.matmul(sc, lhsT=l2, rhs=r2, start=False, stop=True)))
        ex = []
        if bias is None:
            ex.append(op("act", [("ps", sb)], [pk], lambda e: e.activation(out=P[:, n0:512], in_=sc, func=AF.Exp)))
        else:
            ex.append(op("act", [("ps", sb)] + bias_reads, [pk], lambda e: e.activation(out=P[:, n0:512], in_=sc, func=AF.Exp, bias=bias)))
        if diag:
            ex.append(op("pool", [pk, "cst"], [pk], lambda e: e.tensor_tensor(out=P[:, n0:n0 + 128], in0=P[:, n0:n0 + 128], in1=C["tri"], op=ALU.mult)))
        pv = []
        ob = self.bank(obank, 128, n0, 512)
        lb = self.bank(lbank, 128, n0, 512)
        pv.append(op("pe", [pk] + v_reads, [("ps", obank)], lambda e: e.matmul(ob, lhsT=v_ap, rhs=P[:, n0:512], start=first, stop=last)))
        pv.append(op("pe", [pk, "cst"], [("ps", lbank)], lambda e: e.matmul(lb, lhsT=C["ones"], rhs=P[:, n0:512], start=first, stop=last)))
        return {"qk": qk, "exp": ex, "pv": pv}

    def outproj(self, L, rowblocks, banks):
        op = self.op
        wname = "e_w_out" if L == 0 else "o_w_out"
        ld = []
        for c, rb in enumerate(rowblocks):
            src = self.T[wname][rb * 128:(rb + 1) * 128, :]
            ld.append(op("pool", [], [("WO", c)], lambda e, c=c, src=src: e.dma_start(out=self.WO[:, c, :], in_=src), dma=True))
        o = []
        i = 0
        for t in range(NT):
            for n in range(2):
                bnk = banks[i % len(banks)]
                i += 1
                out = self.bank(bnk)
                for c in range(4):
                    o.append(op("pe", [("WO", c), self.ak("YT", c, t // 4)], [("ps", bnk)],
                                lambda e, c=c, t=t, n=n, out=out: e.matmul(out, lhsT=self.YT[:, c, t * 128:(t + 1) * 128], rhs=self.WO[:, c, n * 512:(n + 1) * 512], start=(c == 0), stop=(c == 3))))
                hv = self.h[:, t, n * 512:(n + 1) * 512]
                o.append(op("dve", [("ps", bnk), ("h", t, n)], [("h", t, n)], lambda e, hv=hv, out=out: e.tensor_tensor(out=hv, in0=out, in1=hv, op=ALU.add)))
        return ld, o

    def prep_kv(self, L, s, ones_ap, dsz, split):
        op = self.op
        groups = []
        for b in range(NB):
            g1 = self.proj_fm(s, 1, b)
            if split:
                dsts = [(0, 64, self.KA[s][0:64, b * 512:(b + 1) * 512]), (64, 64, self.KB[s][64:128, b * 512:(b + 1) * 512])]
            else:
                dsts = [(0, 128, self.KA[s][:, b * 512:(b + 1) * 512])]
            g2 = self.norm_chain(PJ, 512, ones_ap, dsz, 1, [("K", s, b)], dsts, b % 2)
            groups.append(g1 + g2)
        Wv = self.W[s][2]
        for tg in range(4):
            g = []
            pv = self.bank(PJ)
            for ti in range(4):
                t = tg * 4 + ti
                for c in range(8):
                    g.append(op("pe", [("W", s, 2), ("UT", t)], [("ps", PJ)],
                                lambda e, c=c, t=t, ti=ti: e.matmul(self.ps[:, PJ, ti * 128:(ti + 1) * 128], lhsT=self.UT[:, c, t * 128:(t + 1) * 128], rhs=Wv[:, c, :], start=(c == 0), stop=(c == 7))))
            g.append(op("act", [("ps", PJ)], [("V", s, tg * 4 + ti) for ti in range(4)],
                        lambda e, tg=tg: e.activation(out=self.V[s][:, tg * 4:(tg + 1) * 4, :], in_=pv.rearrange("p (a b) -> p a b", a=4), func=AF.Copy)))
            groups.append(g)
        return groups

    def prep_q(self, L, s, j, ones_ap, dsz, split, gcol=0, wk=0, qx=False):
        g1 = self.proj_fm(s, wk, j)
        if qx:
            dsts = [(0, 128, self.QX[:, :])]
            qkey = self.ak("QX")
        elif split:
            dsts = [(0, 64, self.QA[j][0:64, :]), (64, 64, self.QB[j][64:128, :])]
            qkey = self.ak("Q", j)
        else:
            dsts = [(0, 128, self.QA[j][:, :])]
            qkey = self.ak("Q", j)
        g2 = self.norm_chain(PJ, 512, ones_ap, dsz, gcol, [qkey], dsts, j % 2)
        return [g1 + g2]

    def prep_z(self, s, k, j, zi):
        return [self.proj_fm(s, k, j) + self.gate_ops(PJ, zi)]

    def attn_head(self, L, hd, slot, s, zbase):
        op = self.op
        C = self.consts()
        split = (L == 0)
        ones_ap = C["bd"] if split else C["ones"]
        dsz = 64 if split else 128
        kv = self.prep_kv(L, s, ones_ap, dsz, split)
        if L == 0:
            kv = [[op("sp", [], [("Kaug", s)], lambda e: e.dma_start(out=self.KA[s][64:68, :], in_=self.T["c_ka"][hd]), dma=True),
                   op("sp", [], [("Kaug", s)], lambda e: e.dma_start(out=self.KB[s][0:4, :], in_=self.T["c_ka"][hd]), dma=True)]] + kv
        blocks = []
        for j in range(NB):
            qz = self.prep_q(L, s, j, ones_ap, dsz, split) + self.prep_z(s, 3, j, zbase + j % 2)
            units = []
            ntile = 4 * j + 4
            subs = (0, 1) if L == 0 else (0,)
            for i in range(ntile):
                n0 = max(0, i - 4 * j) * 128
                diag = i >= 4 * j
                for sub in subs:
                    if L == 0:
                        if sub == 0:
                            kT = self.KA[s][0:68, i * 128:(i + 1) * 128]
                            qT = self.QA[j][0:68, n0:512]
                        else:
                            kT = self.KB[s][:, i * 128:(i + 1) * 128]
                            qT = self.QB[j][:, n0:512]
                        obank, lbank = (A0, A2) if sub == 0 else (A1, A3)
                        u = self.make_unit(kT, [("K", s, i // 4), ("Kaug", s)], qT, [self.ak("Q", j), self.ak("Qaug")], n0, None, None, [], diag,
                                           self.V[s][:, i, :], [("V", s, i)], obank, lbank, i == 0, i == ntile - 1)
                    else:
                        kT = self.KA[s][:, i * 128:(i + 1) * 128]
                        qT = self.QA[j][:, n0:512]
                        extra = (self.sel[0:76, hd * 128:(hd + 1) * 128], self.Csplit[0:76, j * 512 + n0:(j + 1) * 512], ["sel", ("Csplit",)])
                        bias = self.kb[:, i * 12 + hd:i * 12 + hd + 1]
                        ob1, lb1 = (A0, A1) if j % 2 == 0 else (A2, A3)
                        u = self.make_unit(kT, [("K", s, i // 4)], qT, [self.ak("Q", j)], n0, extra, bias, [("kb",)], diag,
                                           self.V[s][:, i, :], [("V", s, i)], ob1, lb1, i == 0, i == ntile - 1)
                    units.append(u)
            ugroups, _ = self.attn_block(units, None)
            evac = []
            post = []
            yk = self.ak("YT", slot, j)
            Y = self.YT[:, slot, j * 512:(j + 1) * 512]
            zk = self.ak("ZS", zbase + j % 2)
            Z = self.ZS[zbase + j % 2]
            t1, t2 = self.t1, self.t2
            k1, k2 = self.ak("t1"), self.ak("t2")
            if L == 1:
                ob1, lb1 = (A0, A1) if j % 2 == 0 else (A2, A3)
                post.append(op("act", [("ps", lb1)], [k1], lambda e, lb1=lb1: e.activation(out=t1, in_=self.bank(lb1), func=AF.Ln)))
                post.append(op("act", [k1], [k1], lambda e: e.activation(out=t1, in_=t1, func=AF.Exp, scale=-1.0)))
                post.append(op("dve", [("ps", ob1), k1], [k1], lambda e, ob1=ob1: e.tensor_tensor(out=t1, in0=self.bank(ob1), in1=t1, op=ALU.mult)))
                post.append(op("dve", [k1, zk], [yk], lambda e, Y=Y, Z=Z: e.tensor_tensor(out=Y, in0=t1, in1=Z, op=ALU.mult)))
            else:
                oc1, oc2 = self.OC
                ko1, ko2 = self.ak("OC", 0), self.ak("OC", 1)
                evac.append(op("act", [("ps", A2)], [k1], lambda e: e.activation(out=t1, in_=self.bank(A2), func=AF.Ln)))
                evac.append(op("act", [("ps", A3)], [k2], lambda e: e.activation(out=t2, in_=self.bank(A3), func=AF.Ln)))
                evac.append(op("dve", [("ps", A0)], [ko1], lambda e: e.tensor_copy(out=oc1, in_=self.bank(A0))))
                evac.append(op("dve", [("ps", A1)], [ko2], lambda e: e.tensor_copy(out=oc2, in_=self.bank(A1))))
                post.append(op("act", [k1], [k1], lambda e: e.activation(out=t1, in_=t1, func=AF.Exp, scale=-1.0)))
                post.append(op("act", [k2], [k2], lambda e: e.activation(out=t2, in_=t2, func=AF.Exp, scale=-1.0)))
                post.append(op("dve", [ko1, k1], [k1], lambda e: e.tensor_tensor(out=t1, in0=oc1, in1=t1, op=ALU.mult)))
                post.append(op("dve", [ko2, k2], [k2], lambda e: e.tensor_tensor(out=t2, in0=oc2, in1=t2, op=ALU.mult)))
                post.append(op("dve", [k1, k2, ("col", 5)], [k1], lambda e: e.scalar_tensor_tensor(out=t1, in0=t2, scalar=self.cols[:, 5:6], in1=t1, op0=ALU.mult, op1=ALU.add)))
                sqk, ssek, rk = self.ak("sq", 0), self.ak("sse"), self.ak("rstd")
                sq, sse, rstd = self.sq[0], self.sse, self.rstd
                pb = self.bank(PB)
                post.append(op("act", [k1], [sqk], lambda e: e.activation(out=sq, in_=t1, func=AF.Square)))
                post.append(op("pe", [sqk, "cst"], [("ps", PB)], lambda e: e.matmul(pb, lhsT=C["ones"], rhs=sq, start=True, stop=True)))
                post.append(op("act", [("ps", PB)], [ssek], lambda e: e.activation(out=sse, in_=pb, func=AF.Ln, bias=float(128 * EPS))))
                post.append(op("act", [ssek], [rk], lambda e: e.activation(out=rstd, in_=sse, func=AF.Exp, scale=-0.5)))
                post.append(op("dve", [k1, rk, ("col", 2)], [k1], lambda e: e.scalar_tensor_tensor(out=t1, in0=t1, scalar=self.cols[:, 2:3], in1=rstd, op0=ALU.mult, op1=ALU.mult)))
                post.append(op("dve", [k1, zk], [yk], lambda e, Y=Y, Z=Z: e.tensor_tensor(out=Y, in0=t1, in1=Z, op=ALU.mult)))
            blocks.append((qz, ugroups, evac, post))
        return kv, blocks

    def xattn_head(self, L, hx, slot, s, zi):
        op = self.op
        C = self.consts()
        groups = []
        t3 = self.t3
        k3 = self.ak("t3")
        for j in range(NB):
            g = []
            g += self.prep_q(L, s, j, C["ones"], 128, False, gcol=3, wk=1, qx=True)[0]
            g += self.prep_z(s, 2, j, zi)[0]
            for mt in range(2):
                kT = self.MKT[:, hx, mt * 128:(mt + 1) * 128]
                qT = self.QX[:, :]
                u = self.make_unit(kT, [("MKT", hx)], qT, [self.ak("QX")], 0, None, None, [], False,
                                   self.MV[:, mt, hx * 128:(hx + 1) * 128], [("MV", mt)], PJ, PB, mt == 0, mt == 1, px=True)
                for q_ in u["qk"]:
                    q_.glue = True
                g += u["qk"] + u["exp"] + u["pv"]
            yk = self.ak("YT", slot, j)
            Y = self.YT[:, slot, j * 512:(j + 1) * 512]
            zk = self.ak("ZS", zi)
            Z = self.ZS[zi]
            g.append(op("act", [("ps", PB)], [k3], lambda e: e.activation(out=t3, in_=self.bank(PB), func=AF.Ln)))
            g.append(op("act", [k3], [k3], lambda e: e.activation(out=t3, in_=t3, func=AF.Exp, scale=-1.0)))
            g.append(op("dve", [("ps", PJ), k3], [k3], lambda e: e.tensor_tensor(out=t3, in0=self.bank(PJ), in1=t3, op=ALU.mult)))
            g.append(op("dve", [k3, zk], [yk], lambda e, Y=Y, Z=Z: e.tensor_tensor(out=Y, in0=t3, in1=Z, op=ALU.mult)))
            groups.append(g)
        return groups

    def conv_chunk(self, cc, slot, s, zi, wname):
        op = self.op
        groups = []
        xk = ("xpad",)
        xp = self.xpad
        cb = 6 + cc * 4
        t3 = self.t3
        k3 = self.ak("t3")
        cl = self.cols
        groups.append([op("pool", [], [xk], lambda e: e.memset(xp[:, 0:2], 0.0))])
        for j in range(NB):
            g = []
            g += self.proj_fm(s, 1, j)
            g.append(op("act", [("ps", PJ)], [k3], lambda e: e.activation(out=t3, in_=self.bank(PJ), func=AF.Copy)))
            g += self.proj_fm(s, 2, j, pbank=PB)
            g.append(op("dve", [("ps", PB), k3], [xk], lambda e, j=j: e.tensor_tensor(out=xp[:, 2 + j * 512:2 + (j + 1) * 512], in0=self.bank(PB), in1=t3, op=ALU.mult)))
            groups.append(g)
        groups.append([self.load_w(wname, 5120 + cc * 128, s, 1), self.load_w(wname, 5632 + cc * 128, s, 2)])
        for j in range(NB):
            g = []
            yk = self.ak("YT", slot, j)
            Y = self.YT[:, slot, j * 512:(j + 1) * 512]
            x0 = xp[:, j * 512:j * 512 + 512]
            x1 = xp[:, j * 512 + 1:j * 512 + 513]
            x2 = xp[:, j * 512 + 2:j * 512 + 514]
            g.append(op("dve", [xk, ("col", cb), ("col", cb + 3)], [k3], lambda e, x0=x0: e.tensor_scalar(out=t3, in0=x0, scalar1=cl[:, cb:cb + 1], scalar2=cl[:, cb + 3:cb + 4], op0=ALU.mult, op1=ALU.add)))
            g.append(op("dve", [xk, k3, ("col", cb + 1)], [k3], lambda e, x1=x1: e.scalar_tensor_tensor(out=t3, in0=x1, scalar=cl[:, cb + 1:cb + 2], in1=t3, op0=ALU.mult, op1=ALU.add)))
            g.append(op("dve", [xk, k3, ("col", cb + 2)], [k3], lambda e, x2=x2: e.scalar_tensor_tensor(out=t3, in0=x2, scalar=cl[:, cb + 2:cb + 3], in1=t3, op0=ALU.mult, op1=ALU.add)))
            g += self.proj_fm(s, 1, j)
            g.append(op("dve", [("ps", PJ), k3], [k3], lambda e: e.tensor_tensor(out=t3, in0=self.bank(PJ), in1=t3, op=ALU.mult)))
            g += self.proj_fm(s, 2, j, pbank=PB)
            g += self.gate_ops(PB, zi)
            zk = self.ak("ZS", zi)
            Z = self.ZS[zi]
            g.append(op("dve", [k3, zk], [yk], lambda e, Y=Y, Z=Z: e.tensor_tensor(out=Y, in0=t3, in1=Z, op=ALU.mult)))
            groups.append(g)
        return groups

    def fox_prologue(self):
        op = self.op
        T = self.T
        o = []
        WF = self.WF
        wfk = self.ak("WF")
        o.append(op("pool", [], [wfk], lambda e: e.memset(WF, 0.0)))
        for r0 in (0, 32, 64):
            o.append(op("pool", [wfk], [wfk], lambda e, r0=r0: e.dma_start(out=WF[:, :, r0:r0 + 12], in_=T["o_w_in"][:, 6144:6156].rearrange("(c p) j -> p c j", p=128)), dma=True))
            o.append(op("sp", [], [("col", 22)], lambda e, r0=r0: e.dma_start(out=self.cols[r0:r0 + 12, 22:23], in_=T["o_c_forget_b"].rearrange("(p o) -> p o", o=1)), dma=True))
        o.append(op("dve", [("col", 22)], [("col", 23)], lambda e: e.tensor_scalar(out=self.cols[:, 23:24], in0=self.cols[:, 22:23], scalar1=-1.0, scalar2=None, op0=ALU.mult)))
        f0, f1, f2 = self.fx
        kf = [self.ak("fx", i) for i in range(3)]
        hb, mb = self.fxb
        kh, km = self.ak("fxb", 0), self.ak("fxb", 1)
        nck = self.ak("WMh")
        o.append(op("sp", [], ["sel"], lambda e: e.dma_start(out=self.sel[0:76, :], in_=T["c_sel"]), dma=True))
        ok1 = self.ak("onesf")
        o.append(op("pool", [], [ok1], lambda e: e.memset(self.onesf, 1.0)))
        ncum = self.ncum
        R = 76
        for b in range(NB):
            pf = self.bank(PJ, R)
            for c in range(8):
                o.append(op("pe", [wfk] + [("UT", 4 * b + i) for i in range(4)], [("ps", PJ)],
                            lambda e, c=c, b=b: e.matmul(pf, lhsT=WF[:, c, 0:R], rhs=self.UT[:, c, b * 512:(b + 1) * 512], start=(c == 0), stop=(c == 7))))
            o.append(op("act", [("ps", PJ), ("col", 23)], [kf[0]], lambda e: e.activation(out=f0[0:R], in_=pf, func=AF.Exp, bias=self.cols[0:R, 23:24], scale=-1.0)))
            o.append(op("act", [kf[0]], [kf[0]], lambda e: e.activation(out=f0[0:R], in_=f0[0:R], func=AF.Ln, bias=1.0)))
            init = 0.0 if b == 0 else ncum[0:R, b * 512 - 1:b * 512]
            o.append(op("dve", [kf[0], ok1, nck], [nck], lambda e, b=b, init=init: e.tensor_tensor_scan(out=ncum[0:R, b * 512:(b + 1) * 512], data0=self.onesf[0:R, :], data1=f0[0:R], initial=init, op0=ALU.mult, op1=ALU.add)))
            nb = ncum[0:R, b * 512:(b + 1) * 512]
            o.append(op("dve", [nck], [kf[1]], lambda e, nb=nb: e.tensor_scalar(out=f1[0:R], in0=nb, scalar1=-1.0, scalar2=None, op0=ALU.mult)))
            o.append(op("dve", [kf[1]], [kh], lambda e: e.tensor_copy(out=hb[0:R], in_=f1[0:R])))
            o.append(op("dve", [kf[1], kh], [kf[2]], lambda e: e.tensor_tensor(out=f2[0:R], in0=f1[0:R], in1=hb[0:R], op=ALU.subtract)))
            o.append(op("dve", [kf[2]], [km], lambda e: e.tensor_copy(out=mb[0:R], in_=f2[0:R])))
            o.append(op("dve", [kf[2], km], [kf[2]], lambda e: e.tensor_tensor(out=f2[0:R], in0=f2[0:R], in1=mb[0:R], op=ALU.subtract)))
            cs = self.Csplit
            o.append(op("dve", [kh], [("Csplit",)], lambda e, b=b: e.tensor_copy(out=cs[0:32, b * 512:(b + 1) * 512], in_=hb[0:32])))
            o.append(op("dve", [km], [("Csplit",)], lambda e, b=b: e.tensor_copy(out=cs[32:64, b * 512:(b + 1) * 512], in_=mb[32:64])))
            o.append(op("dve", [kf[2]], [("Csplit",)], lambda e, b=b: e.tensor_copy(out=cs[64:R, b * 512:(b + 1) * 512], in_=f2[64:R])))
        pt = self.ps[:, A0, 0:192]
        for t in range(NT):
            o.append(op("pe", [nck, "identf"], [("ps", A0)], lambda e, t=t: e.transpose(out=self.ps[:, A0, t * 12:(t + 1) * 12], in_=ncum[0:12, t * 128:(t + 1) * 128], identity=self.identf[0:12, 0:12])))
        o.append(op("dve", [("ps", A0)], [("kb",)], lambda e: e.tensor_copy(out=self.kb[:, :], in_=pt)))
        return o

    def main_phase(self, L):
        op = self.op
        T = self.T
        o = []
        wname = "e_w_in" if L == 0 else "o_w_in"
        if L == 0:
            A = [("attn", h, h) for h in range(8)]
            Cv = [("conv", c, 8 + c) for c in range(4)]
            X = [("xattn", x, 12 + x) for x in range(4)]
            chunks = [A[0], Cv[0], A[1], Cv[1], A[2], Cv[2], A[3], Cv[3], A[4], X[0], A[5], X[1], A[6], X[2], A[7], X[3]]
        else:
            A = [("attn", h, h) for h in range(12)]
            X = [("xattn", x, 12 + x) for x in range(4)]
            chunks = [A[0], A[1], A[2], X[0], A[3], A[4], A[5], X[1], A[6], A[7], A[8], X[2], A[9], A[10], A[11], X[3]]
        nch = len(chunks)
        hidx = []
        hi = -1
        for (kind, idx, rb) in chunks:
            if kind == "attn":
                hi += 1
            hidx.append(hi)

        def wloads(ci):
            kind, idx, _ = chunks[ci]
            s = hidx[ci] % 2
            if kind == "attn":
                if L == 0:
                    offs = [idx * 128, 1024 + idx * 128, 2048 + idx * 128, 3072 + idx * 128]
                else:
                    offs = [idx * 128, 1536 + idx * 128, 3072 + idx * 128, 4608 + idx * 128]
                return [self.load_w(wname, offs[k], s, k) for k in range(4)]
            if kind == "conv":
                return [self.load_w(wname, 4608 + idx * 128, s, 1), self.load_w(wname, 4096 + idx * 128, s, 2)]
            if L == 0:
                return [self.load_w(wname, 6144 + idx * 128, s, 1), self.load_w(wname, 6656 + idx * 128, s, 2)]
            return [self.load_w(wname, 6156 + idx * 128, s, 1), self.load_w(wname, 6668 + idx * 128, s, 2)]

        if L == 0:
            for j in range(NB):
                o.append(op("pool", [], [self.ak("Qaug"), self.ak("Q", j)], lambda e, j=j: e.memset(self.QB[j][0:64, :], 0.0)))
                o.append(op("sp", [], [self.ak("Qaug")], lambda e, j=j: e.dma_start(out=self.QA[j][64:68, :], in_=T["c_qa"][:, j * 512:(j + 1) * 512]), dma=True))
                o.append(op("sp", [self.ak("Qaug")], [self.ak("Qaug")], lambda e, j=j: e.dma_start(out=self.QB[j][0:4, :], in_=T["c_qa"][:, j * 512:(j + 1) * 512]), dma=True))

        def flat(groups):
            r = []
            for g in groups:
                r.extend(g)
            return r

        gen = []
        for ci, (kind, idx, rb) in enumerate(chunks):
            s = hidx[ci] % 2
            slot = ci % 4
            zl = 2 * ((hidx[ci] + 1) % 2) + 1
            if kind == "conv":
                gen.append(("light", self.conv_chunk(idx, slot, s, zl, wname)))
            elif kind == "xattn":
                gen.append(("light", self.xattn_head(L, idx, slot, s, zl)))
            else:
                kv, blocks = self.attn_head(L, idx, slot, s, 2 * (hidx[ci] % 2))
                gen.append(("heavy", kv, blocks))

        if self.ydbg is not None:
            li = self.layers.index(L)
            for ci, (kind, idx, rb) in enumerate(chunks):
                slot = ci % 4
                dop = op("sp", [self.ak("YT", slot, j) for j in range(NB)], [("ydbg", ci)],
                         lambda e, slot=slot, rb=rb: e.dma_start(out=self.ydbg[li, rb * 128:(rb + 1) * 128, :], in_=self.YT[:, slot, :]), dma=True)
                if gen[ci][0] == "light":
                    gen[ci][1][-1].append(dop)
                else:
                    gen[ci][2][NB - 1][3].append(dop)
        fill_banks = [PJ, PB]
        o += wloads(0)
        assert gen[0][0] == "heavy"
        o += flat(gen[0][1]) + flat(gen[0][2][0][0])
        pending_out = None
        pending_post = []
        ci = 0
        while ci < nch:
            assert gen[ci][0] == "heavy"
            blocks = gen[ci][2]
            ahead = []
            cj = ci + 1
            loads = []
            import os as _os
            dbg = _os.environ.get("MKDBG", "")
            seq_l, seq_k = [], []
            while cj < nch and gen[cj][0] == "light":
                loads += wloads(cj)
                if "L" in dbg:
                    seq_l += gen[cj][1]
                else:
                    ahead += gen[cj][1]
                cj += 1
            tail = []
            if cj < nch:
                loads += wloads(cj)
                if "K" in dbg:
                    seq_k += gen[cj][1]
                else:
                    ahead += gen[cj][1]
                tail = gen[cj][2][0][0]
            o += loads
            import os as _os
            dbg = _os.environ.get("MKDBG", "")
            post_seq = seq_l + seq_k
            if "T" in dbg:
                post_seq += tail
                tail = []
            if "A" in dbg:
                post_seq = ahead + post_seq
                ahead = []
            tot = sum(len(g) for g in ahead) + sum(len(g) for g in tail)
            parts = [[], [], []]
            acc = 0
            for g in ahead + tail:
                k = min(2, (acc * 3) // max(tot, 1))
                parts[k].extend(g)
                acc += len(g)
            for j in range(NB):
                qz, ug, evac, post = blocks[j]
                fl = []
                fl += pending_post
                pending_post = []
                if j == 0 and pending_out is not None:
                    fl += pending_out
                    pending_out = None
                if j + 1 < NB:
                    fl += flat(blocks[j + 1][0])
                if j >= 1:
                    fl += parts[j - 1]
                nu = len(ug)
                slices = self.stage_slices(fl)
                ns = len(slices)
                si_ = 0
                for u in range(nu):
                    o += ug[u]
                    tgt = ((u + 1) * ns) // nu
                    while si_ < tgt:
                        o += slices[si_]
                        si_ += 1
                o += evac
                pending_post = post
            if not (cj < nch):
                o += pending_post
                pending_post = []
            o += flat(post_seq)
            for ck in range(ci, cj):
                if ck % 4 == 3:
                    g = ck // 4
                    rbs = [chunks[g * 4 + c][2] for c in range(4)]
                    last = not (cj < nch)
                    ld, oo = self.outproj(L, rbs, [A0, A1, A2, A3] if last else fill_banks)
                    o += ld
                    if not last and ck == cj - 1:
                        pending_out = oo
                    else:
                        o += oo
            ci = cj
        assert pending_out is None and not pending_post
        return o

    def store_out(self):
        o = []
        T = self.T
        for t in range(NT):
            o.append(self.op("sp", self.hk(t), [("out", t)], lambda e, t=t: e.dma_start(out=T["out"][t * 128:(t + 1) * 128, :], in_=self.h[:, t, :]), dma=True))
        o.append(Op("sp", None, reads=[("out", t) for t in range(NT)]))
        return o


_CACHE = {}


def _get_prog(layers):
    key = tuple(layers)
    if key not in _CACHE:
        g = Gen(layers)
        nc = g.build()
        _CACHE[key] = (nc, g.in_names)
    return _CACHE[key]


def _run(layers, xin, mem, params, consts):
    nc, in_names = _get_prog(layers)
    in_maps = []
    for b in range(8):
        m = {}
        for n in in_names:
            if n == "x":
                m[n] = np.ascontiguousarray(xin[b])
            elif n == "mem":
                m[n] = np.ascontiguousarray(mem[b])
            elif n in consts:
                m[n] = consts[n]
            else:
                m[n] = params[n]
        in_maps.append(m)
    res = run_bass_kernel_spmd(nc, in_maps, core_ids=list(range(8)))
    return np.stack([np.asarray(r["out"]) for r in res.results], 0)


FUSED = True


def kernel(**inputs):
    x = np.asarray(inputs["x"], np.float32)
    mem = np.asarray(inputs["mem"], np.float32)
    params = {}
    for n in L0_NAMES + L1_NAMES:
        a = np.asarray(inputs[n], np.float32)
        params[n] = np.ascontiguousarray(a[0])
    consts = _consts()
    if FUSED:
        return _run((0, 1), x, mem, params, consts).astype(np.float32)
    h1 = _run((0,), x, mem, params, consts)
    return _run((1,), h1, mem, params, consts).astype(np.float32)
```

```python
import contextlib
import math
import numpy as np
import ml_dtypes
import concourse.bass as bass
import concourse.mybir as mybir
from concourse.bass_utils import run_bass_kernel_spmd

F32 = mybir.dt.float32
BF16 = mybir.dt.bfloat16
AF = mybir.ActivationFunctionType
ALU = mybir.AluOpType
AX = mybir.AxisListType

ENGS = ("pe", "act", "dve", "pool", "sp")
EPS = 1e-6
S_LEN = 2048
D = 1024
NT = 16
NB = 4


class Op:
    __slots__ = ("eng", "fn", "reads", "writes", "dma", "idx", "sig", "deps", "dsem", "dtarget", "dprev", "glue")

    def __init__(self, eng, fn, reads=(), writes=(), dma=False):
        self.eng = eng
        self.fn = fn
        self.reads = tuple(reads)
        self.writes = tuple(writes)
        self.dma = dma
        self.sig = None
        self.deps = ()
        self.dsem = None
        self.glue = False


class Sched:
    NDMA_SEMS = 8

    def __init__(self, same_engine_sync=True):
        self.ops = []
        self.same_engine_sync = same_engine_sync

    def add(self, ops):
        if isinstance(ops, Op):
            self.ops.append(ops)
        else:
            for o in ops:
                self.add(o)

    def plan(self):
        last_w = {}
        readers = {}
        ops = self.ops
        for i, op in enumerate(ops):
            op.idx = i
            deps = set()
            for r in op.reads:
                w = last_w.get(r)
                if w is not None:
                    deps.add(w)
            for wr in op.writes:
                w = last_w.get(wr)
                if w is not None:
                    deps.add(w)
                rs = readers.get(wr)
                if rs:
                    deps.update(rs)
            deps.discard(i)
            for r in op.reads:
                readers.setdefault(r, []).append(i)
            for wr in op.writes:
                last_w[wr] = i
                readers[wr] = []
            latest = {}
            dmadeps = []
            for dix in deps:
                d = ops[dix]
                if d.dma:
                    dmadeps.append(dix)
                else:
                    if d.eng == op.eng and not op.dma:
                        if op.eng == "pe" or not self.same_engine_sync:
                            continue
                    if d.eng not in latest or latest[d.eng] < dix:
                        latest[d.eng] = dix
            op.deps = tuple(sorted(list(latest.values()) + dmadeps))
        need = set()
        for op in ops:
            for dix in op.deps:
                if not ops[dix].dma:
                    need.add(dix)
        cnt = {e: 0 for e in ENGS}
        for op in ops:
            if op.dma:
                continue
            if op.idx in need:
                cnt[op.eng] += 1
                op.sig = cnt[op.eng]
        dcount = {e: 0 for e in ENGS}
        for op in ops:
            if op.dma:
                n = dcount[op.eng]
                dcount[op.eng] += 1
                op.dsem = (op.eng, n % self.NDMA_SEMS)
                op.dtarget = 16 * (n // self.NDMA_SEMS + 1)
                op.dprev = 16 * (n // self.NDMA_SEMS)
        self.sigcount = cnt
        self.dmacount = dcount

    def emit(self, block, esems, dsems):
        handles = {"pe": block.tensor, "act": block.scalar, "dve": block.vector, "pool": block.gpsimd,
                   "sp": block.sync}
        ops = self.ops
        for eng in ENGS:
            mine = [op for op in ops if op.eng == eng]

            def body(e, mine=mine, eng=eng):
                waited = {}

                def wait(key, sem, val):
                    if waited.get(key, 0) >= val:
                        return
                    waited[key] = val
                    e.wait_ge(sem, val)

                for op in mine:
                    for dix in op.deps:
                        d = ops[dix]
                        if d.dma:
                            wait(d.dsem, dsems[d.dsem], d.dtarget)
                        else:
                            wait(d.eng, esems[d.eng], d.sig)
                    if op.dma and op.dprev > 0:
                        wait(op.dsem, dsems[op.dsem], op.dprev)
                    if op.fn is None:
                        continue
                    ins = op.fn(e)
                    if op.dma:
                        ins.then_inc(dsems[op.dsem], 16)
                    elif op.sig is not None:
                        ins.then_inc(esems[eng], 1)

            handles[eng](body)


def interleave(units, fillers):
    out = []
    nu, nf = len(units), len(fillers)
    if nu == 0:
        for f in fillers:
            out.extend(f)
        return out
    fi = 0
    for u in range(nu):
        out.extend(units[u])
        tgt = ((u + 1) * nf) // nu
        while fi < tgt:
            out.extend(fillers[fi])
            fi += 1
    return out


def _consts():
    bf = ml_dtypes.bfloat16
    tri = (np.arange(128)[None, :] >= np.arange(128)[:, None]).astype(np.float32)
    bd = np.zeros((128, 128), np.float32)
    bd[:64, :64] = 1.0
    bd[64:, 64:] = 1.0
    ident = np.eye(128, dtype=np.float32)
    ones = np.ones((128, 128), np.float32)
    cst = np.concatenate([tri, bd, ident, ones], axis=1).astype(bf)
    pos = np.arange(S_LEN)
    qa = np.stack([(pos % 128).astype(np.float32), (pos // 128).astype(np.float32),
                   np.ones(S_LEN, np.float32), np.ones(S_LEN, np.float32)], 0)
    ka = np.zeros((8, 4, S_LEN), np.float32)
    for h in range(8):
        sl = 2.0 ** (-(h + 1))
        ka[h, 0] = -sl
        ka[h, 1] = -sl * 128.0
        ka[h, 2] = sl * (pos % 128)
        ka[h, 3] = sl * 128.0 * (pos // 128)
    assert np.array_equal(ka.astype(bf).astype(np.float32), ka)
    assert np.array_equal(qa.astype(bf).astype(np.float32), qa)
    sel = np.zeros((76, 12 * 128), np.float32)
    for h in range(12):
        for r in (h, 32 + h, 64 + h):
            sel[r, h * 128:(h + 1) * 128] = 1.0
    identf = np.eye(128, dtype=np.float32)
    return {"c_cst": cst, "c_qa": qa.astype(bf), "c_ka": ka.astype(bf), "c_sel": sel.astype(bf),
            "c_identf": identf}


L0_NAMES = ["e_norm_g", "e_w_in", "e_w_out", "e_a_q_norm_g", "e_a_k_norm_g", "e_a_lambda", "e_a_out_norm_g",
            "e_b_conv_w", "e_b_conv_b", "e_x_q_norm_g", "e_x_k_norm_g", "e_mem_norm_g", "e_w_mem_kv"]
L1_NAMES = ["o_norm_g", "o_w_in", "o_w_out", "o_c_q_norm_g", "o_c_k_norm_g", "o_c_forget_b", "o_x_q_norm_g",
            "o_x_k_norm_g", "o_mem_norm_g", "o_w_mem_kv"]
SHAPES = {
    "e_norm_g": [1024], "e_w_in": [1024, 7168], "e_w_out": [2048, 1024], "e_a_q_norm_g": [64],
    "e_a_k_norm_g": [64], "e_a_lambda": [4, 64], "e_a_out_norm_g": [128], "e_b_conv_w": [3, 512],
    "e_b_conv_b": [512], "e_x_q_norm_g": [128], "e_x_k_norm_g": [128], "e_mem_norm_g": [1024],
    "e_w_mem_kv": [1024, 1024],
    "o_norm_g": [1024], "o_w_in": [1024, 7180], "o_w_out": [2048, 1024], "o_c_q_norm_g": [128],
    "o_c_k_norm_g": [128], "o_c_forget_b": [12], "o_x_q_norm_g": [128], "o_x_k_norm_g": [128],
    "o_mem_norm_g": [1024], "o_w_mem_kv": [1024, 1024],
}

PJ, PB, S0, S1, A0, A1, A2, A3 = range(8)


class Gen:
    def __init__(self, layers, interleave_on=True):
        self.layers = tuple(layers)
        self.il = interleave_on
        self.nc = bass.Bass("TRN2", target_bir_lowering=False)
        self.S = Sched()
        self.stack = contextlib.ExitStack()
        self.sb_bytes = 0
        self.arena_keys = set()
        self.wctr = 0
        self.pctr = 0
        self.sctr = 0

    def sb(self, name, shape, dt):
        n = 1
        for s in shape[1:]:
            n *= s
        self.sb_bytes += n * (4 if dt == F32 else 2)
        return self.stack.enter_context(self.nc.sbuf_tensor(name, shape, dt))

    def carve(self, nbytes):
        off = self.ar_off
        self.ar_off += nbytes
        assert self.ar_off <= self.AR_BYTES, (self.ar_off, self.AR_BYTES)
        return off

    def arv(self, off, shape, dt, base=None):
        n = 1
        for s in shape:
            n *= s
        nb = n * (4 if dt == F32 else 2)
        v = (self.AR if base is None else base)[:, off // 2:(off + nb) // 2]
        if dt == F32:
            v = v.bitcast(F32)
        if len(shape) == 2:
            v = v.rearrange("p (a b) -> p a b", a=shape[0])
        elif len(shape) == 3:
            v = v.rearrange("p (a b c) -> p a b c", a=shape[0], b=shape[1])
        return v

    def ak(self, *key):
        self.arena_keys.add(key)
        return key

    def op(self, eng, reads, writes, fn, dma=False):
        return Op(eng, fn, reads, writes, dma)

    def fence(self):
        keys = sorted(self.arena_keys, key=repr)
        d = self.dummy
        return [Op("pool", lambda e: e.memset(d[:], 0.0), reads=(), writes=keys)]

    def build(self):
        nc = self.nc
        T = {}
        T["x"] = nc.dram_tensor("x", [S_LEN, D], F32, kind="ExternalInput").ap()
        T["mem"] = nc.dram_tensor("mem", [256, D], F32, kind="ExternalInput").ap()
        names = []
        if 0 in self.layers:
            names += L0_NAMES
        if 1 in self.layers:
            names += L1_NAMES
        for n in names:
            T[n] = nc.dram_tensor(n, SHAPES[n], F32, kind="ExternalInput").ap()
        T["c_cst"] = nc.dram_tensor("c_cst", [128, 512], BF16, kind="ExternalInput").ap()
        T["c_qa"] = nc.dram_tensor("c_qa", [4, S_LEN], BF16, kind="ExternalInput").ap()
        T["c_ka"] = nc.dram_tensor("c_ka", [8, 4, S_LEN], BF16, kind="ExternalInput").ap()
        T["c_sel"] = nc.dram_tensor("c_sel", [76, 1536], BF16, kind="ExternalInput").ap()
        T["c_identf"] = nc.dram_tensor("c_identf", [128, 128], F32, kind="ExternalInput").ap()
        T["out"] = nc.dram_tensor("out", [S_LEN, D], F32, kind="ExternalOutput").ap()
        import os as _os
        self.ydbg = None
        if _os.environ.get("MKYDBG"):
            self.ydbg = nc.dram_tensor("ydbg", [len(self.layers), 2048, S_LEN], BF16, kind="ExternalOutput").ap()
        self.T = T
        self.in_names = ["x", "mem"] + names + ["c_cst", "c_qa", "c_ka", "c_sel", "c_identf"]

        st = self.stack
        sb = self.sb
        self.h = sb("h", [128, NT, D], F32)
        self.UT = sb("UT", [128, 8, S_LEN], BF16)
        self.KA = [sb(f"KA{s}", [128, S_LEN], BF16) for s in range(2)]
        self.V = [sb(f"V{s}", [128, NT, 128], BF16) for s in range(2)]
        self.W = [[sb(f"W{s}_{k}", [128, 8, 128], BF16) for k in range(4)] for s in range(2)]
        self.WO = sb("WO", [128, 4, D], BF16)
        self.cols = sb("cols", [128, 64], F32)
        self.cst = sb("cst", [128, 512], BF16)
        self.MKT = sb("MKT", [128, 4, 256], BF16)
        self.MV = sb("MV", [128, 2, 512], BF16)
        self.AR2 = sb("AR2", [128, 16448 // 2], BF16)
        self.KB = [self.arv(4096 * i, [S_LEN], BF16, base=self.AR2) for i in range(2)]
        self.xpad = self.arv(8192, [2064], F32, base=self.AR2)
        self.Csplit = self.arv(0, [S_LEN], BF16, base=self.AR2)
        self.sel = self.arv(4096, [1536], BF16, base=self.AR2)
        self.kb = self.arv(7168, [192], F32, base=self.AR2)
        for s_ in range(2):
            self.arena_keys.add(("Kaug", s_))
            for b_ in range(NB):
                self.arena_keys.add(("K", s_, b_))
        self.arena_keys.update([("Csplit",), "sel", ("kb",), ("xpad",)])
        self.identf = sb("identf", [128, 16], F32)
        self.dummy = sb("dmy_t", [128, 2], F32)
        self.small = sb("small", [128, 64], F32)
        self.AR_BYTES = 51200
        self.AR = sb("AR", [128, self.AR_BYTES // 2], BF16)
        assert self.sb_bytes <= 212800, self.sb_bytes
        self.ar_off = 0
        c = self.carve
        self.YT = self.arv(c(16384), [4, S_LEN], BF16)
        self.QA = [self.arv(c(1024), [512], BF16) for _ in range(4)]
        self.QB = [self.arv(c(1024), [512], BF16) for _ in range(4)]
        self.ZS = [self.arv(c(1024), [512], BF16) for _ in range(4)]
        self.t1 = self.arv(c(2048), [512], F32)
        self.t2 = self.arv(c(2048), [512], F32)
        self.P = [self.arv(c(1024), [512], BF16) for _ in range(4)]
        self.t3 = self.arv(c(2048), [512], F32)
        self.sq = [self.arv(c(1024), [512], BF16) for _ in range(2)]
        self.sse = self.arv(c(2048), [512], F32)
        self.rstd = self.arv(c(2048), [512], F32)
        self.QX = self.arv(c(1024), [512], BF16)
        self.PX = self.arv(c(1024), [512], BF16)
        self.OC = [self.arv(c(1024), [512], BF16) for _ in range(2)]
        self.raw = [self.arv(c(1024), [512], BF16) for _ in range(2)]
        assert self.ar_off <= self.AR_BYTES
        self.ar_off = 0
        self.gbc = self.arv(c(4096), [D], F32)
        self.Usc = [self.arv(c(2048), [D], BF16) for _ in range(2)]
        woff = self.ar_off
        self.WMh = self.arv(c(8192), [8, 512], BF16)
        self.ncum = self.arv(woff, [S_LEN], F32)
        self.memf = self.arv(c(4096), [D], F32)
        self.memUT = self.arv(c(4096), [8, 256], BF16)
        foff = self.ar_off
        self.fx = [self.arv(c(2048), [512], F32) for _ in range(3)]
        self.junk = self.arv(foff, [D], BF16)
        self.fxb = [self.arv(c(1024), [512], BF16) for _ in range(2)]
        self.WF = self.arv(c(2048), [8, 128], BF16)
        self.onesf = self.arv(c(2048), [512], F32)
        assert self.ar_off <= 38912, self.ar_off

        self.ps = st.enter_context(nc.psum_tensor("ps", [128, 8, 512], F32))
        self.esems = {e: st.enter_context(nc.semaphore(f"s_{e}")) for e in ENGS}
        self.dsems = {(e, k): st.enter_context(nc.semaphore(f"d_{e}{k}"))
                      for e in ("sp", "pool") for k in range(Sched.NDMA_SEMS)}

        S = self.S
        segs = [self.setup_ops()]
        for li, L in enumerate(self.layers):
            segs.append("fence")
            segs.append(self.prologue(L))
            segs.append("fence")
            segs.append(self.main_phase(L))
        segs.append(self.store_out())
        for sg in segs:
            S.add(self.fence() if isinstance(sg, str) else sg)
        S.plan()
        with nc.Block() as block:
            S.emit(block, self.esems, self.dsems)
        self.stack.close()
        return nc

    def bank(self, b, rows=128, c0=0, c1=512):
        return self.ps[0:rows, b, c0:c1]

    def col(self, j, rows=128, r0=0):
        return self.cols[r0:r0 + rows, j:j + 1]

    def hk(self, t):
        return [("h", t, 0), ("h", t, 1)]

    def setup_ops(self):
        T = self.T
        o = []
        op = self.op
        h = self.h
        for t in range(NT):
            o.append(op("sp", [], self.hk(t), lambda e, t=t: e.dma_start(out=h[:, t, :], in_=T["x"][t * 128:(t + 1) * 128, :]), dma=True))
        o.append(op("sp", [], ["cst"], lambda e: e.dma_start(out=self.cst[:], in_=T["c_cst"]), dma=True))
        o.append(op("sp", [], ["identf"], lambda e: e.dma_start(out=self.identf[:], in_=T["c_identf"][:, 0:16]), dma=True))
        o.append(op("pool", [], ["dummy"], lambda e: e.memset(self.dummy[:], 0.0)))
        o.append(op("pool", [], [("col", j) for j in range(64)], lambda e: e.memset(self.cols[:], 0.0)))
        if 0 in self.layers:
            for s in range(2):
                o.append(op("pool", [], [("K", s, b) for b in range(NB)] + [("Kaug", s)], lambda e, s=s: e.memset(self.KB[s][0:64, :], 0.0)))
        return o

    def consts(self):
        cst = self.cst
        return {"tri": cst[:, 0:128], "bd": cst[:, 128:256], "ident": cst[:, 256:384], "ones": cst[:, 384:512]}

    def small_rstd(self, src_key, src_ap, dst_key, dst_ap, n, scale, eps):
        o = []
        tmpk = ("small_tmp",)
        tmp = self.small[:, 32:32 + n]
        o.append(self.op("act", [src_key], [tmpk], lambda e: e.activation(out=tmp, in_=src_ap, func=AF.Ln, bias=float(eps), scale=float(scale))))
        o.append(self.op("act", [tmpk], [dst_key], lambda e: e.activation(out=dst_ap, in_=tmp, func=AF.Exp, scale=-0.5)))
        return o

    def norm_rows_to_T(self, src_key_list, src_ap, rstd_col_key, rstd_col, ui, dst_writes, dst_ap_fn, tb):
        o = []
        C = self.consts()
        uk = self.ak("Usc", ui)
        U = self.Usc[ui]
        o.append(self.op("dve", list(src_key_list) + [rstd_col_key, self.ak("gbc")], [uk],
                         lambda e: e.scalar_tensor_tensor(out=U, in0=src_ap, scalar=rstd_col, in1=self.gbc, op0=ALU.mult, op1=ALU.mult)))
        psT = self.ps[:, tb, :].bitcast(BF16)
        for c in range(8):
            o.append(self.op("pe", [uk, "cst"], [("ps", tb)],
                             lambda e, c=c: e.transpose(out=psT[:, c * 128:(c + 1) * 128], in_=U[:, c * 128:(c + 1) * 128], identity=C["ident"])))
        o.append(self.op("act", [("ps", tb)], dst_writes,
                         lambda e: e.activation(out=dst_ap_fn(), in_=psT.rearrange("p (c j) -> p c j", c=8), func=AF.Copy)))
        return o

    def prologue(self, L):
        T = self.T
        op = self.op
        pre = "e_" if L == 0 else "o_"
        o = []
        C = self.consts()
        h = self.h
        cols = self.cols
        def vec_col(name, j, n=128, r0=0, src_off=0):
            src = T[name]
            ap = src[src_off:src_off + n].rearrange("(p o) -> p o", o=1)
            return op("sp", [], [("col", j)], lambda e: e.dma_start(out=cols[r0:r0 + n, j:j + 1], in_=ap), dma=True)

        def scale_col(j, f):
            return op("dve", [("col", j)], [("col", j)], lambda e: e.tensor_scalar(out=cols[:, j:j + 1], in0=cols[:, j:j + 1], scalar1=float(f), scalar2=None, op0=ALU.mult))

        if L == 0:
            for r0 in (0, 64):
                o.append(vec_col("e_a_q_norm_g", 0, 64, r0))
                o.append(vec_col("e_a_k_norm_g", 1, 64, r0))
            o.append(scale_col(1, 8.0))
            o.append(vec_col("e_a_out_norm_g", 2))
            lam_init = 0.8 - 0.6 * math.exp(-0.3 * 0)
            o.append(scale_col(2, math.sqrt(128.0) * (1.0 - lam_init)))
            lpb = self.fx[0][:, 0:256]
            lpk = self.ak("fx", 0)
            o.append(op("sp", [], [lpk], lambda e: e.dma_start(out=lpb, in_=T["e_a_lambda"].rearrange("a b -> (a b)").rearrange("(o n) -> o n", o=1).partition_broadcast(128)), dma=True))
            prod = self.fx[1][:, 0:128]
            pk = self.ak("fx", 1)
            lp4 = lpb.rearrange("p (a b) -> p a b", a=4)
            for q in range(2):
                o.append(op("dve", [lpk], [pk], lambda e, q=q: e.tensor_tensor(out=prod[:, q * 64:(q + 1) * 64], in0=lp4[:, 2 * q, :], in1=lp4[:, 2 * q + 1, :], op=ALU.mult)))
            sm = self.small
            o.append(op("dve", [pk], [("sm", 0)], lambda e: e.reduce_sum(out=sm[:, 0:2], in_=prod.rearrange("p (a b) -> p a b", a=2), axis=AX.X)))
            o.append(op("act", [("sm", 0)], [("sm", 1)], lambda e: e.activation(out=sm[:, 2:4], in_=sm[:, 0:2], func=AF.Exp)))
            o.append(op("dve", [("sm", 1)], [("sm", 2)], lambda e: e.tensor_tensor(out=sm[:, 4:5], in0=sm[:, 2:3], in1=sm[:, 3:4], op=ALU.subtract)))
            o.append(op("dve", [("sm", 2)], [("col", 5)], lambda e: e.tensor_scalar(out=cols[:, 5:6], in0=sm[:, 4:5], scalar1=lam_init, scalar2=-1.0, op0=ALU.add, op1=ALU.mult)))
            for cc in range(4):
                for k in range(3):
                    o.append(op("sp", [], [("col", 6 + cc * 4 + k)], lambda e, cc=cc, k=k: e.dma_start(out=cols[:, 6 + cc * 4 + k:7 + cc * 4 + k], in_=T["e_b_conv_w"][k, cc * 128:(cc + 1) * 128].rearrange("(p o) -> p o", o=1)), dma=True))
                o.append(op("sp", [], [("col", 6 + cc * 4 + 3)], lambda e, cc=cc: e.dma_start(out=cols[:, 9 + cc * 4:10 + cc * 4], in_=T["e_b_conv_b"][cc * 128:(cc + 1) * 128].rearrange("(p o) -> p o", o=1)), dma=True))
        else:
            o.append(vec_col("o_c_q_norm_g", 0))
            o.append(vec_col("o_c_k_norm_g", 1))
            o.append(scale_col(1, math.sqrt(128.0)))
        o.append(vec_col(pre + "x_q_norm_g", 3))
        o.append(vec_col(pre + "x_k_norm_g", 4))
        o.append(scale_col(4, math.sqrt(128.0)))

        gk = self.ak("gbc")
        o.append(op("sp", [], [gk], lambda e: e.dma_start(out=self.gbc, in_=T[pre + "mem_norm_g"].rearrange("(o n) -> o n", o=1).partition_broadcast(128)), dma=True))
        mfk = self.ak("memf")
        mutk = self.ak("memUT")
        sm = self.small
        for mt in range(2):
            o.append(op("sp", [], [mfk], lambda e, mt=mt: e.dma_start(out=self.memf, in_=T["mem"][mt * 128:(mt + 1) * 128, :]), dma=True))
            jk = self.ak("fx", 0)
            o.append(op("act", [mfk], [jk, ("sm", 10)], lambda e: e.activation(out=self.junk, in_=self.memf, func=AF.Square, accum_out=sm[:, 10:11])))
            o += self.small_rstd(("sm", 10), sm[:, 10:11], ("sm", 11), sm[:, 11:12], 1, 1.0 / D, EPS)
            tb = A0 + mt
            o += self.norm_rows_to_T([mfk], self.memf, ("sm", 11), sm[:, 11:12], mt, [mutk],
                                     lambda mt=mt: self.memUT[:, :, mt * 128:(mt + 1) * 128], tb)
        wmk = self.ak("WMh")
        wsrc = T[pre + "w_mem_kv"]
        o.append(op("pool", [], [wmk], lambda e: e.dma_start(out=self.WMh, in_=wsrc[:, 0:512].rearrange("(c p) j -> p c j", p=128)), dma=True))
        for hx in range(4):
            pj = self.bank(PJ, 128, 0, 256)
            for c in range(8):
                o.append(op("pe", [wmk, mutk], [("ps", PJ)], lambda e, c=c, hx=hx: e.matmul(pj, lhsT=self.WMh[:, c, hx * 128:(hx + 1) * 128], rhs=self.memUT[:, c, :], start=(c == 0), stop=(c == 7))))
            o += self.norm_chain(PJ, 256, C["ones"], 128, 4, [("MKT", hx)],
                                 [(0, 128, self.MKT[:, hx, :])], 0)
        o.append(op("pool", [], [wmk], lambda e: e.dma_start(out=self.WMh, in_=wsrc[:, 512:1024].rearrange("(c p) j -> p c j", p=128)), dma=True))
        for mt in range(2):
            pv = self.bank(PJ)
            for c in range(8):
                o.append(op("pe", [wmk, mutk], [("ps", PJ)], lambda e, c=c, mt=mt: e.matmul(pv, lhsT=self.memUT[:, c, mt * 128:(mt + 1) * 128], rhs=self.WMh[:, c, :], start=(c == 0), stop=(c == 7))))
            o.append(op("act", [("ps", PJ)], [("MV", mt)], lambda e, mt=mt: e.activation(out=self.MV[:, mt, :], in_=pv, func=AF.Copy)))

        o.append(op("sp", [], [gk], lambda e: e.dma_start(out=self.gbc, in_=T[pre + "norm_g"].rearrange("(o n) -> o n", o=1).partition_broadcast(128)), dma=True))
        jk = self.ak("fx", 0)
        for t in range(NT):
            o.append(op("act", self.hk(t), [jk, ("hss", t)], lambda e, t=t: e.activation(out=self.junk, in_=h[:, t, :], func=AF.Square, accum_out=sm[:, 12 + t:13 + t])))
        o.append(op("act", [("hss", t) for t in range(NT)], [("small_tmp",)], lambda e: e.activation(out=sm[:, 32:48], in_=sm[:, 12:28], func=AF.Ln, bias=float(EPS), scale=1.0 / D)))
        o.append(op("act", [("small_tmp",)], [("hrstd",)], lambda e: e.activation(out=sm[:, 48:64], in_=sm[:, 32:48], func=AF.Exp, scale=-0.5)))
        for t in range(NT):
            o += self.norm_rows_to_T(self.hk(t), h[:, t, :], ("hrstd",), sm[:, 48 + t:49 + t], t % 2, [("UT", t)],
                                     lambda t=t: self.UT[:, :, t * 128:(t + 1) * 128], A0 + (t % 4))
        if L == 1:
            o += self.fox_prologue()
        return o

    def norm_chain(self, pbank, n, ones_ap, dsz, gcol, dst_writes, dsts, si):
        op = self.op
        o = []
        pj = self.bank(pbank, 128, 0, n)
        sq = self.sq[si][:, 0:n]
        sse = self.sse[:, 0:n]
        rstd = self.rstd[:, 0:n]
        sqk, ssek, rk = self.ak("sq", si), self.ak("sse"), self.ak("rstd")
        pb = self.bank(PB, 128, 0, n)
        raw = self.raw[si][:, 0:n]
        rwk = self.ak("raw", si)
        o.append(op("dve", [("ps", pbank)], [rwk], lambda e: e.tensor_copy(out=raw, in_=pj)))
        o.append(op("act", [rwk], [sqk], lambda e: e.activation(out=sq, in_=raw, func=AF.Square)))
        o.append(op("pe", [sqk, "cst"], [("ps", PB)], lambda e: e.matmul(pb, lhsT=ones_ap, rhs=sq, start=True, stop=True)))
        o.append(op("act", [("ps", PB)], [ssek], lambda e: e.activation(out=sse, in_=pb, func=AF.Ln, bias=float(dsz * EPS))))
        o.append(op("act", [ssek], [rk], lambda e: e.activation(out=rstd, in_=sse, func=AF.Exp, scale=-0.5)))
        for (r0, nr, dst) in dsts:
            o.append(op("dve", [rwk, rk, ("col", gcol)], dst_writes,
                        lambda e, r0=r0, nr=nr, dst=dst: e.scalar_tensor_tensor(out=dst, in0=self.raw[si][r0:r0 + nr, 0:n], scalar=self.cols[r0:r0 + nr, gcol:gcol + 1], in1=self.rstd[r0:r0 + nr, 0:n], op0=ALU.mult, op1=ALU.mult)))
        return o

    def load_w(self, wname, col0, s, k, ncols=128):
        src = self.T[wname][:, col0:col0 + ncols].rearrange("(c p) j -> p c j", p=128)
        dst = self.W[s][k]
        return self.op("pool", [], [("W", s, k)], lambda e: e.dma_start(out=dst[:, :, 0:ncols], in_=src), dma=True)

    def proj_fm(self, s, k, b, pbank=PJ, n=512):
        o = []
        W = self.W[s][k]
        out = self.bank(pbank)
        for c in range(8):
            o.append(self.op("pe", [("W", s, k)] + [("UT", 4 * b + i) for i in range(4)], [("ps", pbank)],
                             lambda e, c=c: e.matmul(out, lhsT=W[:, c, :], rhs=self.UT[:, c, b * 512:(b + 1) * 512], start=(c == 0), stop=(c == 7))))
        return o

    def gate_ops(self, pbank, zi):
        o = []
        zk = self.ak("ZS", zi)
        Z = self.ZS[zi]
        pj = self.bank(pbank)
        ri = zi % 2
        R = self.raw[ri]
        rwk = self.ak("raw", ri)
        o.append(self.op("dve", [("ps", pbank)], [zk], lambda e: e.tensor_copy(out=Z, in_=pj)))
        o.append(self.op("act", [zk], [rwk], lambda e: e.activation(out=R, in_=Z, func=AF.Exp, scale=-1.0)))
        o.append(self.op("act", [rwk], [rwk], lambda e: e.activation(out=R, in_=R, func=AF.Ln, bias=1.0)))
        o.append(self.op("act", [rwk], [rwk], lambda e: e.activation(out=R, in_=R, func=AF.Exp, scale=-1.0)))
        o.append(self.op("dve", [rwk, zk], [zk], lambda e: e.tensor_tensor(out=Z, in0=Z, in1=R, op=ALU.mult)))
        return o

    def attn_block(self, units, post):
        LA = 2
        groups = []
        n = len(units)
        for u in range(n + LA):
            g = []
            if u < n:
                g += units[u]["qk"] + units[u]["exp"]
            if u - LA >= 0:
                g += units[u - LA]["pv"]
            groups.append(g)
        return groups, post

    def stage_slices(self, fl):
        slices = []
        cur = []
        wr = {}
        rd = {}
        nact = 0
        for op_ in fl:
            cut = False
            if cur and not cur[-1].glue:
                if op_.eng == "act" and nact >= 1:
                    cut = True
                for r in op_.reads:
                    e = wr.get(r)
                    if e is not None and e != op_.eng:
                        cut = True
                        break
                if not cut:
                    for w in op_.writes:
                        e = wr.get(w)
                        if e is not None and e != op_.eng:
                            cut = True
                            break
                        es = rd.get(w)
                        if es and (len(es) > 1 or op_.eng not in es):
                            cut = True
                            break
            if cut:
                slices.append(cur)
                cur = []
                wr = {}
                rd = {}
                nact = 0
            if op_.eng == "act":
                nact += 1
            cur.append(op_)
            weng = "dmaq" if op_.dma else op_.eng
            for r in op_.reads:
                rd.setdefault(r, set()).add(weng)
            for w in op_.writes:
                wr[w] = weng
        if cur:
            slices.append(cur)
        return slices

    def next_p(self):
        i = self.pctr % 4
        self.pctr += 1
        return i

    def next_s(self):
        i = self.sctr % 2
        self.sctr += 1
        return S0 + i

    def make_unit(self, kT, k_reads, qT, q_reads, n0, extra_mm, bias, bias_reads, diag, v_ap, v_reads, obank, lbank, first, last, px=False):
        op = self.op
        C = self.consts()
        sb = self.next_s()
        if px:
            P = self.PX
            pk = self.ak("PX")
        else:
            pi = self.next_p()
            P = self.P[pi]
            pk = self.ak("P", pi)
        sc = self.bank(sb, 128, n0, 512)
        qk = []
        qk.append(op("pe", k_reads + q_reads, [("ps", sb)], lambda e: e.matmul(sc, lhsT=kT, rhs=qT, start=True, stop=(extra_mm is None))))
        if extra_mm is not None:
            l2, r2, rd2 = extra_mm
            qk.append(op("pe", rd2, [("ps", sb)], lambda e: e.matmul(sc, lhsT=l2, rhs=r2, start=False, stop=True)))
        ex = []
        if bias is None:
            ex.append(op("act", [("ps", sb)], [pk], lambda e: e.activation(out=P[:, n0:512], in_=sc, func=AF.Exp)))
        else:
            ex.append(op("act", [("ps", sb)] + bias_reads, [pk], lambda e: e.activation(out=P[:, n0:512], in_=sc, func=AF.Exp, bias=bias)))
        if diag:
            ex.append(op("pool", [pk, "cst"], [pk], lambda e: e.tensor_tensor(out=P[:, n0:n0 + 128], in0=P[:, n0:n0 + 128], in1=C["tri"], op=ALU.mult)))
        pv = []
        ob = self.bank(obank, 128, n0, 512)
        lb = self.bank(lbank, 128, n0, 512)
        pv.append(op("pe", [pk] + v_reads, [("ps", obank)], lambda e: e.matmul(ob, lhsT=v_ap, rhs=P[:, n0:512], start=first, stop=last)))
        pv.append(op("pe", [pk, "cst"], [("ps", lbank)], lambda e: e.matmul(lb, lhsT=C["ones"], rhs=P[:, n0:512], start=first, stop=last)))
        return {"qk": qk, "exp": ex, "pv": pv}

    def outproj(self, L, rowblocks, banks):
        op = self.op
        wname = "e_w_out" if L == 0 else "o_w_out"
        ld = []
        for c, rb in enumerate(rowblocks):
            src = self.T[wname][rb * 128:(rb + 1) * 128, :]
            ld.append(op("pool", [], [("WO", c)], lambda e, c=c, src=src: e.dma_start(out=self.WO[:, c, :], in_=src), dma=True))
        o = []
        i = 0
        for t in range(NT):
            for n in range(2):
                bnk = banks[i % len(banks)]
                i += 1
                out = self.bank(bnk)
                for c in range(4):
                    o.append(op("pe", [("WO", c), self.ak("YT", c, t // 4)], [("ps", bnk)],
                                lambda e, c=c, t=t, n=n, out=out: e.matmul(out, lhsT=self.YT[:, c, t * 128:(t + 1) * 128], rhs=self.WO[:, c, n * 512:(n + 1) * 512], start=(c == 0), stop=(c == 3))))
                hv = self.h[:, t, n * 512:(n + 1) * 512]
                o.append(op("dve", [("ps", bnk), ("h", t, n)], [("h", t, n)], lambda e, hv=hv, out=out: e.tensor_tensor(out=hv, in0=out, in1=hv, op=ALU.add)))
        return ld, o

    def prep_kv(self, L, s, ones_ap, dsz, split):
        op = self.op
        groups = []
        for b in range(NB):
            g1 = self.proj_fm(s, 1, b)
            if split:
                dsts = [(0, 64, self.KA[s][0:64, b * 512:(b + 1) * 512]), (64, 64, self.KB[s][64:128, b * 512:(b + 1) * 512])]
            else:
                dsts = [(0, 128, self.KA[s][:, b * 512:(b + 1) * 512])]
            g2 = self.norm_chain(PJ, 512, ones_ap, dsz, 1, [("K", s, b)], dsts, b % 2)
            groups.append(g1 + g2)
        Wv = self.W[s][2]
        for tg in range(4):
            g = []
            pv = self.bank(PJ)
            for ti in range(4):
                t = tg * 4 + ti
                for c in range(8):
                    g.append(op("pe", [("W", s, 2), ("UT", t)], [("ps", PJ)],
                                lambda e, c=c, t=t, ti=ti: e.matmul(self.ps[:, PJ, ti * 128:(ti + 1) * 128], lhsT=self.UT[:, c, t * 128:(t + 1) * 128], rhs=Wv[:, c, :], start=(c == 0), stop=(c == 7))))
            g.append(op("act", [("ps", PJ)], [("V", s, tg * 4 + ti) for ti in range(4)],
                        lambda e, tg=tg: e.activation(out=self.V[s][:, tg * 4:(tg + 1) * 4, :], in_=pv.rearrange("p (a b) -> p a b", a=4), func=AF.Copy)))
            groups.append(g)
        return groups

    def prep_q(self, L, s, j, ones_ap, dsz, split, gcol=0, wk=0, qx=False):
        g1 = self.proj_fm(s, wk, j)
        if qx:
            dsts = [(0, 128, self.QX[:, :])]
            qkey = self.ak("QX")
        elif split:
            dsts = [(0, 64, self.QA[j][0:64, :]), (64, 64, self.QB[j][64:128, :])]
            qkey = self.ak("Q", j)
        else:
            dsts = [(0, 128, self.QA[j][:, :])]
            qkey = self.ak("Q", j)
        g2 = self.norm_chain(PJ, 512, ones_ap, dsz, gcol, [qkey], dsts, j % 2)
        return [g1 + g2]

    def prep_z(self, s, k, j, zi):
        return [self.proj_fm(s, k, j) + self.gate_ops(PJ, zi)]

    def attn_head(self, L, hd, slot, s, zbase):
        op = self.op
        C = self.consts()
        split = (L == 0)
        ones_ap = C["bd"] if split else C["ones"]
        dsz = 64 if split else 128
        kv = self.prep_kv(L, s, ones_ap, dsz, split)
        if L == 0:
            kv = [[op("sp", [], [("Kaug", s)], lambda e: e.dma_start(out=self.KA[s][64:68, :], in_=self.T["c_ka"][hd]), dma=True),
                   op("sp", [], [("Kaug", s)], lambda e: e.dma_start(out=self.KB[s][0:4, :], in_=self.T["c_ka"][hd]), dma=True)]] + kv
        blocks = []
        for j in range(NB):
            qz = self.prep_q(L, s, j, ones_ap, dsz, split) + self.prep_z(s, 3, j, zbase + j % 2)
            units = []
            ntile = 4 * j + 4
            subs = (0, 1) if L == 0 else (0,)
            for i in range(ntile):
                n0 = max(0, i - 4 * j) * 128
                diag = i >= 4 * j
                for sub in subs:
                    if L == 0:
                        if sub == 0:
                            kT = self.KA[s][0:68, i * 128:(i + 1) * 128]
                            qT = self.QA[j][0:68, n0:512]
                        else:
                            kT = self.KB[s][:, i * 128:(i + 1) * 128]
                            qT = self.QB[j][:, n0:512]
                        obank, lbank = (A0, A2) if sub == 0 else (A1, A3)
                        u = self.make_unit(kT, [("K", s, i // 4), ("Kaug", s)], qT, [self.ak("Q", j), self.ak("Qaug")], n0, None, None, [], diag,
                                           self.V[s][:, i, :], [("V", s, i)], obank, lbank, i == 0, i == ntile - 1)
                    else:
                        kT = self.KA[s][:, i * 128:(i + 1) * 128]
                        qT = self.QA[j][:, n0:512]
                        extra = (self.sel[0:76, hd * 128:(hd + 1) * 128], self.Csplit[0:76, j * 512 + n0:(j + 1) * 512], ["sel", ("Csplit",)])
                        bias = self.kb[:, i * 12 + hd:i * 12 + hd + 1]
                        ob1, lb1 = (A0, A1) if j % 2 == 0 else (A2, A3)
                        u = self.make_unit(kT, [("K", s, i // 4)], qT, [self.ak("Q", j)], n0, extra, bias, [("kb",)], diag,
                                           self.V[s][:, i, :], [("V", s, i)], ob1, lb1, i == 0, i == ntile - 1)
                    units.append(u)
            ugroups, _ = self.attn_block(units, None)
            evac = []
            post = []
            yk = self.ak("YT", slot, j)
            Y = self.YT[:, slot, j * 512:(j + 1) * 512]
            zk = self.ak("ZS", zbase + j % 2)
            Z = self.ZS[zbase + j % 2]
            t1, t2 = self.t1, self.t2
            k1, k2 = self.ak("t1"), self.ak("t2")
            if L == 1:
                ob1, lb1 = (A0, A1) if j % 2 == 0 else (A2, A3)
                post.append(op("act", [("ps", lb1)], [k1], lambda e, lb1=lb1: e.activation(out=t1, in_=self.bank(lb1), func=AF.Ln)))
                post.append(op("act", [k1], [k1], lambda e: e.activation(out=t1, in_=t1, func=AF.Exp, scale=-1.0)))
                post.append(op("dve", [("ps", ob1), k1], [k1], lambda e, ob1=ob1: e.tensor_tensor(out=t1, in0=self.bank(ob1), in1=t1, op=ALU.mult)))
                post.append(op("dve", [k1, zk], [yk], lambda e, Y=Y, Z=Z: e.tensor_tensor(out=Y, in0=t1, in1=Z, op=ALU.mult)))
            else:
                oc1, oc2 = self.OC
                ko1, ko2 = self.ak("OC", 0), self.ak("OC", 1)
                evac.append(op("act", [("ps", A2)], [k1], lambda e: e.activation(out=t1, in_=self.bank(A2), func=AF.Ln)))
                evac.append(op("act", [("ps", A3)], [k2], lambda e: e.activation(out=t2, in_=self.bank(A3), func=AF.Ln)))
                evac.append(op("dve", [("ps", A0)], [ko1], lambda e: e.tensor_copy(out=oc1, in_=self.bank(A0))))
                evac.append(op("dve", [("ps", A1)], [ko2], lambda e: e.tensor_copy(out=oc2, in_=self.bank(A1))))
                post.append(op("act", [k1], [k1], lambda e: e.activation(out=t1, in_=t1, func=AF.Exp, scale=-1.0)))
                post.append(op("act", [k2], [k2], lambda e: e.activation(out=t2, in_=t2, func=AF.Exp, scale=-1.0)))
                post.append(op("dve", [ko1, k1], [k1], lambda e: e.tensor_tensor(out=t1, in0=oc1, in1=t1, op=ALU.mult)))
                post.append(op("dve", [ko2, k2], [k2], lambda e: e.tensor_tensor(out=t2, in0=oc2, in1=t2, op=ALU.mult)))
                post.append(op("dve", [k1, k2, ("col", 5)], [k1], lambda e: e.scalar_tensor_tensor(out=t1, in0=t2, scalar=self.cols[:, 5:6], in1=t1, op0=ALU.mult, op1=ALU.add)))
                sqk, ssek, rk = self.ak("sq", 0), self.ak("sse"), self.ak("rstd")
                sq, sse, rstd = self.sq[0], self.sse, self.rstd
                pb = self.bank(PB)
                post.append(op("act", [k1], [sqk], lambda e: e.activation(out=sq, in_=t1, func=AF.Square)))
                post.append(op("pe", [sqk, "cst"], [("ps", PB)], lambda e: e.matmul(pb, lhsT=C["ones"], rhs=sq, start=True, stop=True)))
                post.append(op("act", [("ps", PB)], [ssek], lambda e: e.activation(out=sse, in_=pb, func=AF.Ln, bias=float(128 * EPS))))
                post.append(op("act", [ssek], [rk], lambda e: e.activation(out=rstd, in_=sse, func=AF.Exp, scale=-0.5)))
                post.append(op("dve", [k1, rk, ("col", 2)], [k1], lambda e: e.scalar_tensor_tensor(out=t1, in0=t1, scalar=self.cols[:, 2:3], in1=rstd, op0=ALU.mult, op1=ALU.mult)))
                post.append(op("dve", [k1, zk], [yk], lambda e, Y=Y, Z=Z: e.tensor_tensor(out=Y, in0=t1, in1=Z, op=ALU.mult)))
            blocks.append((qz, ugroups, evac, post))
        return kv, blocks

    def xattn_head(self, L, hx, slot, s, zi):
        op = self.op
        C = self.consts()
        groups = []
        t3 = self.t3
        k3 = self.ak("t3")
        for j in range(NB):
            g = []
            g += self.prep_q(L, s, j, C["ones"], 128, False, gcol=3, wk=1, qx=True)[0]
            g += self.prep_z(s, 2, j, zi)[0]
            for mt in range(2):
                kT = self.MKT[:, hx, mt * 128:(mt + 1) * 128]
                qT = self.QX[:, :]
                u = self.make_unit(kT, [("MKT", hx)], qT, [self.ak("QX")], 0, None, None, [], False,
                                   self.MV[:, mt, hx * 128:(hx + 1) * 128], [("MV", mt)], PJ, PB, mt == 0, mt == 1, px=True)
                for q_ in u["qk"]:
                    q_.glue = True
                g += u["qk"] + u["exp"] + u["pv"]
            yk = self.ak("YT", slot, j)
            Y = self.YT[:, slot, j * 512:(j + 1) * 512]
            zk = self.ak("ZS", zi)
            Z = self.ZS[zi]
            g.append(op("act", [("ps", PB)], [k3], lambda e: e.activation(out=t3, in_=self.bank(PB), func=AF.Ln)))
            g.append(op("act", [k3], [k3], lambda e: e.activation(out=t3, in_=t3, func=AF.Exp, scale=-1.0)))
            g.append(op("dve", [("ps", PJ), k3], [k3], lambda e: e.tensor_tensor(out=t3, in0=self.bank(PJ), in1=t3, op=ALU.mult)))
            g.append(op("dve", [k3, zk], [yk], lambda e, Y=Y, Z=Z: e.tensor_tensor(out=Y, in0=t3, in1=Z, op=ALU.mult)))
            groups.append(g)
        return groups

    def conv_chunk(self, cc, slot, s, zi, wname):
        op = self.op
        groups = []
        xk = ("xpad",)
        xp = self.xpad
        cb = 6 + cc * 4
        t3 = self.t3
        k3 = self.ak("t3")
        cl = self.cols
        groups.append([op("pool", [], [xk], lambda e: e.memset(xp[:, 0:2], 0.0))])
        for j in range(NB):
            g = []
            g += self.proj_fm(s, 1, j)
            g.append(op("act", [("ps", PJ)], [k3], lambda e: e.activation(out=t3, in_=self.bank(PJ), func=AF.Copy)))
            g += self.proj_fm(s, 2, j, pbank=PB)
            g.append(op("dve", [("ps", PB), k3], [xk], lambda e, j=j: e.tensor_tensor(out=xp[:, 2 + j * 512:2 + (j + 1) * 512], in0=self.bank(PB), in1=t3, op=ALU.mult)))
            groups.append(g)
        groups.append([self.load_w(wname, 5120 + cc * 128, s, 1), self.load_w(wname, 5632 + cc * 128, s, 2)])
        for j in range(NB):
            g = []
            yk = self.ak("YT", slot, j)
            Y = self.YT[:, slot, j * 512:(j + 1) * 512]
            x0 = xp[:, j * 512:j * 512 + 512]
            x1 = xp[:, j * 512 + 1:j * 512 + 513]
            x2 = xp[:, j * 512 + 2:j * 512 + 514]
            g.append(op("dve", [xk, ("col", cb), ("col", cb + 3)], [k3], lambda e, x0=x0: e.tensor_scalar(out=t3, in0=x0, scalar1=cl[:, cb:cb + 1], scalar2=cl[:, cb + 3:cb + 4], op0=ALU.mult, op1=ALU.add)))
            g.append(op("dve", [xk, k3, ("col", cb + 1)], [k3], lambda e, x1=x1: e.scalar_tensor_tensor(out=t3, in0=x1, scalar=cl[:, cb + 1:cb + 2], in1=t3, op0=ALU.mult, op1=ALU.add)))
            g.append(op("dve", [xk, k3, ("col", cb + 2)], [k3], lambda e, x2=x2: e.scalar_tensor_tensor(out=t3, in0=x2, scalar=cl[:, cb + 2:cb + 3], in1=t3, op0=ALU.mult, op1=ALU.add)))
            g += self.proj_fm(s, 1, j)
            g.append(op("dve", [("ps", PJ), k3], [k3], lambda e: e.tensor_tensor(out=t3, in0=self.bank(PJ), in1=t3, op=ALU.mult)))
            g += self.proj_fm(s, 2, j, pbank=PB)
            g += self.gate_ops(PB, zi)
            zk = self.ak("ZS", zi)
            Z = self.ZS[zi]
            g.append(op("dve", [k3, zk], [yk], lambda e, Y=Y, Z=Z: e.tensor_tensor(out=Y, in0=t3, in1=Z, op=ALU.mult)))
            groups.append(g)
        return groups

    def fox_prologue(self):
        op = self.op
        T = self.T
        o = []
        WF = self.WF
        wfk = self.ak("WF")
        o.append(op("pool", [], [wfk], lambda e: e.memset(WF, 0.0)))
        for r0 in (0, 32, 64):
            o.append(op("pool", [wfk], [wfk], lambda e, r0=r0: e.dma_start(out=WF[:, :, r0:r0 + 12], in_=T["o_w_in"][:, 6144:6156].rearrange("(c p) j -> p c j", p=128)), dma=True))
            o.append(op("sp", [], [("col", 22)], lambda e, r0=r0: e.dma_start(out=self.cols[r0:r0 + 12, 22:23], in_=T["o_c_forget_b"].rearrange("(p o) -> p o", o=1)), dma=True))
        o.append(op("dve", [("col", 22)], [("col", 23)], lambda e: e.tensor_scalar(out=self.cols[:, 23:24], in0=self.cols[:, 22:23], scalar1=-1.0, scalar2=None, op0=ALU.mult)))
        f0, f1, f2 = self.fx
        kf = [self.ak("fx", i) for i in range(3)]
        hb, mb = self.fxb
        kh, km = self.ak("fxb", 0), self.ak("fxb", 1)
        nck = self.ak("WMh")
        o.append(op("sp", [], ["sel"], lambda e: e.dma_start(out=self.sel[0:76, :], in_=T["c_sel"]), dma=True))
        ok1 = self.ak("onesf")
        o.append(op("pool", [], [ok1], lambda e: e.memset(self.onesf, 1.0)))
        ncum = self.ncum
        R = 76
        for b in range(NB):
            pf = self.bank(PJ, R)
            for c in range(8):
                o.append(op("pe", [wfk] + [("UT", 4 * b + i) for i in range(4)], [("ps", PJ)],
                            lambda e, c=c, b=b: e.matmul(pf, lhsT=WF[:, c, 0:R], rhs=self.UT[:, c, b * 512:(b + 1) * 512], start=(c == 0), stop=(c == 7))))
            o.append(op("act", [("ps", PJ), ("col", 23)], [kf[0]], lambda e: e.activation(out=f0[0:R], in_=pf, func=AF.Exp, bias=self.cols[0:R, 23:24], scale=-1.0)))
            o.append(op("act", [kf[0]], [kf[0]], lambda e: e.activation(out=f0[0:R], in_=f0[0:R], func=AF.Ln, bias=1.0)))
            init = 0.0 if b == 0 else ncum[0:R, b * 512 - 1:b * 512]
            o.append(op("dve", [kf[0], ok1, nck], [nck], lambda e, b=b, init=init: e.tensor_tensor_scan(out=ncum[0:R, b * 512:(b + 1) * 512], data0=self.onesf[0:R, :], data1=f0[0:R], initial=init, op0=ALU.mult, op1=ALU.add)))
            nb = ncum[0:R, b * 512:(b + 1) * 512]
            o.append(op("dve", [nck], [kf[1]], lambda e, nb=nb: e.tensor_scalar(out=f1[0:R], in0=nb, scalar1=-1.0, scalar2=None, op0=ALU.mult)))
            o.append(op("dve", [kf[1]], [kh], lambda e: e.tensor_copy(out=hb[0:R], in_=f1[0:R])))
            o.append(op("dve", [kf[1], kh], [kf[2]], lambda e: e.tensor_tensor(out=f2[0:R], in0=f1[0:R], in1=hb[0:R], op=ALU.subtract)))
            o.append(op("dve", [kf[2]], [km], lambda e: e.tensor_copy(out=mb[0:R], in_=f2[0:R])))
            o.append(op("dve", [kf[2], km], [kf[2]], lambda e: e.tensor_tensor(out=f2[0:R], in0=f2[0:R], in1=mb[0:R], op=ALU.subtract)))
            cs = self.Csplit
            o.append(op("dve", [kh], [("Csplit",)], lambda e, b=b: e.tensor_copy(out=cs[0:32, b * 512:(b + 1) * 512], in_=hb[0:32])))
            o.append(op("dve", [km], [("Csplit",)], lambda e, b=b: e.tensor_copy(out=cs[32:64, b * 512:(b + 1) * 512], in_=mb[32:64])))
            o.append(op("dve", [kf[2]], [("Csplit",)], lambda e, b=b: e.tensor_copy(out=cs[64:R, b * 512:(b + 1) * 512], in_=f2[64:R])))
        pt = self.ps[:, A0, 0:192]
        for t in range(NT):
            o.append(op("pe", [nck, "identf"], [("ps", A0)], lambda e, t=t: e.transpose(out=self.ps[:, A0, t * 12:(t + 1) * 12], in_=ncum[0:12, t * 128:(t + 1) * 128], identity=self.identf[0:12, 0:12])))
        o.append(op("dve", [("ps", A0)], [("kb",)], lambda e: e.tensor_copy(out=self.kb[:, :], in_=pt)))
        return o

    def main_phase(self, L):
        op = self.op
        T = self.T
        o = []
        wname = "e_w_in" if L == 0 else "o_w_in"
        if L == 0:
            A = [("attn", h, h) for h in range(8)]
            Cv = [("conv", c, 8 + c) for c in range(4)]
            X = [("xattn", x, 12 + x) for x in range(4)]
            chunks = [A[0], Cv[0], A[1], Cv[1], A[2], Cv[2], A[3], Cv[3], A[4], X[0], A[5], X[1], A[6], X[2], A[7], X[3]]
        else:
            A = [("attn", h, h) for h in range(12)]
            X = [("xattn", x, 12 + x) for x in range(4)]
            chunks = [A[0], A[1], A[2], X[0], A[3], A[4], A[5], X[1], A[6], A[7], A[8], X[2], A[9], A[10], A[11], X[3]]
        nch = len(chunks)
        hidx = []
        hi = -1
        for (kind, idx, rb) in chunks:
            if kind == "attn":
                hi += 1
            hidx.append(hi)

        def wloads(ci):
            kind, idx, _ = chunks[ci]
            s = hidx[ci] % 2
            if kind == "attn":
                if L == 0:
                    offs = [idx * 128, 1024 + idx * 128, 2048 + idx * 128, 3072 + idx * 128]
                else:
                    offs = [idx * 128, 1536 + idx * 128, 3072 + idx * 128, 4608 + idx * 128]
                return [self.load_w(wname, offs[k], s, k) for k in range(4)]
            if kind == "conv":
                return [self.load_w(wname, 4608 + idx * 128, s, 1), self.load_w(wname, 4096 + idx * 128, s, 2)]
            if L == 0:
                return [self.load_w(wname, 6144 + idx * 128, s, 1), self.load_w(wname, 6656 + idx * 128, s, 2)]
            return [self.load_w(wname, 6156 + idx * 128, s, 1), self.load_w(wname, 6668 + idx * 128, s, 2)]

        if L == 0:
            for j in range(NB):
                o.append(op("pool", [], [self.ak("Qaug"), self.ak("Q", j)], lambda e, j=j: e.memset(self.QB[j][0:64, :], 0.0)))
                o.append(op("sp", [], [self.ak("Qaug")], lambda e, j=j: e.dma_start(out=self.QA[j][64:68, :], in_=T["c_qa"][:, j * 512:(j + 1) * 512]), dma=True))
                o.append(op("sp", [self.ak("Qaug")], [self.ak("Qaug")], lambda e, j=j: e.dma_start(out=self.QB[j][0:4, :], in_=T["c_qa"][:, j * 512:(j + 1) * 512]), dma=True))

        def flat(groups):
            r = []
            for g in groups:
                r.extend(g)
            return r

        gen = []
        for ci, (kind, idx, rb) in enumerate(chunks):
            s = hidx[ci] % 2
            slot = ci % 4
            zl = 2 * ((hidx[ci] + 1) % 2) + 1
            if kind == "conv":
                gen.append(("light", self.conv_chunk(idx, slot, s, zl, wname)))
            elif kind == "xattn":
                gen.append(("light", self.xattn_head(L, idx, slot, s, zl)))
            else:
                kv, blocks = self.attn_head(L, idx, slot, s, 2 * (hidx[ci] % 2))
                gen.append(("heavy", kv, blocks))

        if self.ydbg is not None:
            li = self.layers.index(L)
            for ci, (kind, idx, rb) in enumerate(chunks):
                slot = ci % 4
                dop = op("sp", [self.ak("YT", slot, j) for j in range(NB)], [("ydbg", ci)],
                         lambda e, slot=slot, rb=rb: e.dma_start(out=self.ydbg[li, rb * 128:(rb + 1) * 128, :], in_=self.YT[:, slot, :]), dma=True)
                if gen[ci][0] == "light":
                    gen[ci][1][-1].append(dop)
                else:
                    gen[ci][2][NB - 1][3].append(dop)
        fill_banks = [PJ, PB]
        o += wloads(0)
        assert gen[0][0] == "heavy"
        o += flat(gen[0][1]) + flat(gen[0][2][0][0])
        pending_out = None
        pending_post = []
        ci = 0
        while ci < nch:
            assert gen[ci][0] == "heavy"
            blocks = gen[ci][2]
            ahead = []
            cj = ci + 1
            loads = []
            import os as _os
            dbg = _os.environ.get("MKDBG", "")
            seq_l, seq_k = [], []
            while cj < nch and gen[cj][0] == "light":
                loads += wloads(cj)
                if "L" in dbg:
                    seq_l += gen[cj][1]
                else:
                    ahead += gen[cj][1]
                cj += 1
            tail = []
            if cj < nch:
                loads += wloads(cj)
                if "K" in dbg:
                    seq_k += gen[cj][1]
                else:
                    ahead += gen[cj][1]
                tail = gen[cj][2][0][0]
            o += loads
            import os as _os
            dbg = _os.environ.get("MKDBG", "")
            post_seq = seq_l + seq_k
            if "T" in dbg:
                post_seq += tail
                tail = []
            if "A" in dbg:
                post_seq = ahead + post_seq
                ahead = []
            tot = sum(len(g) for g in ahead) + sum(len(g) for g in tail)
            parts = [[], [], []]
            acc = 0
            for g in ahead + tail:
                fr = acc / max(tot, 1)
                k = 0 if fr < 0.2 else (1 if fr < 0.52 else 2)
                parts[k].extend(g)
                acc += len(g)
            for j in range(NB):
                qz, ug, evac, post = blocks[j]
                fl = []
                fl += pending_post
                pending_post = []
                if j == 0 and pending_out is not None:
                    fl += pending_out
                    pending_out = None
                if j + 1 < NB:
                    fl += flat(blocks[j + 1][0])
                if j >= 1:
                    fl += parts[j - 1]
                nu = len(ug)
                slices = self.stage_slices(fl)
                ns = len(slices)
                si_ = 0
                for u in range(nu):
                    o += ug[u]
                    tgt = ((u + 1) * ns) // nu
                    while si_ < tgt:
                        o += slices[si_]
                        si_ += 1
                o += evac
                pending_post = post
            if not (cj < nch):
                o += pending_post
                pending_post = []
            o += flat(post_seq)
            for ck in range(ci, cj):
                if ck % 4 == 3:
                    g = ck // 4
                    rbs = [chunks[g * 4 + c][2] for c in range(4)]
                    last = not (cj < nch)
                    ld, oo = self.outproj(L, rbs, [A0, A1, A2, A3] if last else fill_banks)
                    o += ld
                    if not last and ck == cj - 1:
                        pending_out = oo
                    else:
                        o += oo
            ci = cj
        assert pending_out is None and not pending_post
        return o

    def store_out(self):
        o = []
        T = self.T
        for t in range(NT):
            o.append(self.op("sp", self.hk(t), [("out", t)], lambda e, t=t: e.dma_start(out=T["out"][t * 128:(t + 1) * 128, :], in_=self.h[:, t, :]), dma=True))
        o.append(Op("sp", None, reads=[("out", t) for t in range(NT)]))
        return o


_CACHE = {}


def _get_prog(layers):
    key = tuple(layers)
    if key not in _CACHE:
        g = Gen(layers)
        nc = g.build()
        _CACHE[key] = (nc, g.in_names)
    return _CACHE[key]


def _run(layers, xin, mem, params, consts):
    nc, in_names = _get_prog(layers)
    in_maps = []
    for b in range(8):
        m = {}
        for n in in_names:
            if n == "x":
                m[n] = np.ascontiguousarray(xin[b])
            elif n == "mem":
                m[n] = np.ascontiguousarray(mem[b])
            elif n in consts:
                m[n] = consts[n]
            else:
                m[n] = params[n]
        in_maps.append(m)
    res = run_bass_kernel_spmd(nc, in_maps, core_ids=list(range(8)))
    return np.stack([np.asarray(r["out"]) for r in res.results], 0)


FUSED = True


def kernel(**inputs):
    x = np.asarray(inputs["x"], np.float32)
    mem = np.asarray(inputs["mem"], np.float32)
    params = {}
    for n in L0_NAMES + L1_NAMES:
        a = np.asarray(inputs[n], np.float32)
        params[n] = np.ascontiguousarray(a[0])
    consts = _consts()
    if FUSED:
        return _run((0, 1), x, mem, params, consts).astype(np.float32)
    h1 = _run((0,), x, mem, params, consts)
    return _run((1,), h1, mem, params, consts).astype(np.float32)
```

```python
import contextlib
import math
import numpy as np
import ml_dtypes
import concourse.bass as bass
import concourse.mybir as mybir
from concourse.bass_utils import run_bass_kernel_spmd

F32 = mybir.dt.float32
BF16 = mybir.dt.bfloat16
AF = mybir.ActivationFunctionType
ALU = mybir.AluOpType
AX = mybir.AxisListType

ENGS = ("pe", "act", "dve", "pool", "sp")
EPS = 1e-6
S_LEN = 2048
D = 1024
NT = 16
NB = 4


class Op:
    __slots__ = ("eng", "fn", "reads", "writes", "dma", "idx", "sig", "deps", "dsem", "dtarget", "dprev", "glue")

    def __init__(self, eng, fn, reads=(), writes=(), dma=False):
        self.eng = eng
        self.fn = fn
        self.reads = tuple(reads)
        self.writes = tuple(writes)
        self.dma = dma
        self.sig = None
        self.deps = ()
        self.dsem = None
        self.glue = False


class Sched:
    NDMA_SEMS = 8

    def __init__(self, same_engine_sync=True):
        self.ops = []
        self.same_engine_sync = same_engine_sync

    def add(self, ops):
        if isinstance(ops, Op):
            self.ops.append(ops)
        else:
            for o in ops:
                self.add(o)

    def plan(self):
        last_w = {}
        readers = {}
        ops = self.ops
        for i, op in enumerate(ops):
            op.idx = i
            deps = set()
            for r in op.reads:
                w = last_w.get(r)
                if w is not None:
                    deps.add(w)
            for wr in op.writes:
                w = last_w.get(wr)
                if w is not None:
                    deps.add(w)
                rs = readers.get(wr)
                if rs:
                    deps.update(rs)
            deps.discard(i)
            for r in op.reads:
                readers.setdefault(r, []).append(i)
            for wr in op.writes:
                last_w[wr] = i
                readers[wr] = []
            latest = {}
            dmadeps = []
            for dix in deps:
                d = ops[dix]
                if d.dma:
                    dmadeps.append(dix)
                else:
                    if d.eng == op.eng and not op.dma:
                        if op.eng == "pe" or not self.same_engine_sync:
                            continue
                    if d.eng not in latest or latest[d.eng] < dix:
                        latest[d.eng] = dix
            op.deps = tuple(sorted(list(latest.values()) + dmadeps))
        need = set()
        for op in ops:
            for dix in op.deps:
                if not ops[dix].dma:
                    need.add(dix)
        cnt = {e: 0 for e in ENGS}
        for op in ops:
            if op.dma:
                continue
            if op.idx in need:
                cnt[op.eng] += 1
                op.sig = cnt[op.eng]
        dcount = {e: 0 for e in ENGS}
        for op in ops:
            if op.dma:
                n = dcount[op.eng]
                dcount[op.eng] += 1
                op.dsem = (op.eng, n % self.NDMA_SEMS)
                op.dtarget = 16 * (n // self.NDMA_SEMS + 1)
                op.dprev = 16 * (n // self.NDMA_SEMS)
        self.sigcount = cnt
        self.dmacount = dcount

    def emit(self, block, esems, dsems):
        handles = {"pe": block.tensor, "act": block.scalar, "dve": block.vector, "pool": block.gpsimd,
                   "sp": block.sync}
        ops = self.ops
        for eng in ENGS:
            mine = [op for op in ops if op.eng == eng]

            def body(e, mine=mine, eng=eng):
                waited = {}

                def wait(key, sem, val):
                    if waited.get(key, 0) >= val:
                        return
                    waited[key] = val
                    e.wait_ge(sem, val)

                for op in mine:
                    for dix in op.deps:
                        d = ops[dix]
                        if d.dma:
                            wait(d.dsem, dsems[d.dsem], d.dtarget)
                        else:
                            wait(d.eng, esems[d.eng], d.sig)
                    if op.dma and op.dprev > 0:
                        wait(op.dsem, dsems[op.dsem], op.dprev)
                    if op.fn is None:
                        continue
                    ins = op.fn(e)
                    if op.dma:
                        ins.then_inc(dsems[op.dsem], 16)
                    elif op.sig is not None:
                        ins.then_inc(esems[eng], 1)

            handles[eng](body)


def interleave(units, fillers):
    out = []
    nu, nf = len(units), len(fillers)
    if nu == 0:
        for f in fillers:
            out.extend(f)
        return out
    fi = 0
    for u in range(nu):
        out.extend(units[u])
        tgt = ((u + 1) * nf) // nu
        while fi < tgt:
            out.extend(fillers[fi])
            fi += 1
    return out


def _consts():
    bf = ml_dtypes.bfloat16
    tri = (np.arange(128)[None, :] >= np.arange(128)[:, None]).astype(np.float32)
    bd = np.zeros((128, 128), np.float32)
    bd[:64, :64] = 1.0
    bd[64:, 64:] = 1.0
    ident = np.eye(128, dtype=np.float32)
    ones = np.ones((128, 128), np.float32)
    cst = np.concatenate([tri, bd, ident, ones], axis=1).astype(bf)
    pos = np.arange(S_LEN)
    qa = np.stack([(pos % 128).astype(np.float32), (pos // 128).astype(np.float32),
                   np.ones(S_LEN, np.float32), np.ones(S_LEN, np.float32)], 0)
    ka = np.zeros((8, 4, S_LEN), np.float32)
    for h in range(8):
        sl = 2.0 ** (-(h + 1))
        ka[h, 0] = -sl
        ka[h, 1] = -sl * 128.0
        ka[h, 2] = sl * (pos % 128)
        ka[h, 3] = sl * 128.0 * (pos // 128)
    assert np.array_equal(ka.astype(bf).astype(np.float32), ka)
    assert np.array_equal(qa.astype(bf).astype(np.float32), qa)
    sel = np.zeros((76, 12 * 128), np.float32)
    for h in range(12):
        for r in (h, 32 + h, 64 + h):
            sel[r, h * 128:(h + 1) * 128] = 1.0
    identf = np.eye(128, dtype=np.float32)
    return {"c_cst": cst, "c_qa": qa.astype(bf), "c_ka": ka.astype(bf), "c_sel": sel.astype(bf),
            "c_identf": identf}


L0_NAMES = ["e_norm_g", "e_w_in", "e_w_out", "e_a_q_norm_g", "e_a_k_norm_g", "e_a_lambda", "e_a_out_norm_g",
            "e_b_conv_w", "e_b_conv_b", "e_x_q_norm_g", "e_x_k_norm_g", "e_mem_norm_g", "e_w_mem_kv"]
L1_NAMES = ["o_norm_g", "o_w_in", "o_w_out", "o_c_q_norm_g", "o_c_k_norm_g", "o_c_forget_b", "o_x_q_norm_g",
            "o_x_k_norm_g", "o_mem_norm_g", "o_w_mem_kv"]
SHAPES = {
    "e_norm_g": [1024], "e_w_in": [1024, 7168], "e_w_out": [2048, 1024], "e_a_q_norm_g": [64],
    "e_a_k_norm_g": [64], "e_a_lambda": [4, 64], "e_a_out_norm_g": [128], "e_b_conv_w": [3, 512],
    "e_b_conv_b": [512], "e_x_q_norm_g": [128], "e_x_k_norm_g": [128], "e_mem_norm_g": [1024],
    "e_w_mem_kv": [1024, 1024],
    "o_norm_g": [1024], "o_w_in": [1024, 7180], "o_w_out": [2048, 1024], "o_c_q_norm_g": [128],
    "o_c_k_norm_g": [128], "o_c_forget_b": [12], "o_x_q_norm_g": [128], "o_x_k_norm_g": [128],
    "o_mem_norm_g": [1024], "o_w_mem_kv": [1024, 1024],
}

PJ, PB, S0, S1, A0, A1, A2, A3 = range(8)


class Gen:
    def __init__(self, layers, interleave_on=True):
        self.layers = tuple(layers)
        self.il = interleave_on
        self.nc = bass.Bass("TRN2", target_bir_lowering=False)
        self.S = Sched()
        self.stack = contextlib.ExitStack()
        self.sb_bytes = 0
        self.arena_keys = set()
        self.wctr = 0
        self.pctr = 0
        self.sctr = 0

    def sb(self, name, shape, dt):
        n = 1
        for s in shape[1:]:
            n *= s
        self.sb_bytes += n * (4 if dt == F32 else 2)
        return self.stack.enter_context(self.nc.sbuf_tensor(name, shape, dt))

    def carve(self, nbytes):
        off = self.ar_off
        self.ar_off += nbytes
        assert self.ar_off <= self.AR_BYTES, (self.ar_off, self.AR_BYTES)
        return off

    def arv(self, off, shape, dt, base=None):
        n = 1
        for s in shape:
            n *= s
        nb = n * (4 if dt == F32 else 2)
        v = (self.AR if base is None else base)[:, off // 2:(off + nb) // 2]
        if dt == F32:
            v = v.bitcast(F32)
        if len(shape) == 2:
            v = v.rearrange("p (a b) -> p a b", a=shape[0])
        elif len(shape) == 3:
            v = v.rearrange("p (a b c) -> p a b c", a=shape[0], b=shape[1])
        return v

    def ak(self, *key):
        self.arena_keys.add(key)
        return key

    def op(self, eng, reads, writes, fn, dma=False):
        return Op(eng, fn, reads, writes, dma)

    def fence(self):
        keys = sorted(self.arena_keys, key=repr)
        d = self.dummy
        return [Op("pool", lambda e: e.memset(d[:], 0.0), reads=(), writes=keys)]

    def build(self):
        nc = self.nc
        T = {}
        T["x"] = nc.dram_tensor("x", [S_LEN, D], F32, kind="ExternalInput").ap()
        T["mem"] = nc.dram_tensor("mem", [256, D], F32, kind="ExternalInput").ap()
        names = []
        if 0 in self.layers:
            names += L0_NAMES
        if 1 in self.layers:
            names += L1_NAMES
        for n in names:
            T[n] = nc.dram_tensor(n, SHAPES[n], F32, kind="ExternalInput").ap()
        T["c_cst"] = nc.dram_tensor("c_cst", [128, 512], BF16, kind="ExternalInput").ap()
        T["c_qa"] = nc.dram_tensor("c_qa", [4, S_LEN], BF16, kind="ExternalInput").ap()
        T["c_ka"] = nc.dram_tensor("c_ka", [8, 4, S_LEN], BF16, kind="ExternalInput").ap()
        T["c_sel"] = nc.dram_tensor("c_sel", [76, 1536], BF16, kind="ExternalInput").ap()
        T["c_identf"] = nc.dram_tensor("c_identf", [128, 128], F32, kind="ExternalInput").ap()
        T["out"] = nc.dram_tensor("out", [S_LEN, D], F32, kind="ExternalOutput").ap()
        import os as _os
        self.ydbg = None
        if _os.environ.get("MKYDBG"):
            self.ydbg = nc.dram_tensor("ydbg", [len(self.layers), 2048, S_LEN], BF16, kind="ExternalOutput").ap()
        self.T = T
        self.in_names = ["x", "mem"] + names + ["c_cst", "c_qa", "c_ka", "c_sel", "c_identf"]

        st = self.stack
        sb = self.sb
        self.h = sb("h", [128, NT, D], F32)
        self.UT = sb("UT", [128, 8, S_LEN], BF16)
        self.KA = [sb(f"KA{s}", [128, S_LEN], BF16) for s in range(2)]
        self.V = [sb(f"V{s}", [128, NT, 128], BF16) for s in range(2)]
        self.W = [[sb(f"W{s}_{k}", [128, 8, 128], BF16) for k in range(4)] for s in range(2)]
        self.WO = sb("WO", [128, 4, D], BF16)
        self.cols = sb("cols", [128, 64], F32)
        self.cst = sb("cst", [128, 512], BF16)
        self.MKT = sb("MKT", [128, 4, 256], BF16)
        self.MV = sb("MV", [128, 2, 512], BF16)
        self.AR2 = sb("AR2", [128, 16448 // 2], BF16)
        self.KB = [self.arv(4096 * i, [S_LEN], BF16, base=self.AR2) for i in range(2)]
        self.xpad = self.arv(8192, [2064], F32, base=self.AR2)
        self.Csplit = self.arv(0, [S_LEN], BF16, base=self.AR2)
        self.sel = self.arv(4096, [1536], BF16, base=self.AR2)
        self.kb = self.arv(7168, [192], F32, base=self.AR2)
        for s_ in range(2):
            self.arena_keys.add(("Kaug", s_))
            for b_ in range(NB):
                self.arena_keys.add(("K", s_, b_))
        self.arena_keys.update([("Csplit",), "sel", ("kb",), ("xpad",)])
        self.identf = sb("identf", [128, 16], F32)
        self.dummy = sb("dmy_t", [128, 2], F32)
        self.small = sb("small", [128, 64], F32)
        self.AR_BYTES = 51200
        self.AR = sb("AR", [128, self.AR_BYTES // 2], BF16)
        assert self.sb_bytes <= 212800, self.sb_bytes
        self.ar_off = 0
        c = self.carve
        self.YT = self.arv(c(16384), [4, S_LEN], BF16)
        self.QA = [self.arv(c(1024), [512], BF16) for _ in range(4)]
        self.QB = [self.arv(c(1024), [512], BF16) for _ in range(4)]
        self.ZS = [self.arv(c(1024), [512], BF16) for _ in range(4)]
        self.t1 = self.arv(c(2048), [512], F32)
        self.t2 = self.arv(c(2048), [512], F32)
        self.P = [self.arv(c(1024), [512], BF16) for _ in range(4)]
        self.t3 = self.arv(c(2048), [512], F32)
        self.sq = [self.arv(c(1024), [512], BF16) for _ in range(2)]
        self.sse = self.arv(c(2048), [512], F32)
        self.rstd = self.arv(c(2048), [512], F32)
        self.QX = self.arv(c(1024), [512], BF16)
        self.PX = self.arv(c(1024), [512], BF16)
        self.OC = [self.arv(c(1024), [512], BF16) for _ in range(2)]
        self.raw = [self.arv(c(1024), [512], BF16) for _ in range(2)]
        assert self.ar_off <= self.AR_BYTES
        self.ar_off = 0
        self.gbc = self.arv(c(4096), [D], F32)
        self.Usc = [self.arv(c(2048), [D], BF16) for _ in range(2)]
        woff = self.ar_off
        self.WMh = self.arv(c(8192), [8, 512], BF16)
        self.ncum = self.arv(woff, [S_LEN], F32)
        self.memf = self.arv(c(4096), [D], F32)
        self.memUT = self.arv(c(4096), [8, 256], BF16)
        foff = self.ar_off
        self.fx = [self.arv(c(2048), [512], F32) for _ in range(3)]
        self.junk = self.arv(foff, [D], BF16)
        self.fxb = [self.arv(c(1024), [512], BF16) for _ in range(2)]
        self.WF = self.arv(c(2048), [8, 128], BF16)
        self.onesf = self.arv(c(2048), [512], F32)
        assert self.ar_off <= 38912, self.ar_off

        self.ps = st.enter_context(nc.psum_tensor("ps", [128, 8, 512], F32))
        self.esems = {e: st.enter_context(nc.semaphore(f"s_{e}")) for e in ENGS}
        self.dsems = {(e, k): st.enter_context(nc.semaphore(f"d_{e}{k}"))
                      for e in ("sp", "pool") for k in range(Sched.NDMA_SEMS)}

        S = self.S
        segs = [self.setup_ops()]
        for li, L in enumerate(self.layers):
            segs.append("fence")
            segs.append(self.prologue(L))
            segs.append("fence")
            segs.append(self.main_phase(L))
        segs.append(self.store_out())
        for sg in segs:
            S.add(self.fence() if isinstance(sg, str) else sg)
        S.plan()
        with nc.Block() as block:
            S.emit(block, self.esems, self.dsems)
        self.stack.close()
        return nc

    def bank(self, b, rows=128, c0=0, c1=512):
        return self.ps[0:rows, b, c0:c1]

    def col(self, j, rows=128, r0=0):
        return self.cols[r0:r0 + rows, j:j + 1]

    def hk(self, t):
        return [("h", t, 0), ("h", t, 1)]

    def setup_ops(self):
        T = self.T
        o = []
        op = self.op
        h = self.h
        for t in range(NT):
            o.append(op("sp", [], self.hk(t), lambda e, t=t: e.dma_start(out=h[:, t, :], in_=T["x"][t * 128:(t + 1) * 128, :]), dma=True))
        o.append(op("sp", [], ["cst"], lambda e: e.dma_start(out=self.cst[:], in_=T["c_cst"]), dma=True))
        o.append(op("sp", [], ["identf"], lambda e: e.dma_start(out=self.identf[:], in_=T["c_identf"][:, 0:16]), dma=True))
        o.append(op("pool", [], ["dummy"], lambda e: e.memset(self.dummy[:], 0.0)))
        o.append(op("pool", [], [("col", j) for j in range(64)], lambda e: e.memset(self.cols[:], 0.0)))
        if 0 in self.layers:
            for s in range(2):
                o.append(op("pool", [], [("K", s, b) for b in range(NB)] + [("Kaug", s)], lambda e, s=s: e.memset(self.KB[s][0:64, :], 0.0)))
        return o

    def consts(self):
        cst = self.cst
        return {"tri": cst[:, 0:128], "bd": cst[:, 128:256], "ident": cst[:, 256:384], "ones": cst[:, 384:512]}

    def small_rstd(self, src_key, src_ap, dst_key, dst_ap, n, scale, eps):
        o = []
        tmpk = ("small_tmp",)
        tmp = self.small[:, 32:32 + n]
        o.append(self.op("act", [src_key], [tmpk], lambda e: e.activation(out=tmp, in_=src_ap, func=AF.Ln, bias=float(eps), scale=float(scale))))
        o.append(self.op("act", [tmpk], [dst_key], lambda e: e.activation(out=dst_ap, in_=tmp, func=AF.Exp, scale=-0.5)))
        return o

    def norm_rows_to_T(self, src_key_list, src_ap, rstd_col_key, rstd_col, ui, dst_writes, dst_ap_fn, tb):
        o = []
        C = self.consts()
        uk = self.ak("Usc", ui)
        U = self.Usc[ui]
        o.append(self.op("dve", list(src_key_list) + [rstd_col_key, self.ak("gbc")], [uk],
                         lambda e: e.scalar_tensor_tensor(out=U, in0=src_ap, scalar=rstd_col, in1=self.gbc, op0=ALU.mult, op1=ALU.mult)))
        psT = self.ps[:, tb, :].bitcast(BF16)
        for c in range(8):
            o.append(self.op("pe", [uk, "cst"], [("ps", tb)],
                             lambda e, c=c: e.transpose(out=psT[:, c * 128:(c + 1) * 128], in_=U[:, c * 128:(c + 1) * 128], identity=C["ident"])))
        o.append(self.op("act", [("ps", tb)], dst_writes,
                         lambda e: e.activation(out=dst_ap_fn(), in_=psT.rearrange("p (c j) -> p c j", c=8), func=AF.Copy)))
        return o

    def prologue(self, L):
        T = self.T
        op = self.op
        pre = "e_" if L == 0 else "o_"
        o = []
        C = self.consts()
        h = self.h
        cols = self.cols
        def vec_col(name, j, n=128, r0=0, src_off=0):
            src = T[name]
            ap = src[src_off:src_off + n].rearrange("(p o) -> p o", o=1)
            return op("sp", [], [("col", j)], lambda e: e.dma_start(out=cols[r0:r0 + n, j:j + 1], in_=ap), dma=True)

        def scale_col(j, f):
            return op("dve", [("col", j)], [("col", j)], lambda e: e.tensor_scalar(out=cols[:, j:j + 1], in0=cols[:, j:j + 1], scalar1=float(f), scalar2=None, op0=ALU.mult))

        if L == 0:
            for r0 in (0, 64):
                o.append(vec_col("e_a_q_norm_g", 0, 64, r0))
                o.append(vec_col("e_a_k_norm_g", 1, 64, r0))
            o.append(scale_col(1, 8.0))
            o.append(vec_col("e_a_out_norm_g", 2))
            lam_init = 0.8 - 0.6 * math.exp(-0.3 * 0)
            o.append(scale_col(2, math.sqrt(128.0) * (1.0 - lam_init)))
            lpb = self.fx[0][:, 0:256]
            lpk = self.ak("fx", 0)
            o.append(op("sp", [], [lpk], lambda e: e.dma_start(out=lpb, in_=T["e_a_lambda"].rearrange("a b -> (a b)").rearrange("(o n) -> o n", o=1).partition_broadcast(128)), dma=True))
            prod = self.fx[1][:, 0:128]
            pk = self.ak("fx", 1)
            lp4 = lpb.rearrange("p (a b) -> p a b", a=4)
            for q in range(2):
                o.append(op("dve", [lpk], [pk], lambda e, q=q: e.tensor_tensor(out=prod[:, q * 64:(q + 1) * 64], in0=lp4[:, 2 * q, :], in1=lp4[:, 2 * q + 1, :], op=ALU.mult)))
            sm = self.small
            o.append(op("dve", [pk], [("sm", 0)], lambda e: e.reduce_sum(out=sm[:, 0:2], in_=prod.rearrange("p (a b) -> p a b", a=2), axis=AX.X)))
            o.append(op("act", [("sm", 0)], [("sm", 1)], lambda e: e.activation(out=sm[:, 2:4], in_=sm[:, 0:2], func=AF.Exp)))
            o.append(op("dve", [("sm", 1)], [("sm", 2)], lambda e: e.tensor_tensor(out=sm[:, 4:5], in0=sm[:, 2:3], in1=sm[:, 3:4], op=ALU.subtract)))
            o.append(op("dve", [("sm", 2)], [("col", 5)], lambda e: e.tensor_scalar(out=cols[:, 5:6], in0=sm[:, 4:5], scalar1=lam_init, scalar2=-1.0, op0=ALU.add, op1=ALU.mult)))
            for cc in range(4):
                for k in range(3):
                    o.append(op("sp", [], [("col", 6 + cc * 4 + k)], lambda e, cc=cc, k=k: e.dma_start(out=cols[:, 6 + cc * 4 + k:7 + cc * 4 + k], in_=T["e_b_conv_w"][k, cc * 128:(cc + 1) * 128].rearrange("(p o) -> p o", o=1)), dma=True))
                o.append(op("sp", [], [("col", 6 + cc * 4 + 3)], lambda e, cc=cc: e.dma_start(out=cols[:, 9 + cc * 4:10 + cc * 4], in_=T["e_b_conv_b"][cc * 128:(cc + 1) * 128].rearrange("(p o) -> p o", o=1)), dma=True))
        else:
            o.append(vec_col("o_c_q_norm_g", 0))
            o.append(vec_col("o_c_k_norm_g", 1))
            o.append(scale_col(1, math.sqrt(128.0)))
        o.append(vec_col(pre + "x_q_norm_g", 3))
        o.append(vec_col(pre + "x_k_norm_g", 4))
        o.append(scale_col(4, math.sqrt(128.0)))

        gk = self.ak("gbc")
        o.append(op("sp", [], [gk], lambda e: e.dma_start(out=self.gbc, in_=T[pre + "mem_norm_g"].rearrange("(o n) -> o n", o=1).partition_broadcast(128)), dma=True))
        mfk = self.ak("memf")
        mutk = self.ak("memUT")
        sm = self.small
        for mt in range(2):
            o.append(op("sp", [], [mfk], lambda e, mt=mt: e.dma_start(out=self.memf, in_=T["mem"][mt * 128:(mt + 1) * 128, :]), dma=True))
            jk = self.ak("fx", 0)
            o.append(op("act", [mfk], [jk, ("sm", 10)], lambda e: e.activation(out=self.junk, in_=self.memf, func=AF.Square, accum_out=sm[:, 10:11])))
            o += self.small_rstd(("sm", 10), sm[:, 10:11], ("sm", 11), sm[:, 11:12], 1, 1.0 / D, EPS)
            tb = A0 + mt
            o += self.norm_rows_to_T([mfk], self.memf, ("sm", 11), sm[:, 11:12], mt, [mutk],
                                     lambda mt=mt: self.memUT[:, :, mt * 128:(mt + 1) * 128], tb)
        wmk = self.ak("WMh")
        wsrc = T[pre + "w_mem_kv"]
        o.append(op("pool", [], [wmk], lambda e: e.dma_start(out=self.WMh, in_=wsrc[:, 0:512].rearrange("(c p) j -> p c j", p=128)), dma=True))
        for hx in range(4):
            pj = self.bank(PJ, 128, 0, 256)
            for c in range(8):
                o.append(op("pe", [wmk, mutk], [("ps", PJ)], lambda e, c=c, hx=hx: e.matmul(pj, lhsT=self.WMh[:, c, hx * 128:(hx + 1) * 128], rhs=self.memUT[:, c, :], start=(c == 0), stop=(c == 7))))
            hd_, tl_ = self.norm_chain(PJ, 256, C["ones"], 128, 4, [("MKT", hx)],
                                       [(0, 128, self.MKT[:, hx, :])], 0)
            o += hd_ + tl_
        o.append(op("pool", [], [wmk], lambda e: e.dma_start(out=self.WMh, in_=wsrc[:, 512:1024].rearrange("(c p) j -> p c j", p=128)), dma=True))
        for mt in range(2):
            pv = self.bank(PJ)
            for c in range(8):
                o.append(op("pe", [wmk, mutk], [("ps", PJ)], lambda e, c=c, mt=mt: e.matmul(pv, lhsT=self.memUT[:, c, mt * 128:(mt + 1) * 128], rhs=self.WMh[:, c, :], start=(c == 0), stop=(c == 7))))
            o.append(op("act", [("ps", PJ)], [("MV", mt)], lambda e, mt=mt: e.activation(out=self.MV[:, mt, :], in_=pv, func=AF.Copy)))

        o.append(op("sp", [], [gk], lambda e: e.dma_start(out=self.gbc, in_=T[pre + "norm_g"].rearrange("(o n) -> o n", o=1).partition_broadcast(128)), dma=True))
        jk = self.ak("fx", 0)
        for t in range(NT):
            o.append(op("act", self.hk(t), [jk, ("hss", t)], lambda e, t=t: e.activation(out=self.junk, in_=h[:, t, :], func=AF.Square, accum_out=sm[:, 12 + t:13 + t])))
        o.append(op("act", [("hss", t) for t in range(NT)], [("small_tmp",)], lambda e: e.activation(out=sm[:, 32:48], in_=sm[:, 12:28], func=AF.Ln, bias=float(EPS), scale=1.0 / D)))
        o.append(op("act", [("small_tmp",)], [("hrstd",)], lambda e: e.activation(out=sm[:, 48:64], in_=sm[:, 32:48], func=AF.Exp, scale=-0.5)))
        for t in range(NT):
            o += self.norm_rows_to_T(self.hk(t), h[:, t, :], ("hrstd",), sm[:, 48 + t:49 + t], t % 2, [("UT", t)],
                                     lambda t=t: self.UT[:, :, t * 128:(t + 1) * 128], A0 + (t % 4))
        if L == 1:
            o += self.fox_prologue()
        return o

    def norm_chain(self, pbank, n, ones_ap, dsz, gcol, dst_writes, dsts, si):
        op = self.op
        o = []
        pj = self.bank(pbank, 128, 0, n)
        sq = self.sq[si][:, 0:n]
        sse = self.sse[:, 0:n]
        rstd = self.rstd[:, 0:n]
        sqk, ssek, rk = self.ak("sq", si), self.ak("sse"), self.ak("rstd")
        sbk = self.next_s()
        pb = self.bank(sbk, 128, 0, n)
        raw = self.raw[si][:, 0:n]
        rwk = self.ak("raw", si)
        o.append(op("dve", [("ps", pbank)], [rwk], lambda e: e.tensor_copy(out=raw, in_=pj)))
        o.append(op("dve", [rwk], [sqk], lambda e: e.tensor_tensor(out=sq, in0=raw, in1=raw, op=ALU.mult)))
        head = o
        o = []
        ssm = op("pe", [sqk, "cst"], [("ps", sbk)], lambda e: e.matmul(pb, lhsT=ones_ap, rhs=sq, start=True, stop=True))
        ssm.glue = True
        o.append(ssm)
        o.append(op("act", [("ps", sbk)], [ssek], lambda e: e.activation(out=sse, in_=pb, func=AF.Ln, bias=float(dsz * EPS))))
        o.append(op("act", [ssek], [rk], lambda e: e.activation(out=rstd, in_=sse, func=AF.Exp, scale=-0.5)))
        for (r0, nr, dst) in dsts:
            o.append(op("dve", [rwk, rk, ("col", gcol)], dst_writes,
                        lambda e, r0=r0, nr=nr, dst=dst: e.scalar_tensor_tensor(out=dst, in0=self.raw[si][r0:r0 + nr, 0:n], scalar=self.cols[r0:r0 + nr, gcol:gcol + 1], in1=self.rstd[r0:r0 + nr, 0:n], op0=ALU.mult, op1=ALU.mult)))
        return head, o

    def load_w(self, wname, col0, s, k, ncols=128):
        src = self.T[wname][:, col0:col0 + ncols].rearrange("(c p) j -> p c j", p=128)
        dst = self.W[s][k]
        return self.op("pool", [], [("W", s, k)], lambda e: e.dma_start(out=dst[:, :, 0:ncols], in_=src), dma=True)

    def proj_fm(self, s, k, b, pbank=PJ, n=512):
        o = []
        W = self.W[s][k]
        out = self.bank(pbank)
        for c in range(8):
            o.append(self.op("pe", [("W", s, k)] + [("UT", 4 * b + i) for i in range(4)], [("ps", pbank)],
                             lambda e, c=c: e.matmul(out, lhsT=W[:, c, :], rhs=self.UT[:, c, b * 512:(b + 1) * 512], start=(c == 0), stop=(c == 7))))
        return o

    def gate_ops(self, pbank, zi):
        o = []
        zk = self.ak("ZS", zi)
        Z = self.ZS[zi]
        pj = self.bank(pbank)
        ri = zi % 2
        R = self.raw[ri]
        rwk = self.ak("raw", ri)
        o.append(self.op("dve", [("ps", pbank)], [zk], lambda e: e.tensor_copy(out=Z, in_=pj)))
        head = o
        o = []
        o.append(self.op("act", [zk], [rwk], lambda e: e.activation(out=R, in_=Z, func=AF.Exp, scale=-1.0)))
        o.append(self.op("act", [rwk], [rwk], lambda e: e.activation(out=R, in_=R, func=AF.Ln, bias=1.0)))
        o.append(self.op("act", [rwk], [rwk], lambda e: e.activation(out=R, in_=R, func=AF.Exp, scale=-1.0)))
        o.append(self.op("dve", [rwk, zk], [zk], lambda e: e.tensor_tensor(out=Z, in0=Z, in1=R, op=ALU.mult)))
        return head, o

    def attn_block(self, units, post):
        LA = 2
        groups = []
        n = len(units)
        for u in range(n + LA):
            g = []
            if u < n:
                g += units[u]["qk"] + units[u]["exp"]
            if u - LA >= 0:
                g += units[u - LA]["pv"]
            groups.append(g)
        return groups, post

    def stage_slices(self, fl):
        slices = []
        cur = []
        wr = {}
        rd = {}
        nact = 0
        for op_ in fl:
            cut = False
            if cur and not cur[-1].glue:
                if op_.eng == "act" and nact >= 1:
                    cut = True
                for r in op_.reads:
                    e = wr.get(r)
                    if e is not None and e != op_.eng:
                        cut = True
                        break
                if not cut:
                    for w in op_.writes:
                        e = wr.get(w)
                        if e is not None and e != op_.eng:
                            cut = True
                            break
                        es = rd.get(w)
                        if es and (len(es) > 1 or op_.eng not in es):
                            cut = True
                            break
            if cut:
                slices.append(cur)
                cur = []
                wr = {}
                rd = {}
                nact = 0
            if op_.eng == "act":
                nact += 1
            cur.append(op_)
            weng = "dmaq" if op_.dma else op_.eng
            for r in op_.reads:
                rd.setdefault(r, set()).add(weng)
            for w in op_.writes:
                wr[w] = weng
        if cur:
            slices.append(cur)
        return slices

    def next_p(self):
        i = self.pctr % 4
        self.pctr += 1
        return i

    def next_s(self):
        i = self.sctr % 2
        self.sctr += 1
        return S0 + i

    def make_unit(self, kT, k_reads, qT, q_reads, n0, extra_mm, bias, bias_reads, diag, v_ap, v_reads, obank, lbank, first, last, px=False):
        op = self.op
        C = self.consts()
        sb = self.next_s()
        if px:
            P = self.PX
            pk = self.ak("PX")
        else:
            pi = self.next_p()
            P = self.P[pi]
            pk = self.ak("P", pi)
        sc = self.bank(sb, 128, n0, 512)
        qk = []
        qk.append(op("pe", k_reads + q_reads, [("ps", sb)], lambda e: e.matmul(sc, lhsT=kT, rhs=qT, start=True, stop=(extra_mm is None))))
        if extra_mm is not None:
            l2, r2, rd2 = extra_mm
            qk.append(op("pe", rd2, [("ps", sb)], lambda e: e.matmul(sc, lhsT=l2, rhs=r2, start=False, stop=True)))
        ex = []
        if bias is None:
            ex.append(op("act", [("ps", sb)], [pk], lambda e: e.activation(out=P[:, n0:512], in_=sc, func=AF.Exp)))
        else:
            ex.append(op("act", [("ps", sb)] + bias_reads, [pk], lambda e: e.activation(out=P[:, n0:512], in_=sc, func=AF.Exp, bias=bias)))
        if diag:
            ex.append(op("pool", [pk, "cst"], [pk], lambda e: e.tensor_tensor(out=P[:, n0:n0 + 128], in0=P[:, n0:n0 + 128], in1=C["tri"], op=ALU.mult)))
        pv = []
        ob = self.bank(obank, 128, n0, 512)
        lb = self.bank(lbank, 128, n0, 512)
        pv.append(op("pe", [pk] + v_reads, [("ps", obank)], lambda e: e.matmul(ob, lhsT=v_ap, rhs=P[:, n0:512], start=first, stop=last)))
        pv.append(op("pe", [pk, "cst"], [("ps", lbank)], lambda e: e.matmul(lb, lhsT=C["ones"], rhs=P[:, n0:512], start=first, stop=last)))
        return {"qk": qk, "exp": ex, "pv": pv}

    def outproj_loads(self, L, rowblocks):
        op = self.op
        wname = "e_w_out" if L == 0 else "o_w_out"
        ld = []
        for c, rb in enumerate(rowblocks):
            src = self.T[wname][rb * 128:(rb + 1) * 128, :]
            ld.append(op("pool", [], [("WO", c)], lambda e, c=c, src=src: e.dma_start(out=self.WO[:, c, :], in_=src), dma=True))
        return ld

    def outproj(self, L, rowblocks, banks):
        op = self.op
        ld = []
        o = []
        i = 0
        for t in range(NT):
            for n in range(2):
                bnk = banks[i % len(banks)]
                i += 1
                out = self.bank(bnk)
                for c in range(4):
                    o.append(op("pe", [("WO", c), self.ak("YT", c, t // 4)], [("ps", bnk)],
                                lambda e, c=c, t=t, n=n, out=out: e.matmul(out, lhsT=self.YT[:, c, t * 128:(t + 1) * 128], rhs=self.WO[:, c, n * 512:(n + 1) * 512], start=(c == 0), stop=(c == 3))))
                hv = self.h[:, t, n * 512:(n + 1) * 512]
                o.append(op("dve", [("ps", bnk), ("h", t, n)], [("h", t, n)], lambda e, hv=hv, out=out: e.tensor_tensor(out=hv, in0=out, in1=hv, op=ALU.add)))
        return ld, o

    def prep_kv(self, L, s, ones_ap, dsz, split):
        op = self.op
        seq = []
        prev_tail = []
        for b in range(NB):
            if split:
                dsts = [(0, 64, self.KA[s][0:64, b * 512:(b + 1) * 512]), (64, 64, self.KB[s][64:128, b * 512:(b + 1) * 512])]
            else:
                dsts = [(0, 128, self.KA[s][:, b * 512:(b + 1) * 512])]
            kbank = PJ if b % 2 == 0 else PB
            head, tail = self.norm_chain(kbank, 512, ones_ap, dsz, 1, [("K", s, b)], dsts, b % 2)
            seq += self.proj_fm(s, 1, b, pbank=kbank) + prev_tail + head
            prev_tail = tail
        Wv = self.W[s][2]

        def vmm(tg, bank):
            g = []
            for ti in range(4):
                t = tg * 4 + ti
                for c in range(8):
                    g.append(op("pe", [("W", s, 2), ("UT", t)], [("ps", bank)],
                                lambda e, c=c, t=t, ti=ti: e.matmul(self.ps[:, bank, ti * 128:(ti + 1) * 128], lhsT=self.UT[:, c, t * 128:(t + 1) * 128], rhs=Wv[:, c, :], start=(c == 0), stop=(c == 7))))
            return g

        def evac(tg, bank):
            pv = self.bank(bank)
            return [op("act", [("ps", bank)], [("V", s, tg * 4 + ti) for ti in range(4)],
                       lambda e: e.activation(out=self.V[s][:, tg * 4:(tg + 1) * 4, :], in_=pv.rearrange("p (a b) -> p a b", a=4), func=AF.Copy))]

        seq += vmm(0, PJ) + prev_tail + vmm(1, PB) + evac(0, PJ) + vmm(2, PJ) + evac(1, PB) + vmm(3, PB) + evac(2, PJ) + evac(3, PB)
        return [seq]

    def qz_seq(self, L, s, j, ones_ap, dsz, split, zk_slot, zi, gcol=0, wq=0, qx=False):
        if qx:
            dsts = [(0, 128, self.QX[:, :])]
            qkey = self.ak("QX")
        elif split:
            dsts = [(0, 64, self.QA[j][0:64, :]), (64, 64, self.QB[j][64:128, :])]
            qkey = self.ak("Q", j)
        else:
            dsts = [(0, 128, self.QA[j][:, :])]
            qkey = self.ak("Q", j)
        qh, qt = self.norm_chain(PJ, 512, ones_ap, dsz, gcol, [qkey], dsts, j % 2)
        zh, zt = self.gate_ops(PB, zi)
        return self.proj_fm(s, wq, j) + qh + self.proj_fm(s, zk_slot, j, pbank=PB) + qt + zh + zt

    def attn_head(self, L, hd, slot, s, zbase):
        op = self.op
        C = self.consts()
        split = (L == 0)
        ones_ap = C["bd"] if split else C["ones"]
        dsz = 64 if split else 128
        kv = self.prep_kv(L, s, ones_ap, dsz, split)
        if L == 0:
            kv = [[op("sp", [], [("Kaug", s)], lambda e: e.dma_start(out=self.KA[s][64:68, :], in_=self.T["c_ka"][hd]), dma=True),
                   op("sp", [], [("Kaug", s)], lambda e: e.dma_start(out=self.KB[s][0:4, :], in_=self.T["c_ka"][hd]), dma=True)]] + kv
        blocks = []
        for j in range(NB):
            qz = [self.qz_seq(L, s, j, ones_ap, dsz, split, 3, zbase + j % 2)]
            units = []
            ntile = 4 * j + 4
            subs = (0, 1) if L == 0 else (0,)
            for i in range(ntile):
                n0 = max(0, i - 4 * j) * 128
                diag = i >= 4 * j
                for sub in subs:
                    if L == 0:
                        if sub == 0:
                            kT = self.KA[s][0:68, i * 128:(i + 1) * 128]
                            qT = self.QA[j][0:68, n0:512]
                        else:
                            kT = self.KB[s][:, i * 128:(i + 1) * 128]
                            qT = self.QB[j][:, n0:512]
                        obank, lbank = (A0, A2) if sub == 0 else (A1, A3)
                        u = self.make_unit(kT, [("K", s, i // 4), ("Kaug", s)], qT, [self.ak("Q", j), self.ak("Qaug")], n0, None, None, [], diag,
                                           self.V[s][:, i, :], [("V", s, i)], obank, lbank, i == 0, i == ntile - 1)
                    else:
                        kT = self.KA[s][:, i * 128:(i + 1) * 128]
                        qT = self.QA[j][:, n0:512]
                        extra = (self.sel[0:76, hd * 128:(hd + 1) * 128], self.Csplit[0:76, j * 512 + n0:(j + 1) * 512], ["sel", ("Csplit",)])
                        bias = self.kb[:, i * 12 + hd:i * 12 + hd + 1]
                        ob1, lb1 = (A0, A1) if j % 2 == 0 else (A2, A3)
                        u = self.make_unit(kT, [("K", s, i // 4)], qT, [self.ak("Q", j)], n0, extra, bias, [("kb",)], diag,
                                           self.V[s][:, i, :], [("V", s, i)], ob1, lb1, i == 0, i == ntile - 1)
                    units.append(u)
            ugroups, _ = self.attn_block(units, None)
            evac = []
            post = []
            yk = self.ak("YT", slot, j)
            Y = self.YT[:, slot, j * 512:(j + 1) * 512]
            zk = self.ak("ZS", zbase + j % 2)
            Z = self.ZS[zbase + j % 2]
            t1, t2 = self.t1, self.t2
            k1, k2 = self.ak("t1"), self.ak("t2")
            if L == 1:
                ob1, lb1 = (A0, A1) if j % 2 == 0 else (A2, A3)
                post.append(op("act", [("ps", lb1)], [k1], lambda e, lb1=lb1: e.activation(out=t1, in_=self.bank(lb1), func=AF.Ln)))
                post.append(op("act", [k1], [k1], lambda e: e.activation(out=t1, in_=t1, func=AF.Exp, scale=-1.0)))
                post.append(op("dve", [("ps", ob1), k1], [k1], lambda e, ob1=ob1: e.tensor_tensor(out=t1, in0=self.bank(ob1), in1=t1, op=ALU.mult)))
                post.append(op("dve", [k1, zk], [yk], lambda e, Y=Y, Z=Z: e.tensor_tensor(out=Y, in0=t1, in1=Z, op=ALU.mult)))
            else:
                oc1, oc2 = self.OC
                ko1, ko2 = self.ak("OC", 0), self.ak("OC", 1)
                evac.append(op("act", [("ps", A2)], [k1], lambda e: e.activation(out=t1, in_=self.bank(A2), func=AF.Ln)))
                evac.append(op("act", [("ps", A3)], [k2], lambda e: e.activation(out=t2, in_=self.bank(A3), func=AF.Ln)))
                evac.append(op("dve", [("ps", A0)], [ko1], lambda e: e.tensor_copy(out=oc1, in_=self.bank(A0))))
                evac.append(op("dve", [("ps", A1)], [ko2], lambda e: e.tensor_copy(out=oc2, in_=self.bank(A1))))
                post.append(op("act", [k1], [k1], lambda e: e.activation(out=t1, in_=t1, func=AF.Exp, scale=-1.0)))
                post.append(op("act", [k2], [k2], lambda e: e.activation(out=t2, in_=t2, func=AF.Exp, scale=-1.0)))
                post.append(op("dve", [ko1, k1], [k1], lambda e: e.tensor_tensor(out=t1, in0=oc1, in1=t1, op=ALU.mult)))
                post.append(op("dve", [ko2, k2], [k2], lambda e: e.tensor_tensor(out=t2, in0=oc2, in1=t2, op=ALU.mult)))
                post.append(op("dve", [k1, k2, ("col", 5)], [k1], lambda e: e.scalar_tensor_tensor(out=t1, in0=t2, scalar=self.cols[:, 5:6], in1=t1, op0=ALU.mult, op1=ALU.add)))
                sqk, ssek, rk = self.ak("sq", 0), self.ak("sse"), self.ak("rstd")
                sq, sse, rstd = self.sq[0], self.sse, self.rstd
                sbk = self.next_s()
                pb = self.bank(sbk)
                post.append(op("act", [k1], [sqk], lambda e: e.activation(out=sq, in_=t1, func=AF.Square)))
                ssm = op("pe", [sqk, "cst"], [("ps", sbk)], lambda e, pb=pb: e.matmul(pb, lhsT=C["ones"], rhs=sq, start=True, stop=True))
                ssm.glue = True
                post.append(ssm)
                post.append(op("act", [("ps", sbk)], [ssek], lambda e, pb=pb: e.activation(out=sse, in_=pb, func=AF.Ln, bias=float(128 * EPS))))
                post.append(op("act", [ssek], [rk], lambda e: e.activation(out=rstd, in_=sse, func=AF.Exp, scale=-0.5)))
                post.append(op("dve", [k1, rk, ("col", 2)], [k1], lambda e: e.scalar_tensor_tensor(out=t1, in0=t1, scalar=self.cols[:, 2:3], in1=rstd, op0=ALU.mult, op1=ALU.mult)))
                post.append(op("dve", [k1, zk], [yk], lambda e, Y=Y, Z=Z: e.tensor_tensor(out=Y, in0=t1, in1=Z, op=ALU.mult)))
            blocks.append((qz, ugroups, evac, post))
        return kv, blocks

    def xattn_head(self, L, hx, slot, s, zi):
        op = self.op
        C = self.consts()
        groups = []
        t3 = self.t3
        k3 = self.ak("t3")
        for j in range(NB):
            g = []
            g += self.qz_seq(L, s, j, C["ones"], 128, False, 2, zi, gcol=3, wq=1, qx=True)
            for mt in range(2):
                kT = self.MKT[:, hx, mt * 128:(mt + 1) * 128]
                qT = self.QX[:, :]
                u = self.make_unit(kT, [("MKT", hx)], qT, [self.ak("QX")], 0, None, None, [], False,
                                   self.MV[:, mt, hx * 128:(hx + 1) * 128], [("MV", mt)], PJ, PB, mt == 0, mt == 1, px=True)
                for q_ in u["qk"]:
                    q_.glue = True
                g += u["qk"] + u["exp"] + u["pv"]
            yk = self.ak("YT", slot, j)
            Y = self.YT[:, slot, j * 512:(j + 1) * 512]
            zk = self.ak("ZS", zi)
            Z = self.ZS[zi]
            g.append(op("act", [("ps", PB)], [k3], lambda e: e.activation(out=t3, in_=self.bank(PB), func=AF.Ln)))
            g.append(op("act", [k3], [k3], lambda e: e.activation(out=t3, in_=t3, func=AF.Exp, scale=-1.0)))
            g.append(op("dve", [("ps", PJ), k3], [k3], lambda e: e.tensor_tensor(out=t3, in0=self.bank(PJ), in1=t3, op=ALU.mult)))
            g.append(op("dve", [k3, zk], [yk], lambda e, Y=Y, Z=Z: e.tensor_tensor(out=Y, in0=t3, in1=Z, op=ALU.mult)))
            groups.append(g)
        return groups

    def conv_chunk(self, cc, slot, s, zi, wname):
        op = self.op
        groups = []
        xk = ("xpad",)
        xp = self.xpad
        cb = 6 + cc * 4
        t3 = self.t3
        k3 = self.ak("t3")
        cl = self.cols
        groups.append([op("pool", [], [xk], lambda e: e.memset(xp[:, 0:2], 0.0))])
        for j in range(NB):
            g = []
            g += self.proj_fm(s, 1, j)
            g.append(op("act", [("ps", PJ)], [k3], lambda e: e.activation(out=t3, in_=self.bank(PJ), func=AF.Copy)))
            g += self.proj_fm(s, 2, j, pbank=PB)
            g.append(op("dve", [("ps", PB), k3], [xk], lambda e, j=j: e.tensor_tensor(out=xp[:, 2 + j * 512:2 + (j + 1) * 512], in0=self.bank(PB), in1=t3, op=ALU.mult)))
            groups.append(g)
        groups.append([self.load_w(wname, 5120 + cc * 128, s, 1), self.load_w(wname, 5632 + cc * 128, s, 2)])
        for j in range(NB):
            g = []
            yk = self.ak("YT", slot, j)
            Y = self.YT[:, slot, j * 512:(j + 1) * 512]
            x0 = xp[:, j * 512:j * 512 + 512]
            x1 = xp[:, j * 512 + 1:j * 512 + 513]
            x2 = xp[:, j * 512 + 2:j * 512 + 514]
            g.append(op("dve", [xk, ("col", cb), ("col", cb + 3)], [k3], lambda e, x0=x0: e.tensor_scalar(out=t3, in0=x0, scalar1=cl[:, cb:cb + 1], scalar2=cl[:, cb + 3:cb + 4], op0=ALU.mult, op1=ALU.add)))
            g.append(op("dve", [xk, k3, ("col", cb + 1)], [k3], lambda e, x1=x1: e.scalar_tensor_tensor(out=t3, in0=x1, scalar=cl[:, cb + 1:cb + 2], in1=t3, op0=ALU.mult, op1=ALU.add)))
            g.append(op("dve", [xk, k3, ("col", cb + 2)], [k3], lambda e, x2=x2: e.scalar_tensor_tensor(out=t3, in0=x2, scalar=cl[:, cb + 2:cb + 3], in1=t3, op0=ALU.mult, op1=ALU.add)))
            g += self.proj_fm(s, 1, j)
            g.append(op("dve", [("ps", PJ), k3], [k3], lambda e: e.tensor_tensor(out=t3, in0=self.bank(PJ), in1=t3, op=ALU.mult)))
            g += self.proj_fm(s, 2, j, pbank=PB)
            gh_, gt_ = self.gate_ops(PB, zi)
            g += gh_ + gt_
            zk = self.ak("ZS", zi)
            Z = self.ZS[zi]
            g.append(op("dve", [k3, zk], [yk], lambda e, Y=Y, Z=Z: e.tensor_tensor(out=Y, in0=t3, in1=Z, op=ALU.mult)))
            groups.append(g)
        return groups

    def fox_prologue(self):
        op = self.op
        T = self.T
        o = []
        WF = self.WF
        wfk = self.ak("WF")
        o.append(op("pool", [], [wfk], lambda e: e.memset(WF, 0.0)))
        for r0 in (0, 32, 64):
            o.append(op("pool", [wfk], [wfk], lambda e, r0=r0: e.dma_start(out=WF[:, :, r0:r0 + 12], in_=T["o_w_in"][:, 6144:6156].rearrange("(c p) j -> p c j", p=128)), dma=True))
            o.append(op("sp", [], [("col", 22)], lambda e, r0=r0: e.dma_start(out=self.cols[r0:r0 + 12, 22:23], in_=T["o_c_forget_b"].rearrange("(p o) -> p o", o=1)), dma=True))
        o.append(op("dve", [("col", 22)], [("col", 23)], lambda e: e.tensor_scalar(out=self.cols[:, 23:24], in0=self.cols[:, 22:23], scalar1=-1.0, scalar2=None, op0=ALU.mult)))
        f0, f1, f2 = self.fx
        kf = [self.ak("fx", i) for i in range(3)]
        hb, mb = self.fxb
        kh, km = self.ak("fxb", 0), self.ak("fxb", 1)
        nck = self.ak("WMh")
        o.append(op("sp", [], ["sel"], lambda e: e.dma_start(out=self.sel[0:76, :], in_=T["c_sel"]), dma=True))
        ok1 = self.ak("onesf")
        o.append(op("pool", [], [ok1], lambda e: e.memset(self.onesf, 1.0)))
        ncum = self.ncum
        R = 76
        for b in range(NB):
            pf = self.bank(PJ, R)
            for c in range(8):
                o.append(op("pe", [wfk] + [("UT", 4 * b + i) for i in range(4)], [("ps", PJ)],
                            lambda e, c=c, b=b: e.matmul(pf, lhsT=WF[:, c, 0:R], rhs=self.UT[:, c, b * 512:(b + 1) * 512], start=(c == 0), stop=(c == 7))))
            o.append(op("act", [("ps", PJ), ("col", 23)], [kf[0]], lambda e: e.activation(out=f0[0:R], in_=pf, func=AF.Exp, bias=self.cols[0:R, 23:24], scale=-1.0)))
            o.append(op("act", [kf[0]], [kf[0]], lambda e: e.activation(out=f0[0:R], in_=f0[0:R], func=AF.Ln, bias=1.0)))
            init = 0.0 if b == 0 else ncum[0:R, b * 512 - 1:b * 512]
            o.append(op("dve", [kf[0], ok1, nck], [nck], lambda e, b=b, init=init: e.tensor_tensor_scan(out=ncum[0:R, b * 512:(b + 1) * 512], data0=self.onesf[0:R, :], data1=f0[0:R], initial=init, op0=ALU.mult, op1=ALU.add)))
            nb = ncum[0:R, b * 512:(b + 1) * 512]
            o.append(op("dve", [nck], [kf[1]], lambda e, nb=nb: e.tensor_scalar(out=f1[0:R], in0=nb, scalar1=-1.0, scalar2=None, op0=ALU.mult)))
            o.append(op("dve", [kf[1]], [kh], lambda e: e.tensor_copy(out=hb[0:R], in_=f1[0:R])))
            o.append(op("dve", [kf[1], kh], [kf[2]], lambda e: e.tensor_tensor(out=f2[0:R], in0=f1[0:R], in1=hb[0:R], op=ALU.subtract)))
            o.append(op("dve", [kf[2]], [km], lambda e: e.tensor_copy(out=mb[0:R], in_=f2[0:R])))
            o.append(op("dve", [kf[2], km], [kf[2]], lambda e: e.tensor_tensor(out=f2[0:R], in0=f2[0:R], in1=mb[0:R], op=ALU.subtract)))
            cs = self.Csplit
            o.append(op("dve", [kh], [("Csplit",)], lambda e, b=b: e.tensor_copy(out=cs[0:32, b * 512:(b + 1) * 512], in_=hb[0:32])))
            o.append(op("dve", [km], [("Csplit",)], lambda e, b=b: e.tensor_copy(out=cs[32:64, b * 512:(b + 1) * 512], in_=mb[32:64])))
            o.append(op("dve", [kf[2]], [("Csplit",)], lambda e, b=b: e.tensor_copy(out=cs[64:R, b * 512:(b + 1) * 512], in_=f2[64:R])))
        pt = self.ps[:, A0, 0:192]
        for t in range(NT):
            o.append(op("pe", [nck, "identf"], [("ps", A0)], lambda e, t=t: e.transpose(out=self.ps[:, A0, t * 12:(t + 1) * 12], in_=ncum[0:12, t * 128:(t + 1) * 128], identity=self.identf[0:12, 0:12])))
        o.append(op("dve", [("ps", A0)], [("kb",)], lambda e: e.tensor_copy(out=self.kb[:, :], in_=pt)))
        return o

    def main_phase(self, L):
        op = self.op
        T = self.T
        o = []
        wname = "e_w_in" if L == 0 else "o_w_in"
        if L == 0:
            A = [("attn", h, h) for h in range(8)]
            Cv = [("conv", c, 8 + c) for c in range(4)]
            X = [("xattn", x, 12 + x) for x in range(4)]
            chunks = [A[0], Cv[0], A[1], Cv[1], A[2], Cv[2], A[3], Cv[3], A[4], X[0], A[5], X[1], A[6], X[2], A[7], X[3]]
        else:
            A = [("attn", h, h) for h in range(12)]
            X = [("xattn", x, 12 + x) for x in range(4)]
            chunks = [A[0], A[1], A[2], X[0], A[3], A[4], A[5], X[1], A[6], A[7], A[8], X[2], A[9], A[10], A[11], X[3]]
        nch = len(chunks)
        hidx = []
        hi = -1
        for (kind, idx, rb) in chunks:
            if kind == "attn":
                hi += 1
            hidx.append(hi)

        def wloads(ci):
            kind, idx, _ = chunks[ci]
            s = hidx[ci] % 2
            if kind == "attn":
                if L == 0:
                    offs = [idx * 128, 1024 + idx * 128, 2048 + idx * 128, 3072 + idx * 128]
                else:
                    offs = [idx * 128, 1536 + idx * 128, 3072 + idx * 128, 4608 + idx * 128]
                return [self.load_w(wname, offs[k], s, k) for k in range(4)]
            if kind == "conv":
                return [self.load_w(wname, 4608 + idx * 128, s, 1), self.load_w(wname, 4096 + idx * 128, s, 2)]
            if L == 0:
                return [self.load_w(wname, 6144 + idx * 128, s, 1), self.load_w(wname, 6656 + idx * 128, s, 2)]
            return [self.load_w(wname, 6156 + idx * 128, s, 1), self.load_w(wname, 6668 + idx * 128, s, 2)]

        if L == 0:
            for j in range(NB):
                o.append(op("pool", [], [self.ak("Qaug"), self.ak("Q", j)], lambda e, j=j: e.memset(self.QB[j][0:64, :], 0.0)))
                o.append(op("sp", [], [self.ak("Qaug")], lambda e, j=j: e.dma_start(out=self.QA[j][64:68, :], in_=T["c_qa"][:, j * 512:(j + 1) * 512]), dma=True))
                o.append(op("sp", [self.ak("Qaug")], [self.ak("Qaug")], lambda e, j=j: e.dma_start(out=self.QB[j][0:4, :], in_=T["c_qa"][:, j * 512:(j + 1) * 512]), dma=True))

        def flat(groups):
            r = []
            for g in groups:
                r.extend(g)
            return r

        gen = []
        for ci, (kind, idx, rb) in enumerate(chunks):
            s = hidx[ci] % 2
            slot = ci % 4
            zl = 2 * ((hidx[ci] + 1) % 2) + 1
            if kind == "conv":
                gen.append(("light", self.conv_chunk(idx, slot, s, zl, wname)))
            elif kind == "xattn":
                gen.append(("light", self.xattn_head(L, idx, slot, s, zl)))
            else:
                kv, blocks = self.attn_head(L, idx, slot, s, 2 * (hidx[ci] % 2))
                gen.append(("heavy", kv, blocks))

        if self.ydbg is not None:
            li = self.layers.index(L)
            for ci, (kind, idx, rb) in enumerate(chunks):
                slot = ci % 4
                dop = op("sp", [self.ak("YT", slot, j) for j in range(NB)], [("ydbg", ci)],
                         lambda e, slot=slot, rb=rb: e.dma_start(out=self.ydbg[li, rb * 128:(rb + 1) * 128, :], in_=self.YT[:, slot, :]), dma=True)
                if gen[ci][0] == "light":
                    gen[ci][1][-1].append(dop)
                else:
                    gen[ci][2][NB - 1][3].append(dop)
        fill_banks = [PJ, PB]
        o += wloads(0)
        assert gen[0][0] == "heavy"
        o += flat(gen[0][1]) + flat(gen[0][2][0][0])
        pending_out = None
        pending_post = []
        pending_g = None
        wo_loads = {}
        for g_ in range(nch // 4):
            rbs_ = [chunks[g_ * 4 + c][2] for c in range(4)]
            wo_loads[g_] = self.outproj_loads(L, rbs_)
        o += wo_loads[0]
        ci = 0
        while ci < nch:
            assert gen[ci][0] == "heavy"
            blocks = gen[ci][2]
            ahead = []
            cj = ci + 1
            loads = []
            import os as _os
            dbg = _os.environ.get("MKDBG", "")
            seq_l, seq_k = [], []
            while cj < nch and gen[cj][0] == "light":
                loads += wloads(cj)
                if "L" in dbg:
                    seq_l += gen[cj][1]
                else:
                    ahead += gen[cj][1]
                cj += 1
            tail = []
            if cj < nch:
                loads += wloads(cj)
                if "K" in dbg:
                    seq_k += gen[cj][1]
                else:
                    ahead += gen[cj][1]
                tail = gen[cj][2][0][0]
            o += loads
            import os as _os
            dbg = _os.environ.get("MKDBG", "")
            post_seq = seq_l + seq_k
            if "T" in dbg:
                post_seq += tail
                tail = []
            if "A" in dbg:
                post_seq = ahead + post_seq
                ahead = []
            tot = sum(len(g) for g in ahead) + sum(len(g) for g in tail)
            parts = [[], [], []]
            acc = 0
            for g in ahead + tail:
                fr = acc / max(tot, 1)
                k = 0 if fr < 0.2 else (1 if fr < 0.52 else 2)
                parts[k].extend(g)
                acc += len(g)
            for j in range(NB):
                qz, ug, evac, post = blocks[j]
                fl = []
                fl += pending_post
                pending_post = []
                if j == 0 and pending_out is not None:
                    fl += pending_out
                    if pending_g + 1 in wo_loads:
                        fl += wo_loads[pending_g + 1]
                    pending_out = None
                if j + 1 < NB:
                    fl += flat(blocks[j + 1][0])
                if j >= 1:
                    fl += parts[j - 1]
                nu = len(ug)
                slices = self.stage_slices(fl)
                ns = len(slices)
                si_ = 0
                for u in range(nu):
                    o += ug[u]
                    tgt = ((u + 1) * ns) // nu
                    while si_ < tgt:
                        o += slices[si_]
                        si_ += 1
                o += evac
                pending_post = post
            if not (cj < nch):
                o += pending_post
                pending_post = []
            o += flat(post_seq)
            for ck in range(ci, cj):
                if ck % 4 == 3:
                    g = ck // 4
                    rbs = [chunks[g * 4 + c][2] for c in range(4)]
                    last = not (cj < nch)
                    ld, oo = self.outproj(L, rbs, [A0, A1, A2, A3] if last else fill_banks)
                    if not last and ck == cj - 1:
                        pending_out = oo
                        pending_g = g
                    else:
                        o += oo
            ci = cj
        assert pending_out is None and not pending_post
        return o

    def store_out(self):
        o = []
        T = self.T
        for t in range(NT):
            o.append(self.op("sp", self.hk(t), [("out", t)], lambda e, t=t: e.dma_start(out=T["out"][t * 128:(t + 1) * 128, :], in_=self.h[:, t, :]), dma=True))
        o.append(Op("sp", None, reads=[("out", t) for t in range(NT)]))
        return o


_CACHE = {}


def _get_prog(layers):
    key = tuple(layers)
    if key not in _CACHE:
        g = Gen(layers)
        nc = g.build()
        _CACHE[key] = (nc, g.in_names)
    return _CACHE[key]


def _run(layers, xin, mem, params, consts):
    nc, in_names = _get_prog(layers)
    in_maps = []
    for b in range(8):
        m = {}
        for n in in_names:
            if n == "x":
                m[n] = np.ascontiguousarray(xin[b])
            elif n == "mem":
                m[n] = np.ascontiguousarray(mem[b])
            elif n in consts:
                m[n] = consts[n]
            else:
                m[n] = params[n]
        in_maps.append(m)
    res = run_bass_kernel_spmd(nc, in_maps, core_ids=list(range(8)))
    return np.stack([np.asarray(r["out"]) for r in res.results], 0)


FUSED = True


def kernel(**inputs):
    x = np.asarray(inputs["x"], np.float32)
    mem = np.asarray(inputs["mem"], np.float32)
    params = {}
    for n in L0_NAMES + L1_NAMES:
        a = np.asarray(inputs[n], np.float32)
        params[n] = np.ascontiguousarray(a[0])
    consts = _consts()
    if FUSED:
        return _run((0, 1), x, mem, params, consts).astype(np.float32)
    h1 = _run((0,), x, mem, params, consts)
    return _run((1,), h1, mem, params, consts).astype(np.float32)
```
